# Optimizing a Trainium2 kernel written in Bass

```python
import math
import jax, jax.numpy as jnp
from jax import lax
import numpy as np

D_MODEL = 1024
BATCH = 16
SEQ = 2048
DEPTH = 1

RET_HEADS = 4
RET_DK = 128
RET_DV = 256
RET_CHUNK = 128
RET_ROT_BASE = 10000.0
MOBA_HEADS = 8
MOBA_DH = 64
MOBA_BLOCK = 256
MOBA_TOPK = 3
MOBA_QCHUNK = 128
ROPE_THETA = 500000.0
ROPE_DIMS = MOBA_DH // 4
D_FF = 2816
LN_EPS = 1e-5
GN_EPS = 1e-5
DEEPNORM_ALPHA = (2.0 * DEPTH) ** 0.25
DEEPNORM_BETA = (8.0 * DEPTH) ** -0.25
W_IN_SPLITS = (
    RET_HEADS * RET_DK,
    RET_HEADS * RET_DK,
    RET_HEADS * RET_DV,
    RET_HEADS * RET_DV,
    MOBA_HEADS * MOBA_DH,
    MOBA_HEADS * MOBA_DH,
    MOBA_HEADS * MOBA_DH,
    2 * D_MODEL,
)
W_IN_COLS = sum(W_IN_SPLITS)

kernel_name = "hybrid_retention_moba_macaron_deepnorm"


def layer_norm(x, g, b):
    xf = x.astype(jnp.float32)
    mu = jnp.mean(xf, axis=-1, keepdims=True)
    var = jnp.mean(jnp.square(xf - mu), axis=-1, keepdims=True)
    y = (xf - mu) * lax.rsqrt(var + LN_EPS) * g.astype(jnp.float32) + b.astype(jnp.float32)
    return y.astype(x.dtype)


def swiglu(x, w_gu, w_down):
    gate, up = jnp.split(x @ w_gu, 2, axis=-1)
    return (jax.nn.silu(gate) * up) @ w_down


def rotate(x, pos, inv_freq):
    ang = pos[:, None].astype(jnp.float32) * inv_freq[None, :]
    cos = jnp.cos(ang)[None, :, None, :]
    sin = jnp.sin(ang)[None, :, None, :]
    x1, x2 = jnp.split(x, 2, axis=-1)
    return jnp.concatenate([x1 * cos - x2 * sin, x1 * sin + x2 * cos], axis=-1)


def retention(q, k, v):
    B, S, H, DK = q.shape
    DV = v.shape[-1]
    C = RET_CHUNK
    n = S // C
    pos = jnp.arange(S)
    inv = 1.0 / (RET_ROT_BASE ** jnp.linspace(0.0, 1.0, DK // 2, dtype=jnp.float32))
    q = rotate(q, pos, inv)
    k = rotate(k, pos, inv) * (DK ** -0.5)
    log_g = jnp.log(1.0 - 2.0 ** (-5.0 - jnp.arange(H, dtype=jnp.float32)))

    qc = q.reshape(B, n, C, H, DK).transpose(0, 3, 1, 2, 4)
    kc = k.reshape(B, n, C, H, DK).transpose(0, 3, 1, 2, 4)
    vc = v.reshape(B, n, C, H, DV).transpose(0, 3, 1, 2, 4)

    idx = jnp.arange(C, dtype=jnp.float32)
    diff = idx[:, None] - idx[None, :]
    d_intra = jnp.where(diff >= 0, jnp.exp(log_g[:, None, None] * jnp.maximum(diff, 0.0)), 0.0)
    scores = jnp.einsum('bhncd,bhnkd->bhnck', qc, kc) * d_intra[:, None]
    intra = jnp.einsum('bhnck,bhnke->bhnce', scores, vc)

    k_dec = kc * jnp.exp(log_g[:, None] * (C - 1.0 - idx)[None, :])[:, None, :, None]
    kv = jnp.einsum('bhnkd,bhnke->nbhde', k_dec, vc)
    chunk_decay = jnp.exp(log_g * C)[None, :, None, None]

    def step(state, kv_n):
        return chunk_decay * state + kv_n, state

    _, prev = lax.scan(step, jnp.zeros((B, H, DK, DV), jnp.float32), kv)
    q_dec = qc * jnp.exp(log_g[:, None] * (idx + 1.0)[None, :])[:, None, :, None]
    cross = jnp.einsum('bhncd,nbhde->bhnce', q_dec, prev)

    out = (intra + cross).transpose(0, 2, 3, 1, 4).reshape(B, S, H, DV)
    mu = jnp.mean(out, axis=-1, keepdims=True)
    var = jnp.mean(jnp.square(out - mu), axis=-1, keepdims=True)
    out = (out - mu) * lax.rsqrt(var + GN_EPS)
    return out.reshape(B, S, H * DV)


def moba_attention(q, k, v):
    B, S, H, D = q.shape
    BLK = MOBA_BLOCK
    QC = MOBA_QCHUNK
    pos = jnp.arange(S)
    inv = 1.0 / (ROPE_THETA ** (jnp.arange(0, ROPE_DIMS, 2, dtype=jnp.float32) / ROPE_DIMS))
    q = jnp.concatenate([rotate(q[..., :ROPE_DIMS], pos, inv), q[..., ROPE_DIMS:]], axis=-1)
    k = jnp.concatenate([rotate(k[..., :ROPE_DIMS], pos, inv), k[..., ROPE_DIMS:]], axis=-1)

    nblk = -(-S // BLK)
    s_pad = nblk * BLK
    padw = ((0, 0), (0, s_pad - S), (0, 0), (0, 0))
    kb = jnp.pad(k, padw).reshape(B, nblk, BLK, H, D).transpose(0, 3, 1, 2, 4)
    vb = jnp.pad(v, padw).reshape(B, nblk, BLK, H, D).transpose(0, 3, 1, 2, 4)
    k_mean = jnp.mean(kb, axis=3)

    qh = q.transpose(0, 2, 1, 3)
    gate = jnp.einsum('bhsd,bhnd->bhsn', qh, k_mean)
    cur = pos // BLK
    past = jnp.arange(nblk)[None, :] < cur[:, None]
    gate = jnp.where(past[None, None], gate, -jnp.inf)
    n_sel = min(MOBA_TOPK, nblk)
    _, top_idx = lax.top_k(gate, n_sel)

    nqc = S // QC
    q_chunks = qh.reshape(B, H, nqc, QC, D).transpose(0, 2, 1, 3, 4).reshape(B * nqc, H, QC, D)
    i_chunks = top_idx.reshape(B, H, nqc, QC, n_sel).transpose(0, 2, 1, 3, 4).reshape(B * nqc, H, QC, n_sel)
    scale = D ** -0.5

    def attend(args):
        i, q_c, idx_c = args
        b = i // nqc
        c = i % nqc
        kb_b = lax.dynamic_index_in_dim(kb, b, 0, keepdims=False)
        vb_b = lax.dynamic_index_in_dim(vb, b, 0, keepdims=False)
        gk = jax.vmap(lambda kh, ih: kh[ih])(kb_b, idx_c)
        gv = jax.vmap(lambda vh, ih: vh[ih])(vb_b, idx_c)
        qpos = c * QC + jnp.arange(QC)
        s_sel = jnp.einsum('hqd,hqjpd->hqjp', q_c, gk) * scale
        sel_ok = jnp.arange(n_sel)[None, :] < (qpos // BLK)[:, None]
        s_sel = jnp.where(sel_ok[None, :, :, None], s_sel, -jnp.inf)
        own = (c * QC) // BLK
        ko = lax.dynamic_index_in_dim(kb_b, own, 1, keepdims=False)
        vo = lax.dynamic_index_in_dim(vb_b, own, 1, keepdims=False)
        s_own = jnp.einsum('hqd,hpd->hqp', q_c, ko) * scale
        kpos = own * BLK + jnp.arange(BLK)
        s_own = jnp.where((kpos[None, :] <= qpos[:, None])[None], s_own, -jnp.inf)
        logits = jnp.concatenate([s_sel.reshape(H, QC, n_sel * BLK), s_own], axis=-1)
        p = jax.nn.softmax(logits, axis=-1)
        p_sel = p[..., :n_sel * BLK].reshape(H, QC, n_sel, BLK)
        p_own = p[..., n_sel * BLK:]
        return jnp.einsum('hqjp,hqjpd->hqd', p_sel, gv) + jnp.einsum('hqp,hpd->hqd', p_own, vo)

    out = lax.map(attend, (jnp.arange(B * nqc), q_chunks, i_chunks))
    return out.reshape(B, nqc, H, QC, D).transpose(0, 1, 3, 2, 4).reshape(B, S, H * D)


def hybrid_mixer(h, w_in, ret_proj, moba_proj, w_out):
    B, S, _ = h.shape
    f32 = jnp.float32
    offs = np.cumsum(W_IN_SPLITS)[:-1].tolist()
    rq, rk, rv, rg, mq, mk, mv, gates = jnp.split(h @ w_in, offs, axis=-1)
    y_ret = retention(rq.reshape(B, S, RET_HEADS, RET_DK).astype(f32),
                      rk.reshape(B, S, RET_HEADS, RET_DK).astype(f32),
                      rv.reshape(B, S, RET_HEADS, RET_DV).astype(f32))
    y_ret = y_ret * jax.nn.silu(rg.astype(f32))
    y_a = y_ret.astype(h.dtype) @ ret_proj
    y_moba = moba_attention(mq.reshape(B, S, MOBA_HEADS, MOBA_DH).astype(f32),
                            mk.reshape(B, S, MOBA_HEADS, MOBA_DH).astype(f32),
                            mv.reshape(B, S, MOBA_HEADS, MOBA_DH).astype(f32))
    y_b = y_moba.astype(h.dtype) @ moba_proj
    g_a, g_b = jnp.split(jax.nn.sigmoid(gates), 2, axis=-1)
    return (g_a * y_a + g_b * y_b) @ w_out


def setup_inputs(seed: int = 0) -> dict:
    key = jax.random.key(seed)
    ks = jax.random.split(key, 16)
    f32 = jnp.float32
    L = DEPTH

    def nrm(k, shape, scale):
        return jax.random.normal(k, shape, f32) * scale

    return {
        "x": jax.random.normal(ks[0], (BATCH, SEQ, D_MODEL), f32),
        "ln1_g": 1.0 + nrm(ks[1], (L, D_MODEL), 0.02),
        "ln1_b": nrm(ks[2], (L, D_MODEL), 0.02),
        "ffn1_w_gu": nrm(ks[3], (L, D_MODEL, 2 * D_FF), D_MODEL ** -0.5),
        "ffn1_w_down": nrm(ks[4], (L, D_FF, D_MODEL), DEEPNORM_BETA * D_FF ** -0.5),
        "w_in": nrm(ks[5], (L, D_MODEL, W_IN_COLS), D_MODEL ** -0.5),
        "ret_proj": nrm(ks[6], (L, RET_HEADS * RET_DV, D_MODEL), (RET_HEADS * RET_DV) ** -0.5),
        "moba_proj": nrm(ks[7], (L, MOBA_HEADS * MOBA_DH, D_MODEL), (MOBA_HEADS * MOBA_DH) ** -0.5),
        "w_out": nrm(ks[8], (L, D_MODEL, D_MODEL), DEEPNORM_BETA * D_MODEL ** -0.5),
        "lnm_g": 1.0 + nrm(ks[9], (L, D_MODEL), 0.02),
        "lnm_b": nrm(ks[10], (L, D_MODEL), 0.02),
        "ffn2_w_gu": nrm(ks[11], (L, D_MODEL, 2 * D_FF), D_MODEL ** -0.5),
        "ffn2_w_down": nrm(ks[12], (L, D_FF, D_MODEL), DEEPNORM_BETA * D_FF ** -0.5),
        "ln2_g": 1.0 + nrm(ks[13], (L, D_MODEL), 0.02),
        "ln2_b": nrm(ks[14], (L, D_MODEL), 0.02),
    }


def reference(x, ln1_g, ln1_b, ffn1_w_gu, ffn1_w_down, w_in, ret_proj, moba_proj, w_out,
              lnm_g, lnm_b, ffn2_w_gu, ffn2_w_down, ln2_g, ln2_b):
    for l in range(DEPTH):
        x = layer_norm(DEEPNORM_ALPHA * x + 0.5 * swiglu(x, ffn1_w_gu[l], ffn1_w_down[l]), ln1_g[l], ln1_b[l])
        x = layer_norm(DEEPNORM_ALPHA * x + hybrid_mixer(x, w_in[l], ret_proj[l], moba_proj[l], w_out[l]),
                       lnm_g[l], lnm_b[l])
        x = layer_norm(DEEPNORM_ALPHA * x + 0.5 * swiglu(x, ffn2_w_gu[l], ffn2_w_down[l]), ln2_g[l], ln2_b[l])
    return x
```

```python
import math
import numpy as np
import ml_dtypes
import concourse.bass as bass
import concourse.mybir as mybir
from concourse.bass_utils import run_bass_kernel_spmd

F32 = mybir.dt.float32
BF16 = mybir.dt.bfloat16
AF = mybir.ActivationFunctionType
ALU = mybir.AluOpType
AX = mybir.AxisListType

D = 1024
T = 2048
KC = 8
DFF = 2816
NJ = 22
NCORES = 8
SEQ_PER_CORE = 2
ALPHA = 2.0 ** 0.25
LN_EPS = 1e-5
GN_EPS = 1e-5
WIN = 6656
GROUPS = [(0, 8), (8, 15), (15, 22)]
NEG = -30000.0


class Op:
    __slots__ = ("eng", "fn", "deps", "dsem", "sig", "cnt")

    def __init__(self, eng, fn, deps, dsem):
        self.eng = eng
        self.fn = fn
        self.deps = deps
        self.dsem = dsem
        self.sig = dsem is not None
        self.cnt = 0


class Sched:
    def __init__(self):
        self.ops = []
        self.lastw = {}
        self.readers = {}
        self.bar = set()
        self.last_stream = {}

    def add(self, eng, fn, r=(), w=(), dsem=None):
        i = len(self.ops)
        deps = set(self.bar)
        for k in r:
            j = self.lastw.get(k)
            if j is not None:
                deps.add(j)
        for k in w:
            j = self.lastw.get(k)
            if j is not None:
                deps.add(j)
            rd = self.readers.get(k)
            if rd:
                deps.update(rd.values())
        stream = ("dma", dsem) if dsem is not None else eng
        for k in r:
            self.readers.setdefault(k, {})[stream] = i
        for k in w:
            self.lastw[k] = i
            self.readers[k] = {}
        self.last_stream[stream] = i
        self.ops.append(Op(eng, fn, deps, dsem))
        return i

    def barrier(self):
        self.bar = set(self.last_stream.values())

    def emit(self, nc, engines):
        ops = self.ops

        def stream_of(o):
            return ("dma", o.dsem) if o.dsem is not None else o.eng

        for i, o in enumerate(ops):
            best = {}
            for j in o.deps:
                p = ops[j]
                st = stream_of(p)
                if p.dsem is None and p.eng == "pe" and o.eng == "pe" and o.dsem is None:
                    continue
                if st not in best or best[st] < j:
                    best[st] = j
            o.deps = best
            for j in best.values():
                ops[j].sig = True
        cnt = {}
        for o in ops:
            if o.sig:
                st = stream_of(o)
                inc = 16 if o.dsem is not None else 1
                cnt[st] = cnt.get(st, 0) + inc
                o.cnt = cnt[st]
        sems = {}
        import contextlib
        stack = contextlib.ExitStack()
        for k, st in enumerate(cnt.keys()):
            sems[st] = stack.enter_context(nc.semaphore("s%d" % k))
        self.max_counts = dict(cnt)
        with stack:
            with nc.Block() as block:
                for ename, (deco, _h) in engines.items():
                    my = [(i, o) for i, o in enumerate(ops) if o.eng == ename]
                    if not my:
                        continue

                    def body(eng, my=my, ename=ename):
                        waited = {}
                        for i, o in my:
                            for st, j in o.deps.items():
                                v = ops[j].cnt
                                if waited.get(st, 0) < v:
                                    eng.wait_ge(sems[st], v)
                                    waited[st] = v
                            ins = o.fn(eng)
                            if o.sig:
                                st = stream_of(o)
                                ins.then_inc(sems[st], 16 if o.dsem is not None else 1)
                                if o.dsem is None:
                                    pass
                    getattr(block, deco)(body)


def _tile(nc, name, shape, dt):
    return nc.sbuf_tensor(name, list(shape), dt).__enter__()


def _ptile(nc, name, shape, dt):
    return nc.psum_tensor(name, list(shape), dt).__enter__()


def build_program(nseq=SEQ_PER_CORE, phases=("ffn1", "mix", "ffn2"), dbg=None, mixp=("r", "m", "o1", "o2")):
    nc = bass.Bass("TRN2", target_bir_lowering=False)
    S = Sched()
    NT = nseq * T

    def din(name, shape, dt=F32):
        return nc.dram_tensor(name, list(shape), dt, kind="ExternalInput").ap()

    xT = din("xT", [128, KC, NT])
    outT = nc.dram_tensor("outT", [128, KC, NT], F32, kind="ExternalOutput").ap()
    wgu_d = [din("wgu1", [NJ, 128, 2, KC, 128]), din("wgu2", [NJ, 128, 2, KC, 128])]
    wd_d = [din("wd1", [NJ, 128, D]), din("wd2", [NJ, 128, D])]
    lnp_d = din("lnp", [128, 6, KC])
    cb_d = din("constb", [128, 256], BF16)
    win_d = din("win", [128, KC, WIN])
    rp_d = din("retp", [128, KC, D])
    mp_d = din("mobp", [128, 4, D])
    wo_d = din("wout", [128, KC, D])
    rcs_d = din("rcs", [128, 16, 2, 64])
    mcs_d = din("mcs", [128, 16, 2, 8])
    rdec_d = din("rdec", [128, 1032])
    mcb_d = din("mconstb", [128, 1152], BF16)

    R = _tile(nc, "R", [128, KC, T], F32)
    ARENA = _tile(nc, "ARENA", [128, 72704], BF16)
    lnp = _tile(nc, "lnp_sb", [128, 6, KC], F32)
    constb = _tile(nc, "constb_sb", [128, 256], BF16)
    ident = constb[:, 0:128]
    ones_div = constb[:, 128:256]
    epsc = _tile(nc, "epsc", [128, 2], F32)

    def carve(off_bytes, shape, dt):
        n = int(np.prod(shape[1:]))
        if dt == BF16:
            a = ARENA[:, off_bytes // 2: off_bytes // 2 + n]
        else:
            a = ARENA[:, off_bytes // 2: off_bytes // 2 + 2 * n].bitcast(F32)
        if len(shape) == 2:
            return a
        names = " ".join("d%d" % i for i in range(1, len(shape)))
        kw = {"d%d" % i: shape[i] for i in range(1, len(shape))}
        return a.rearrange("p (%s) -> p %s" % (names, names), **kw)

    K = 1024
    Xb = carve(0, [128, KC, T], BF16)
    actT = carve(32 * K, [128, 8, T], BF16)
    Wgu = [carve(64 * K + 4 * K * b, [128, 2, KC, 128], BF16) for b in range(3)]
    Wd = carve(76 * K, [128, 8, D], BF16)
    zb = carve(92 * K, [128, KC, 512], BF16)
    zsq = carve(100 * K, [128, KC, 512], BF16)
    mean_sb = carve(108 * K, [128, 512], F32)
    rstd_sb = carve(110 * K, [128, 512], F32)
    var_sb = carve(112 * K, [128, 512], F32)
    sg = [carve(114 * K + 1 * K * b, [128, 512], BF16) for b in range(2)]

    PSALL = _ptile(nc, "psall", [128, 4096], F32)
    PS = [PSALL[:, b * 512:(b + 1) * 512] for b in range(8)]

    def PSB(b):
        return PSALL[:, b * 512:(b + 1) * 512].bitcast(BF16)

    S.add("sp", lambda e: e.dma_start(out=lnp[:], in_=lnp_d), w=[("lnp",)], dsem="c_lnp")
    S.add("sp", lambda e: e.dma_start(out=constb[:], in_=cb_d), w=[("constb",)], dsem="c_cb")
    S.add("dve", lambda e: e.memset(epsc[:, 0:1], LN_EPS), w=[("epsc",)])
    S.add("dve", lambda e: e.memset(epsc[:, 1:2], GN_EPS), w=[("epsc",)])

    cnt = {"wgu": 0}

    def tbs(tb):
        return slice(tb * 512, (tb + 1) * 512)

    def Rkeys(tb):
        return [("R", tb, kc) for kc in range(KC)]

    def layer_norm(tb, gi):
        sl = tbs(tb)
        Rb = R[:, :, sl]
        S.add("pool", lambda e: e.tensor_copy(out=zb[:], in_=Rb), r=Rkeys(tb), w=[("zb",)])
        S.add("act", lambda e: e.activation(out=zsq[:], in_=Rb, func=AF.Square), r=Rkeys(tb), w=[("zsq",)])
        pm, pq = PS[6], PS[7]
        for kc in range(KC):
            S.add("pe", lambda e, kc=kc: e.matmul(pm[:], lhsT=ones_div, rhs=zb[:, kc, :], start=(kc == 0), stop=(kc == KC - 1)),
                  r=[("zb",), ("constb",)], w=[("ps", 6)])
        for kc in range(KC):
            S.add("pe", lambda e, kc=kc: e.matmul(pq[:], lhsT=ones_div, rhs=zsq[:, kc, :], start=(kc == 0), stop=(kc == KC - 1)),
                  r=[("zsq",), ("constb",)], w=[("ps", 7)])
        S.add("act", lambda e: e.activation(out=mean_sb[:], in_=pm[:], func=AF.Copy), r=[("ps", 6)], w=[("mean",)])
        S.add("act", lambda e: e.activation(out=var_sb[:], in_=pm[:], func=AF.Square), r=[("ps", 6)], w=[("var",)])
        S.add("dve", lambda e: e.tensor_tensor(out=var_sb[:], in0=pq[:], in1=var_sb[:], op=ALU.subtract),
              r=[("ps", 7), ("var",)], w=[("var",)])
        S.add("act", lambda e: e.activation(out=var_sb[:], in_=var_sb[:], func=AF.Sqrt, bias=epsc[:, 0:1]),
              r=[("var",), ("epsc",)], w=[("var",)])
        S.add("dve", lambda e: e.reciprocal(out=rstd_sb[:], in_=var_sb[:]), r=[("var",)], w=[("rstd",)])
        mb = mean_sb[:].unsqueeze(1).to_broadcast([128, KC, 512])
        rb = rstd_sb[:].unsqueeze(1).to_broadcast([128, KC, 512])
        S.add("dve", lambda e: e.tensor_tensor(out=Rb, in0=Rb, in1=mb, op=ALU.subtract),
              r=Rkeys(tb) + [("mean",)], w=Rkeys(tb))
        S.add("dve", lambda e: e.tensor_tensor(out=Rb, in0=Rb, in1=rb, op=ALU.mult),
              r=Rkeys(tb) + [("rstd",)], w=Rkeys(tb))
        for kc in range(KC):
            S.add("act", lambda e, kc=kc: e.activation(out=R[:, kc, sl], in_=R[:, kc, sl], func=AF.Identity,
                                                      scale=lnp[:, 2 * gi, kc:kc + 1], bias=lnp[:, 2 * gi + 1, kc:kc + 1]),
                  r=[("R", tb, kc), ("lnp",)], w=[("R", tb, kc)])

    def ffn_phase(s, f):
        gi = 0 if f == 0 else 2
        if f == 0:
            for tb in range(4):
                S.add("sp", lambda e, tb=tb: e.dma_start(out=R[:, :, tbs(tb)], in_=xT[:, :, s * T + tb * 512: s * T + (tb + 1) * 512]),
                      w=Rkeys(tb), dsem=("xin", tb))
        for tb in range(4):
            S.add("dve", lambda e, tb=tb: e.tensor_copy(out=Xb[:, :, tbs(tb)], in_=R[:, :, tbs(tb)]), r=Rkeys(tb), w=[("xb", tb)])
            S.add("act", lambda e, tb=tb: e.activation(out=R[:, :, tbs(tb)], in_=R[:, :, tbs(tb)], func=AF.Copy, scale=ALPHA),
                  r=Rkeys(tb), w=Rkeys(tb))
        it = 0
        for (j0, j1) in GROUPS:
            G = j1 - j0
            for jl in range(G):
                S.add("pool", lambda e, jl=jl, j=j0 + jl: e.dma_start(out=Wd[:, jl, :], in_=wd_d[f][j]),
                      w=[("wd", jl)], dsem=("wd", jl))
            for jl in range(G):
                j = j0 + jl
                b = cnt["wgu"] % 3
                cnt["wgu"] += 1
                S.add("pool", lambda e, b=b, j=j: e.dma_start(out=Wgu[b][:], in_=wgu_d[f][j]), w=[("wgu", b)], dsem=("wgu", b))
                for tb in range(4):
                    pg, pu = PS[it % 2], PS[2 + it % 2]
                    kg, ku, ks = ("ps", it % 2), ("ps", 2 + it % 2), ("sg", it % 2)
                    sgt = sg[it % 2]
                    it += 1
                    for kc in range(KC):
                        S.add("pe", lambda e, pg=pg, b=b, kc=kc, tb=tb: e.matmul(pg[:], lhsT=Wgu[b][:, 0, kc, :], rhs=Xb[:, kc, tbs(tb)],
                                                                                start=(kc == 0), stop=(kc == KC - 1)),
                              r=[("wgu", b), ("xb", tb)], w=[kg])
                    for kc in range(KC):
                        S.add("pe", lambda e, pu=pu, b=b, kc=kc, tb=tb: e.matmul(pu[:], lhsT=Wgu[b][:, 1, kc, :], rhs=Xb[:, kc, tbs(tb)],
                                                                                start=(kc == 0), stop=(kc == KC - 1)),
                              r=[("wgu", b), ("xb", tb)], w=[ku])
                    S.add("act", lambda e, pg=pg, sgt=sgt: e.activation(out=sgt[:], in_=pg[:], func=AF.Silu), r=[kg], w=[ks])
                    S.add("dve", lambda e, pu=pu, sgt=sgt, jl=jl, tb=tb: e.tensor_tensor(out=actT[:, jl, tbs(tb)], in0=pu[:], in1=sgt[:], op=ALU.mult),
                          r=[ku, ks], w=[("actT", jl, tb)])
            i2 = 0
            for m in range(KC):
                for tb in range(4):
                    pd = PS[4 + i2 % 2]
                    kd = ("ps", 4 + i2 % 2)
                    i2 += 1
                    for jl in range(G):
                        S.add("pe", lambda e, pd=pd, jl=jl, m=m, tb=tb, G=G: e.matmul(pd[:], lhsT=Wd[:, jl, m * 128:(m + 1) * 128], rhs=actT[:, jl, tbs(tb)],
                                                                                   start=(jl == 0), stop=(jl == G - 1)),
                              r=[("wd", jl), ("actT", jl, tb)], w=[kd])
                    S.add("dve", lambda e, pd=pd, m=m, tb=tb: e.scalar_tensor_tensor(out=R[:, m, tbs(tb)], in0=pd[:], scalar=0.5, in1=R[:, m, tbs(tb)],
                                                                                   op0=ALU.mult, op1=ALU.add),
                          r=[kd, ("R", tb, m)], w=[("R", tb, m)])
        for tb in range(4):
            layer_norm(tb, gi)

    def out_phase(s):
        for tb in range(4):
            S.add("sp", lambda e, tb=tb: e.dma_start(out=outT[:, :, s * T + tb * 512: s * T + (tb + 1) * 512], in_=R[:, :, tbs(tb)]),
                  r=Rkeys(tb), dsem=("xout", tb))


    G_H = [1.0 - 2.0 ** (-5.0 - h) for h in range(4)]
    GC = [g ** 128 for g in G_H]

    def tls(t):
        return slice(t * 128, (t + 1) * 128)

    def cast_dma_cols(dst, src_d, c0, c1, keybase, step=1536):
        for pi, a in enumerate(range(c0, c1, step)):
            b_ = min(a + step, c1)
            k = (keybase, pi)
            S.add("pool", lambda e, a=a, b_=b_: e.dma_start(out=dst[:, :, a - c0:b_ - c0], in_=src_d[:, :, a:b_]), w=[k], dsem=k)

    def xbt_cast(dst, t, key):
        S.add("act", lambda e: e.activation(out=dst[:], in_=R[:, :, tls(t)], func=AF.Copy),
              r=[("R", t // 4, kc) for kc in range(KC)], w=[key])

    def pass_ret(s):
        Wr = carve(0, [128, KC, 3072], BF16)
        yretT = carve(48 * K, [128, KC, T], BF16)
        o = 80 * K
        rcs = carve(o, [128, 16, 2, 64], F32); o += 8 * K
        rdec = carve(o, [128, 1032], F32); o += 4128
        DT = rdec[:, 0:512]
        GQ = rdec[:, 512:1024]
        gk = rdec[:, 1024:1028]
        xbt = [carve(o + 2 * K * i, [128, KC, 128], BF16) for i in range(2)]; o += 4 * K
        qkr = carve(o, [128, 8, 2, 64], BF16); o += 2 * K
        tA = carve(o, [128, 8, 2, 64], F32); o += 4 * K
        tB = carve(o, [128, 8, 64], F32); o += 2 * K
        tC = carve(o, [128, 8, 64], F32); o += 2 * K
        kdec = carve(o, [128, 4, 128], BF16); o += 1 * K
        vbf = carve(o, [128, 1024], BF16); o += 2 * K
        srg = carve(o, [128, 1024], F32); o += 4 * K
        qT = carve(o, [128, 4, 128], BF16); o += 1 * K
        qdT = carve(o, [128, 4, 128], BF16); o += 1 * K
        kT = carve(o, [128, 4, 128], BF16); o += 1 * K
        sT = carve(o, [128, 4, 128], BF16); o += 1 * K
        state = carve(o, [128, 4, 256], F32); o += 4 * K
        stbf = carve(o, [128, 4, 256], BF16); o += 2 * K
        stats = carve(o, [128, 4, 6], F32); o += 128
        mv = carve(o, [128, 4, 2], F32); o += 64
        rs = carve(o, [128, 4], F32); o += 64
        yn = carve(o, [128, 1024], F32); o += 4 * K
        yrt = carve(o, [128, 1024], BF16); o += 2 * K
        assert o <= 142 * K, o

        cast_dma_cols(Wr, win_d, 0, 3072, "wr")
        S.add("sp", lambda e: e.dma_start(out=rcs[:], in_=rcs_d), w=[("rcs",)], dsem="c_rcs")
        S.add("sp", lambda e: e.dma_start(out=rdec[:], in_=rdec_d), w=[("rdec",)], dsem="c_rdec")
        S.add("dve", lambda e: e.memset(state[:].rearrange("p h e -> p (h e)"), 0.0), w=[("state",)])
        S.add("dve", lambda e: e.memset(stbf[:].rearrange("p h e -> p (h e)"), 0.0), w=[("stbf",)])

        for t in range(16):
            xb = xbt[t % 2]
            kx = ("xbt", t % 2)
            xbt_cast(xb, t, kx)
            for cb in range(6):
                for kc in range(KC):
                    S.add("pe", lambda e, cb=cb, kc=kc, xb=xb: e.matmul(PS[cb][:], lhsT=xb[:, kc, :], rhs=Wr[:, kc, cb * 512:(cb + 1) * 512],
                                                                     start=(kc == 0), stop=(kc == KC - 1)),
                          r=[kx, ("wr", cb // 3)], w=[("ps", cb)])
            if dbg is not None and dbg < 2:
                continue
            cos16 = rcs[:, t, 0, :].unsqueeze(1).to_broadcast([128, 16, 64])
            sin8 = rcs[:, t, 1, :].unsqueeze(1).to_broadcast([128, 8, 64])
            P16 = PSALL[:, 0:1024].rearrange("p (g i) -> p g i", g=16, i=64)
            P8 = PSALL[:, 0:1024].rearrange("p (g f i) -> p g f i", g=8, f=2, i=64)
            tA16 = tA[:].rearrange("p g f i -> p (g f) i")
            pk = [("ps", 0), ("ps", 1)]
            S.add("dve", lambda e, cos16=cos16: e.tensor_tensor(out=tA16, in0=P16, in1=cos16, op=ALU.mult), r=pk + [("rcs",)], w=[("tA",)])
            S.add("dve", lambda e, sin8=sin8: e.tensor_tensor(out=tB[:], in0=P8[:, :, 1, :], in1=sin8, op=ALU.mult), r=pk + [("rcs",)], w=[("tB",)])
            S.add("dve", lambda e, sin8=sin8: e.tensor_tensor(out=tC[:], in0=P8[:, :, 0, :], in1=sin8, op=ALU.mult), r=pk + [("rcs",)], w=[("tC",)])
            S.add("pool", lambda e: e.tensor_tensor(out=qkr[:, :, 0, :], in0=tA[:, :, 0, :], in1=tB[:], op=ALU.subtract),
                  r=[("tA",), ("tB",)], w=[("qkr0",)])
            S.add("pool", lambda e: e.tensor_tensor(out=qkr[:, :, 1, :], in0=tA[:, :, 1, :], in1=tC[:], op=ALU.add),
                  r=[("tA",), ("tC",)], w=[("qkr1",)])
            qk_keys = [("qkr0",), ("qkr1",)]
            qflat = qkr[:, 0:4].rearrange("p h f i -> p h (f i)")
            kflat = qkr[:, 4:8].rearrange("p h f i -> p h (f i)")
            if dbg is not None and dbg < 3:
                continue
            S.add("pool", lambda e, kflat=kflat: e.tensor_tensor(out=kdec[:], in0=kflat, in1=gk.unsqueeze(2).to_broadcast([128, 4, 128]), op=ALU.mult),
                  r=qk_keys + [("rdec",)], w=[("kdec",)])
            S.add("act", lambda e: e.activation(out=vbf[:], in_=PSALL[:, 1024:2048], func=AF.Copy), r=[("ps", 2), ("ps", 3)], w=[("vbf",)])
            S.add("act", lambda e: e.activation(out=srg[:], in_=PSALL[:, 2048:3072], func=AF.Silu), r=[("ps", 4), ("ps", 5)], w=[("srg",)])
            if dbg is not None and dbg < 4:
                continue
            for h in range(4):
                S.add("pe", lambda e, h=h, qflat=qflat: e.matmul(PS[6][:, h * 128:(h + 1) * 128], lhsT=qflat[:, h, :], rhs=ident, start=True, stop=True),
                      r=qk_keys + [("constb",)], w=[("ps", 6)])
            for h in range(4):
                S.add("pe", lambda e, h=h, kflat=kflat: e.matmul(PS[7][:, h * 128:(h + 1) * 128], lhsT=kflat[:, h, :], rhs=ident, start=True, stop=True),
                      r=qk_keys + [("constb",)], w=[("ps", 7)])
            S.add("act", lambda e: e.activation(out=qT[:].rearrange("p h c -> p (h c)"), in_=PS[6][:], func=AF.Copy), r=[("ps", 6)], w=[("qT",)])
            S.add("dve", lambda e: e.tensor_tensor(out=qdT[:].rearrange("p h c -> p (h c)"), in0=PS[6][:], in1=GQ, op=ALU.mult),
                  r=[("ps", 6), ("rdec",)], w=[("qdT",)])
            S.add("act", lambda e: e.activation(out=kT[:].rearrange("p h c -> p (h c)"), in_=PS[7][:], func=AF.Copy), r=[("ps", 7)], w=[("kT",)])
            if dbg is not None and dbg < 5:
                continue
            for h in range(4):
                S.add("pe", lambda e, h=h: e.matmul(PS[4][:, h * 128:(h + 1) * 128], lhsT=kT[:, h, :], rhs=qT[:, h, :], start=True, stop=True),
                      r=[("kT",), ("qT",)], w=[("ps", 4)])
            S.add("dve", lambda e: e.tensor_tensor(out=sT[:].rearrange("p h c -> p (h c)"), in0=PS[4][:], in1=DT, op=ALU.mult),
                  r=[("ps", 4), ("rdec",)], w=[("sT",)])
            if dbg is not None and dbg < 6:
                continue
            for h in range(4):
                po = PSALL[:, h * 256:(h + 1) * 256]
                S.add("pe", lambda e, h=h, po=po: e.matmul(po, lhsT=sT[:, h, :], rhs=vbf[:, h * 256:(h + 1) * 256], start=True, stop=False),
                      r=[("sT",), ("vbf",)], w=[("ps", h // 2)])
                S.add("pe", lambda e, h=h, po=po: e.matmul(po, lhsT=qdT[:, h, :], rhs=stbf[:, h, :], start=False, stop=True),
                      r=[("qdT",), ("stbf",)], w=[("ps", h // 2)])
            if dbg is not None and dbg < 7:
                continue
            for h in range(4):
                pkv = PSALL[:, 1024 + h * 256:1024 + (h + 1) * 256]
                S.add("pe", lambda e, h=h, pkv=pkv: e.matmul(pkv, lhsT=kdec[:, h, :], rhs=vbf[:, h * 256:(h + 1) * 256], start=True, stop=True),
                      r=[("kdec",), ("vbf",)], w=[("ps", 2 + h // 2)])
            for h in range(4):
                pkv = PSALL[:, 1024 + h * 256:1024 + (h + 1) * 256]
                S.add("dve", lambda e, h=h, pkv=pkv: e.scalar_tensor_tensor(out=state[:, h, :], in0=state[:, h, :], scalar=GC[h], in1=pkv,
                                                                         op0=ALU.mult, op1=ALU.add),
                      r=[("state",), ("ps", 2 + h // 2)], w=[("state",)])
            S.add("act", lambda e: e.activation(out=stbf[:], in_=state[:], func=AF.Copy), r=[("state",)], w=[("stbf",)])
            if dbg is not None and dbg < 8:
                continue
            pk01 = [("ps", 0), ("ps", 1)]
            for h in range(4):
                po = PSALL[:, h * 256:(h + 1) * 256]
                S.add("dve", lambda e, h=h, po=po: e.bn_stats(out=stats[:, h, :], in_=po), r=pk01, w=[("stats",)])
            for h in range(4):
                S.add("dve", lambda e, h=h: e.bn_aggr(out=mv[:, h, :], in_=stats[:, h, :]), r=[("stats",)], w=[("mv",)])
            S.add("act", lambda e: e.activation(out=rs[:], in_=mv[:, :, 1], func=AF.Sqrt, bias=epsc[:, 1:2]), r=[("mv",), ("epsc",)], w=[("rs",)])
            S.add("dve", lambda e: e.reciprocal(out=rs[:], in_=rs[:]), r=[("rs",)], w=[("rs",)])
            for h in range(4):
                po = PSALL[:, h * 256:(h + 1) * 256]
                S.add("dve", lambda e, h=h, po=po: e.tensor_scalar(out=yn[:, h * 256:(h + 1) * 256], in0=po, scalar1=mv[:, h, 0:1], scalar2=rs[:, h:h + 1],
                                                                op0=ALU.subtract, op1=ALU.mult),
                      r=pk01 + [("mv",), ("rs",)], w=[("yn",)])
            S.add("pool", lambda e: e.tensor_tensor(out=yrt[:], in0=yn[:], in1=srg[:], op=ALU.mult), r=[("yn",), ("srg",)], w=[("yrt",)])
            if dbg is not None and dbg < 9:
                continue
            for c in range(8):
                S.add("pe", lambda e, c=c: e.matmul(PSALL[:, 3072 + c * 128:3072 + (c + 1) * 128], lhsT=yrt[:, c * 128:(c + 1) * 128], rhs=ident, start=True, stop=True),
                      r=[("yrt",), ("constb",)], w=[("ps", 6 + c // 4)])
            S.add("act", lambda e, t=t: e.activation(out=yretT[:, :, tls(t)], in_=PSALL[:, 3072:4096].rearrange("p (c q) -> p c q", c=8), func=AF.Copy),
                  r=[("ps", 6), ("ps", 7)], w=[("yretT", t // 4)])

    def pass_moba(s):
        Wm = carve(0, [128, KC, 1536], BF16)
        kTa = carve(24 * K, [128, 4, T], BF16)
        vaug = carve(96 * K, [128, 16, 8, 66], BF16)
        ymobaT = carve(80 * K, [128, 4, T], BF16)
        o = 96 * K + 16896
        mcs = carve(o, [128, 16, 2, 8], F32); o += 1 * K
        mcb = carve(o, [128, 1152], BF16); o += 2304
        tri01 = mcb[:, 0:128]
        E64 = mcb[0:64, 128:1152].rearrange("p (n k) -> p n k", n=8)
        E64hi = mcb[64:128, 128:1152].rearrange("p (n k) -> p n k", n=8)
        xbt = [carve(o + 2 * K * i, [128, KC, 128], BF16) for i in range(2)]; o += 4 * K
        qktm = carve(o, [128, 16, 64], BF16); o += 2 * K
        r1 = carve(o, [128, 16, 8], F32); o += 512
        r2 = carve(o, [128, 16, 8], F32); o += 512
        qTb = carve(o, [128, 4, 256], BF16); o += 2 * K
        ksum = carve(o, [128, 4, 8], F32); o += 128
        ksb = carve(o, [128, 4, 64], BF16); o += 512
        gs = carve(o, [128, 8, 8], F32); o += 256
        cmp_ = carve(o, [128, 8, 8, 8], F32); o += 2 * K
        cntt = carve(o, [128, 8, 8], F32); o += 256
        negm = carve(o, [128, 8, 32], BF16); o += 512
        negT = carve(o, [128, 8, 256], BF16); o += 4 * K
        expP = [carve(o + 512 * i, [128, 256], BF16) for i in range(2)]; o += 1 * K
        rec = carve(o, [128, 2, 2, 4], F32); o += 64
        ytm = carve(o, [128, 2, 512], BF16); o += 2 * K
        qkf = carve(o, [128, 1024], F32); o += 4 * K
        assert o <= 142 * K, o

        cast_dma_cols(Wm, win_d, 3072, 4608, "wm")
        S.add("sp", lambda e: e.dma_start(out=mcs[:], in_=mcs_d), w=[("mcs",)], dsem="c_mcs")
        S.add("sp", lambda e: e.dma_start(out=mcb[:], in_=mcb_d), w=[("mcb",)], dsem="c_mcb")
        S.add("dve", lambda e: e.memset(vaug[:].rearrange("p t h d -> p (t h d)"), 1.0), w=[("vaug", t) for t in range(16)])
        S.add("dve", lambda e: e.memset(negT[:].rearrange("p h q -> p (h q)"), 0.0), w=[("negT",)])
        S.add("dve", lambda e: e.memset(negm[:].rearrange("p h n -> p (h n)"), 0.0), w=[("negm",)])
        S.add("dve", lambda e: e.memset(ksum[:].rearrange("p c n -> p (c n)"), 0.0), w=[("ksum",)])
        S.add("dve", lambda e: e.memset(ksb[:].rearrange("p c n -> p (c n)"), 0.0), w=[("ksb",)])

        PQK = PSALL[:, 0:1024].rearrange("p (g d) -> p g d", g=16, d=64)
        sti = 0
        for b in range(8):
            if dbg == 100:
                break
            for tt in range(2):
                t = 2 * b + tt
                xb = xbt[t % 2]
                kx = ("xbt", t % 2)
                xbt_cast(xb, t, kx)
                for cb in range(3):
                    for kc in range(KC):
                        S.add("pe", lambda e, cb=cb, kc=kc, xb=xb: e.matmul(PS[cb][:], lhsT=xb[:, kc, :], rhs=Wm[:, kc, cb * 512:(cb + 1) * 512],
                                                                         start=(kc == 0), stop=(kc == KC - 1)),
                              r=[kx, ("wm", 0)], w=[("ps", cb)])
                if dbg == 101:
                    continue
                pk = [("ps", 0), ("ps", 1)]
                cosb = mcs[:, t, 0, :].unsqueeze(1).to_broadcast([128, 16, 8])
                sinb = mcs[:, t, 1, :].unsqueeze(1).to_broadcast([128, 16, 8])
                qkf16 = qkf[:].rearrange("p (g d) -> p g d", g=16, d=64)
                x1 = qkf16[:, :, 0:8]
                x2 = qkf16[:, :, 8:16]
                S.add("act", lambda e: e.activation(out=qkf[:], in_=PSALL[:, 0:1024], func=AF.Copy), r=pk, w=[("qkf",)])
                S.add("pool", lambda e, qkf16=qkf16: e.tensor_copy(out=qktm[:], in_=qkf16), r=[("qkf",)], w=[("qktm_c",)])
                S.add("pool", lambda e, x1=x1, cosb=cosb: e.tensor_tensor(out=r1[:], in0=x1, in1=cosb, op=ALU.mult), r=[("qkf",), ("mcs",)], w=[("r1",)])
                S.add("pool", lambda e, x2=x2, sinb=sinb: e.tensor_tensor(out=r2[:], in0=x2, in1=sinb, op=ALU.mult), r=[("qkf",), ("mcs",)], w=[("r2",)])
                S.add("pool", lambda e: e.tensor_tensor(out=qktm[:, :, 0:8], in0=r1[:], in1=r2[:], op=ALU.subtract), r=[("r1",), ("r2",), ("qktm_c",)], w=[("qktm_a",)])
                S.add("pool", lambda e, x1=x1, sinb=sinb: e.tensor_tensor(out=r1[:], in0=x1, in1=sinb, op=ALU.mult), r=[("qkf",), ("mcs",)], w=[("r1",)])
                S.add("pool", lambda e, x2=x2, cosb=cosb: e.tensor_tensor(out=r2[:], in0=x2, in1=cosb, op=ALU.mult), r=[("qkf",), ("mcs",)], w=[("r2",)])
                S.add("pool", lambda e: e.tensor_tensor(out=qktm[:, :, 8:16], in0=r1[:], in1=r2[:], op=ALU.add), r=[("r1",), ("r2",), ("qktm_c",)], w=[("qktm_b",)])
                qk_keys = [("qktm_a",), ("qktm_b",), ("qktm_c",)]
                if dbg == 102:
                    continue
                S.add("act", lambda e, t=t: e.activation(out=vaug[:, t, :, 0:64], in_=PS[2][:].rearrange("p (h d) -> p h d", h=8), func=AF.Copy),
                      r=[("ps", 2)], w=[("vaug", t)])
                if dbg == 103:
                    continue
                qf = qktm[:, 0:8, :].rearrange("p h d -> p (h d)")
                kf = qktm[:, 8:16, :].rearrange("p h d -> p (h d)")
                for c in range(4):
                    S.add("pe", lambda e, c=c, qf=qf: e.matmul(PS[3][:, c * 128:(c + 1) * 128], lhsT=qf[:, c * 128:(c + 1) * 128], rhs=ident, start=True, stop=True),
                          r=qk_keys + [("constb",)], w=[("ps", 3)])
                for c in range(4):
                    S.add("pe", lambda e, c=c, kf=kf: e.matmul(PS[4][:, c * 128:(c + 1) * 128], lhsT=kf[:, c * 128:(c + 1) * 128], rhs=ident, start=True, stop=True),
                          r=qk_keys + [("constb",)], w=[("ps", 4)])
                S.add("act", lambda e, tt=tt: e.activation(out=qTb[:, :, tt * 128:(tt + 1) * 128], in_=PS[3][:].rearrange("p (c q) -> p c q", c=4), func=AF.Copy),
                      r=[("ps", 3)], w=[("qTb", tt)])
                S.add("act", lambda e, t=t: e.activation(out=kTa[:, :, tls(t)], in_=PS[4][:].rearrange("p (c q) -> p c q", c=4), func=AF.Copy),
                      r=[("ps", 4)], w=[("kTa", t)])
            if dbg is not None and (dbg < 11 or (100 <= dbg < 120)):
                continue
            glvl = 4 if (dbg is None or dbg < 120) else dbg - 120
            if b >= 4 and not (dbg is not None and dbg < 12):
                for tt in range(2):
                    for c in range(4):
                        S.add("pe", lambda e, c=c, tt=tt: e.matmul(PS[2][:, c * 64:(c + 1) * 64], lhsT=qTb[:, c, tt * 128:(tt + 1) * 128],
                                                                rhs=ksb[:, c, :], start=True, stop=True),
                              r=[("qTb", tt), ("ksb",)], w=[("ps", 2)])
                    S.add("act", lambda e: e.activation(out=gs[:].rearrange("p (c j) n -> p c (j n)", c=4), in_=PS[2][:, 0:256].rearrange("p (c x) -> p c x", c=4)[:, :, 0:16], func=AF.Copy),
                          r=[("ps", 2)], w=[("gs",)])
                    if glvl < 2:
                        continue
                    gm = gs[:, :, 0:b].unsqueeze(2).to_broadcast([128, 8, b, b])
                    gn = gs[:, :, 0:b].unsqueeze(3).to_broadcast([128, 8, b, b])
                    S.add("dve", lambda e, gm=gm, gn=gn, b=b: e.tensor_tensor(out=cmp_[:, :, 0:b, 0:b], in0=gm, in1=gn, op=ALU.is_gt), r=[("gs",)], w=[("cmp",)])
                    S.add("dve", lambda e, b=b: e.tensor_reduce(out=cntt[:, :, 0:b], in_=cmp_[:, :, 0:b, 0:b], axis=AX.X, op=ALU.add), r=[("cmp",)], w=[("cnt",)])
                    S.add("dve", lambda e, b=b: e.tensor_scalar(out=negm[:, :, 0:b], in0=cntt[:, :, 0:b], scalar1=2.5, scalar2=NEG, op0=ALU.is_gt, op1=ALU.mult),
                          r=[("cnt",)], w=[("negm",)])
                    if glvl < 3:
                        continue
                    for h in range(8):
                        S.add("pe", lambda e, h=h: e.matmul(PSALL[0:32, h * 128:(h + 1) * 128], lhsT=negm[:, h, :], rhs=ident, start=True, stop=True),
                              r=[("negm",), ("constb",)], w=[("ps", h // 4)])
                    S.add("act", lambda e, tt=tt: e.activation(out=negT[0:8, :, tt * 128:(tt + 1) * 128], in_=PSALL[0:8, 0:1024].rearrange("p (h q) -> p h q", h=8), func=AF.Copy),
                          r=[("ps", 0), ("ps", 1)], w=[("negT",)])
                    S.add("act", lambda e, tt=tt: e.activation(out=negT[64:72, :, tt * 128:(tt + 1) * 128], in_=PSALL[0:8, 0:1024].rearrange("p (h q) -> p h q", h=8), func=AF.Copy),
                          r=[("ps", 0), ("ps", 1)], w=[("negT",)])
            for h in range(8):
                hp = slice((h % 2) * 64, (h % 2) * 64 + 64)
                c = h // 2
                hg = h // 4
                pos = [PS[4 + hg], PS[6 + hg]]
                okeys = [("ps", 4 + hg), ("ps", 6 + hg)]
                ktiles = []
                for n in range(b):
                    ktiles.append((2 * n, 0, 256, n, False))
                    ktiles.append((2 * n + 1, 0, 256, n, False))
                ktiles.append((2 * b, 0, 256, None, True))
                ktiles.append((2 * b + 1, 128, 256, None, True))
                first = [True, True]
                nk = len(ktiles)
                for ki, (kt, q0, q1, n, own) in enumerate(ktiles):
                    st = PS[2 + sti % 2]
                    kst = ("ps", 2 + sti % 2)
                    ex = expP[sti % 2]
                    kex = ("expP", sti % 2)
                    sti += 1
                    usemask = (n is not None) and (b >= 4) and not (dbg is not None and dbg < 12) and glvl >= 4
                    S.add("pe", lambda e, st=st, hp=hp, c=c, kt=kt, q0=q0, q1=q1, usemask=usemask: e.matmul(
                        st[:, q0:q1], lhsT=kTa[hp, c, tls(kt)], rhs=qTb[hp, c, q0:q1], start=True, stop=(not usemask)),
                        r=[("kTa", kt), ("qTb", 0), ("qTb", 1)], w=[kst])
                    if usemask:
                        S.add("pe", lambda e, st=st, n=n, h=h, q0=q0, q1=q1, hp=hp: e.matmul(st[:, q0:q1], lhsT=(E64 if h % 2 == 0 else E64hi)[:, n, :], rhs=negT[hp, h, q0:q1], start=False, stop=True),
                              r=[("negT",), ("mcb",)], w=[kst])
                    S.add("act", lambda e, st=st, ex=ex, q0=q0, q1=q1: e.activation(out=ex[:, q0:q1], in_=st[:, q0:q1], func=AF.Exp, scale=0.125), r=[kst], w=[kex])
                    if own:
                        S.add("pool", lambda e, ex=ex, q0=q0: e.tensor_tensor(out=ex[:, q0:q0 + 128], in0=ex[:, q0:q0 + 128], in1=tri01, op=ALU.mult),
                              r=[kex, ("mcb",)], w=[kex])
                    for qt in range(2):
                        if q0 > qt * 128:
                            continue
                        last = (ki == nk - 1) if qt == 1 else (ki == nk - 2)
                        S.add("pe", lambda e, qt=qt, ex=ex, kt=kt, h=h, fs=first[qt], last=last, po=pos[qt]: e.matmul(
                            po[:, (h % 4) * 66:(h % 4) * 66 + 66], lhsT=ex[:, qt * 128:(qt + 1) * 128], rhs=vaug[:, kt, h, :], start=fs, stop=last),
                            r=[kex, ("vaug", kt)], w=[okeys[qt]])
                        first[qt] = False
                if h % 4 == 3:
                    for qt in range(2):
                        po = pos[qt][:, 0:264].rearrange("p (h d) -> p h d", h=4)
                        S.add("dve", lambda e, po=po, qt=qt, hg=hg: e.reciprocal(out=rec[:, qt, hg, :], in_=po[:, :, 64]), r=[okeys[qt]], w=[("rec", qt, hg)])
                        S.add("dve", lambda e, po=po, qt=qt, hg=hg: e.tensor_tensor(
                            out=ytm[:, qt, hg * 256:(hg + 1) * 256].rearrange("p (h d) -> p h d", h=4), in0=po[:, :, 0:64],
                            in1=rec[:, qt, hg, :].unsqueeze(2).to_broadcast([128, 4, 64]), op=ALU.mult),
                            r=[okeys[qt], ("rec", qt, hg)], w=[("ytm", qt, hg)])
            S.add("dve", lambda e, b=b: e.tensor_reduce(out=ksum[:, :, b], in_=kTa[:, :, b * 256:(b + 1) * 256], axis=AX.X, op=ALU.add),
                  r=[("kTa", 2 * b), ("kTa", 2 * b + 1)], w=[("ksum",)])
            S.add("act", lambda e: e.activation(out=ksb[0:64, :, 0:8], in_=ksum[0:64, :, :], func=AF.Copy), r=[("ksum",)], w=[("ksb",)])
            S.add("act", lambda e: e.activation(out=ksb[64:128, :, 8:16], in_=ksum[64:128, :, :], func=AF.Copy), r=[("ksum",)], w=[("ksb",)])
            for qt in range(2):
                for c in range(4):
                    S.add("pe", lambda e, c=c, qt=qt: e.matmul(PS[3][:, c * 128:(c + 1) * 128], lhsT=ytm[:, qt, c * 128:(c + 1) * 128], rhs=ident, start=True, stop=True),
                          r=[("ytm", qt, 0), ("ytm", qt, 1), ("constb",)], w=[("ps", 3)])
                S.add("act", lambda e, t=2 * b + qt: e.activation(out=ymobaT[:, :, tls(t)], in_=PS[3][:].rearrange("p (c q) -> p c q", c=4), func=AF.Copy),
                      r=[("ps", 3)], w=[("ymobaT", t // 4)])

    def pass_o1(s):
        RP = carve(0, [128, KC, D], BF16)
        WGA = carve(16 * K, [128, KC, D], BF16)
        WGB = carve(32 * K, [128, KC, D], BF16)
        yretT = carve(48 * K, [128, KC, T], BF16)
        ymobaT = carve(80 * K, [128, 4, T], BF16)
        MP = carve(96 * K, [128, 4, D], BF16)
        xblk = carve(104 * K, [128, KC, 512], BF16)
        ufin = carve(112 * K, [128, KC, 512], BF16)
        sga = carve(120 * K, [128, 512], F32)
        sgb = carve(122 * K, [128, 512], F32)
        u1 = carve(124 * K, [128, 512], F32)
        u2 = carve(126 * K, [128, 512], F32)
        cast_dma_cols(RP, rp_d, 0, D, "rp", step=D)
        cast_dma_cols(WGA, win_d, 4608, 5632, "wga", step=D)
        cast_dma_cols(WGB, win_d, 5632, 6656, "wgb", step=D)
        cast_dma_cols(MP, mp_d, 0, D, "mp", step=D)
        for tb in range(4):
            sl = tbs(tb)
            S.add("act", lambda e, sl=sl: e.activation(out=xblk[:], in_=R[:, :, sl], func=AF.Copy), r=Rkeys(tb), w=[("xblk",)])
            for m in range(KC):
                o4 = 4 * (m % 2)
                ms = slice(m * 128, (m + 1) * 128)
                for kc in range(KC):
                    S.add("pe", lambda e, kc=kc, ms=ms, o4=o4, sl=sl: e.matmul(PS[o4][:], lhsT=RP[:, kc, ms], rhs=yretT[:, kc, sl], start=(kc == 0), stop=(kc == KC - 1)),
                          r=[("rp", 0), ("yretT", tb)], w=[("ps", o4)])
                for kc in range(KC):
                    S.add("pe", lambda e, kc=kc, ms=ms, o4=o4: e.matmul(PS[o4 + 1][:], lhsT=WGA[:, kc, ms], rhs=xblk[:, kc, :], start=(kc == 0), stop=(kc == KC - 1)),
                          r=[("wga", 0), ("xblk",)], w=[("ps", o4 + 1)])
                for c in range(4):
                    S.add("pe", lambda e, c=c, ms=ms, o4=o4, sl=sl: e.matmul(PS[o4 + 2][:], lhsT=MP[:, c, ms], rhs=ymobaT[:, c, sl], start=(c == 0), stop=(c == 3)),
                          r=[("mp", 0), ("ymobaT", tb)], w=[("ps", o4 + 2)])
                for kc in range(KC):
                    S.add("pe", lambda e, kc=kc, ms=ms, o4=o4: e.matmul(PS[o4 + 3][:], lhsT=WGB[:, kc, ms], rhs=xblk[:, kc, :], start=(kc == 0), stop=(kc == KC - 1)),
                          r=[("wgb", 0), ("xblk",)], w=[("ps", o4 + 3)])
                S.add("act", lambda e, o4=o4: e.activation(out=sga[:], in_=PS[o4 + 1][:], func=AF.Sigmoid), r=[("ps", o4 + 1)], w=[("sga",)])
                S.add("act", lambda e, o4=o4: e.activation(out=sgb[:], in_=PS[o4 + 3][:], func=AF.Sigmoid), r=[("ps", o4 + 3)], w=[("sgb",)])
                S.add("dve", lambda e, o4=o4: e.tensor_tensor(out=u1[:], in0=PS[o4][:], in1=sga[:], op=ALU.mult), r=[("ps", o4), ("sga",)], w=[("u1",)])
                S.add("dve", lambda e, o4=o4: e.tensor_tensor(out=u2[:], in0=PS[o4 + 2][:], in1=sgb[:], op=ALU.mult), r=[("ps", o4 + 2), ("sgb",)], w=[("u2",)])
                S.add("pool", lambda e, m=m: e.tensor_tensor(out=ufin[:, m, :], in0=u1[:], in1=u2[:], op=ALU.add), r=[("u1",), ("u2",)], w=[("ufin",)])
            S.add("act", lambda e, sl=sl: e.activation(out=yretT[:, :, sl], in_=ufin[:], func=AF.Copy), r=[("ufin",)], w=[("yretT", tb)])

    def pass_o2(s):
        WO = carve(0, [128, KC, D], BF16)
        U = carve(48 * K, [128, KC, T], BF16)
        cast_dma_cols(WO, wo_d, 0, D, "wo", step=D)
        i2 = 0
        for tb in range(4):
            sl = tbs(tb)
            for m in range(KC):
                pb = i2 % 2
                i2 += 1
                ms = slice(m * 128, (m + 1) * 128)
                for kc in range(KC):
                    S.add("pe", lambda e, kc=kc, ms=ms, pb=pb, sl=sl: e.matmul(PS[pb][:], lhsT=WO[:, kc, ms], rhs=U[:, kc, sl], start=(kc == 0), stop=(kc == KC - 1)),
                          r=[("wo", 0), ("yretT", tb)], w=[("ps", pb)])
                S.add("dve", lambda e, m=m, pb=pb, sl=sl: e.scalar_tensor_tensor(out=R[:, m, sl], in0=R[:, m, sl], scalar=ALPHA, in1=PS[pb][:], op0=ALU.mult, op1=ALU.add),
                      r=[("ps", pb), ("R", tb, m)], w=[("R", tb, m)])
            layer_norm(tb, 1)

    def mixer_phase(s):
        if "r" in mixp:
            pass_ret(s)
            S.barrier()
        if "m" in mixp:
            pass_moba(s)
            S.barrier()
        if "o1" in mixp:
            pass_o1(s)
            S.barrier()
        if "o2" in mixp:
            pass_o2(s)

    for s in range(nseq):
        if "load" in phases:
            for tb in range(4):
                S.add("sp", lambda e, tb=tb: e.dma_start(out=R[:, :, tbs(tb)], in_=xT[:, :, s * T + tb * 512: s * T + (tb + 1) * 512]),
                      w=Rkeys(tb), dsem=("xin", tb))
            S.barrier()
        if "ffn1" in phases:
            ffn_phase(s, 0)
            S.barrier()
        if "mix" in phases:
            mixer_phase(s)
            S.barrier()
        if "ffn2" in phases:
            ffn_phase(s, 1)
            S.barrier()
        out_phase(s)
    S.add("sp", lambda e: e.nop(), r=[k for tb in range(4) for k in Rkeys(tb)], w=[k for tb in range(4) for k in Rkeys(tb)])

    engines = {
        "pe": ("tensor", nc.tensor),
        "act": ("scalar", nc.scalar),
        "dve": ("vector", nc.vector),
        "pool": ("gpsimd", nc.gpsimd),
        "sp": ("sync", nc.sync),
    }
    S.emit(nc, engines)
    return nc, S


def _feat_major(v):
    return np.ascontiguousarray(np.asarray(v, np.float32).reshape(KC, 128).T)


def _module_consts():
    c = {}
    pos = np.arange(T, dtype=np.float32)
    inv = (1.0 / (np.float32(10000.0) ** np.linspace(0.0, 1.0, 64, dtype=np.float32))).astype(np.float32)
    ang = (pos[:, None] * inv[None, :]).astype(np.float32)
    rcs = np.stack([np.cos(ang), np.sin(ang)], axis=1).astype(np.float32)
    c["rcs"] = np.ascontiguousarray(rcs.reshape(16, 128, 2, 64).transpose(1, 0, 2, 3))
    inv2 = (1.0 / (np.float32(500000.0) ** (np.arange(0, 16, 2, dtype=np.float32) / np.float32(16)))).astype(np.float32)
    ang2 = (pos[:, None] * inv2[None, :]).astype(np.float32)
    mcs = np.stack([np.cos(ang2), np.sin(ang2)], axis=1).astype(np.float32)
    c["mcs"] = np.ascontiguousarray(mcs.reshape(16, 128, 2, 8).transpose(1, 0, 2, 3))
    g = 1.0 - 2.0 ** (-5.0 - np.arange(4, dtype=np.float64))
    idx = np.arange(128, dtype=np.float64)
    sc = 128.0 ** -0.5
    rdec = np.zeros((128, 1032), np.float32)
    for h in range(4):
        diff = idx[None, :] - idx[:, None]
        dt = np.where(diff >= 0, g[h] ** np.maximum(diff, 0.0), 0.0) * sc
        rdec[:, h * 128:(h + 1) * 128] = dt
        rdec[:, 512 + h * 128:512 + (h + 1) * 128] = (g[h] ** (idx + 1.0))[None, :]
        rdec[:, 1024 + h] = g[h] ** (127.0 - idx) * sc
    c["rdec"] = rdec
    mcb = np.zeros((128, 1152), np.float32)
    mcb[:, 0:128] = (idx[:, None] <= idx[None, :]).astype(np.float32)
    for n in range(8):
        mcb[n, 128 + n * 128:128 + (n + 1) * 128] = 1.0
        mcb[64 + n, 128 + n * 128:128 + (n + 1) * 128] = 1.0
    c["mconstb"] = mcb.astype(ml_dtypes.bfloat16)
    return c


def prep_shared(inp):
    sh = {}
    for f, (gu, dn) in enumerate([("ffn1_w_gu", "ffn1_w_down"), ("ffn2_w_gu", "ffn2_w_down")]):
        w = np.asarray(inp[gu], np.float32)[0]
        w = w.reshape(KC, 128, 2, NJ, 128)
        sh["wgu%d" % (f + 1)] = np.ascontiguousarray(w.transpose(3, 1, 2, 0, 4))
        sh["wd%d" % (f + 1)] = np.ascontiguousarray(np.asarray(inp[dn], np.float32)[0].reshape(NJ, 128, D))
    lnp = np.stack([_feat_major(inp[k][0]) for k in ("ln1_g", "ln1_b", "lnm_g", "lnm_b", "ln2_g", "ln2_b")], axis=1)
    sh["lnp"] = np.ascontiguousarray(lnp)
    def kmajor(w, nk):
        w = np.asarray(w, np.float32)
        return np.ascontiguousarray(w.reshape(nk, 128, w.shape[-1]).transpose(1, 0, 2))
    sh["win"] = kmajor(inp["w_in"][0], KC)
    sh["retp"] = kmajor(inp["ret_proj"][0], KC)
    sh["mobp"] = kmajor(inp["moba_proj"][0], 4)
    sh["wout"] = kmajor(inp["w_out"][0], KC)
    sh.update(_module_consts())
    cb = np.zeros((128, 256), np.float32)
    cb[:, 0:128] = np.eye(128, dtype=np.float32)
    cb[:, 128:256] = 1.0 / 1024.0
    sh["constb"] = cb.astype(ml_dtypes.bfloat16)
    return sh


def shard_x(x, nseq=SEQ_PER_CORE, ncores=NCORES):
    x = np.asarray(x, np.float32)
    maps = []
    for c in range(ncores):
        xc = x[c * nseq:(c + 1) * nseq].reshape(nseq * T, KC, 128)
        maps.append(np.ascontiguousarray(xc.transpose(2, 1, 0)))
    return maps


def unshard(outs, nseq=SEQ_PER_CORE):
    res = []
    for o in outs:
        res.append(np.asarray(o, np.float32).transpose(2, 1, 0).reshape(nseq, T, D))
    return np.ascontiguousarray(np.concatenate(res, axis=0))


_CACHE = {}


def kernel(**inputs):
    if "nc" not in _CACHE:
        _CACHE["nc"] = build_program()[0]
    nc = _CACHE["nc"]
    sh = prep_shared(inputs)
    xs = shard_x(inputs["x"])
    in_maps = []
    for c in range(NCORES):
        m = dict(sh)
        m["xT"] = xs[c]
        in_maps.append(m)
    res = run_bass_kernel_spmd(nc, in_maps, core_ids=list(range(NCORES)))
    return unshard([r["outT"] for r in res.results])
```

```python
import math
import numpy as np
import ml_dtypes
import concourse.bass as bass
import concourse.mybir as mybir
from concourse.bass_utils import run_bass_kernel_spmd

F32 = mybir.dt.float32
BF16 = mybir.dt.bfloat16
AF = mybir.ActivationFunctionType
ALU = mybir.AluOpType
AX = mybir.AxisListType

D = 1024
T = 2048
KC = 8
DFF = 2816
NJ = 22
NCORES = 8
SEQ_PER_CORE = 2
ALPHA = 2.0 ** 0.25
LN_EPS = 1e-5
GN_EPS = 1e-5
WIN = 6656
GROUPS = [(0, 8), (8, 15), (15, 22)]
NEG = -30000.0


class Op:
    __slots__ = ("eng", "fn", "deps", "dsem", "sig", "cnt")

    def __init__(self, eng, fn, deps, dsem):
        self.eng = eng
        self.fn = fn
        self.deps = deps
        self.dsem = dsem
        self.sig = dsem is not None
        self.cnt = 0


class Sched:
    def __init__(self):
        self.ops = []
        self.lastw = {}
        self.readers = {}
        self.bar = set()
        self.last_stream = {}

    def add(self, eng, fn, r=(), w=(), dsem=None):
        i = len(self.ops)
        deps = set(self.bar)
        for k in r:
            j = self.lastw.get(k)
            if j is not None:
                deps.add(j)
        for k in w:
            j = self.lastw.get(k)
            if j is not None:
                deps.add(j)
            rd = self.readers.get(k)
            if rd:
                deps.update(rd.values())
        stream = ("dma", dsem) if dsem is not None else eng
        for k in r:
            self.readers.setdefault(k, {})[stream] = i
        for k in w:
            self.lastw[k] = i
            self.readers[k] = {}
        self.last_stream[stream] = i
        self.ops.append(Op(eng, fn, deps, dsem))
        return i

    def barrier(self):
        self.bar = set(self.last_stream.values())

    def emit(self, nc, engines):
        ops = self.ops

        def stream_of(o):
            return ("dma", o.dsem) if o.dsem is not None else o.eng

        for i, o in enumerate(ops):
            best = {}
            for j in o.deps:
                p = ops[j]
                st = stream_of(p)
                if p.dsem is None and p.eng == "pe" and o.eng == "pe" and o.dsem is None:
                    continue
                if st not in best or best[st] < j:
                    best[st] = j
            o.deps = best
            for j in best.values():
                ops[j].sig = True
        cnt = {}
        for o in ops:
            if o.sig:
                st = stream_of(o)
                inc = 16 if o.dsem is not None else 1
                cnt[st] = cnt.get(st, 0) + inc
                o.cnt = cnt[st]
        sems = {}
        import contextlib
        stack = contextlib.ExitStack()
        for k, st in enumerate(cnt.keys()):
            sems[st] = stack.enter_context(nc.semaphore("s%d" % k))
        self.max_counts = dict(cnt)
        with stack:
            with nc.Block() as block:
                for ename, (deco, _h) in engines.items():
                    my = [(i, o) for i, o in enumerate(ops) if o.eng == ename]
                    if not my:
                        continue

                    def body(eng, my=my, ename=ename):
                        waited = {}
                        for i, o in my:
                            for st, j in o.deps.items():
                                v = ops[j].cnt
                                if waited.get(st, 0) < v:
                                    eng.wait_ge(sems[st], v)
                                    waited[st] = v
                            ins = o.fn(eng)
                            if o.sig:
                                st = stream_of(o)
                                ins.then_inc(sems[st], 16 if o.dsem is not None else 1)
                                if o.dsem is None:
                                    pass
                    getattr(block, deco)(body)


def _tile(nc, name, shape, dt):
    return nc.sbuf_tensor(name, list(shape), dt).__enter__()


def _ptile(nc, name, shape, dt):
    return nc.psum_tensor(name, list(shape), dt).__enter__()


def build_program(nseq=SEQ_PER_CORE, phases=("ffn1", "mix", "ffn2"), dbg=None, mixp=("r", "m", "o1", "o2")):
    nc = bass.Bass("TRN2", target_bir_lowering=False)
    S = Sched()
    NT = nseq * T

    def din(name, shape, dt=F32):
        return nc.dram_tensor(name, list(shape), dt, kind="ExternalInput").ap()

    xT = din("xT", [128, KC, NT])
    outT = nc.dram_tensor("outT", [128, KC, NT], F32, kind="ExternalOutput").ap()
    wgu_d = [din("wgu1", [NJ, 128, 2, KC, 128]), din("wgu2", [NJ, 128, 2, KC, 128])]
    wd_d = [din("wd1", [NJ, 128, D]), din("wd2", [NJ, 128, D])]
    lnp_d = din("lnp", [128, 6, KC])
    cb_d = din("constb", [128, 256], BF16)
    win_d = din("win", [128, KC, WIN])
    rp_d = din("retp", [128, KC, D])
    mp_d = din("mobp", [128, 4, D])
    wo_d = din("wout", [128, KC, D])
    rcs_d = din("rcs", [128, 16, 2, 64])
    mcs_d = din("mcs", [128, 16, 2, 8])
    rdec_d = din("rdec", [128, 1032])
    mcb_d = din("mconstb", [128, 1152], BF16)

    R = _tile(nc, "R", [128, KC, T], F32)
    ARENA = _tile(nc, "ARENA", [128, 72704], BF16)
    lnp = _tile(nc, "lnp_sb", [128, 6, KC], F32)
    constb = _tile(nc, "constb_sb", [128, 256], BF16)
    ident = constb[:, 0:128]
    ones_div = constb[:, 128:256]
    epsc = _tile(nc, "epsc", [128, 2], F32)

    def carve(off_bytes, shape, dt):
        n = int(np.prod(shape[1:]))
        if dt == BF16:
            a = ARENA[:, off_bytes // 2: off_bytes // 2 + n]
        else:
            a = ARENA[:, off_bytes // 2: off_bytes // 2 + 2 * n].bitcast(F32)
        if len(shape) == 2:
            return a
        names = " ".join("d%d" % i for i in range(1, len(shape)))
        kw = {"d%d" % i: shape[i] for i in range(1, len(shape))}
        return a.rearrange("p (%s) -> p %s" % (names, names), **kw)

    K = 1024
    Xb = carve(0, [128, KC, T], BF16)
    actT = carve(32 * K, [128, 8, T], BF16)
    Wgu = [carve(64 * K + 4 * K * b, [128, 2, KC, 128], BF16) for b in range(3)]
    Wd = carve(76 * K, [128, 8, D], BF16)
    zb = carve(92 * K, [128, KC, 512], BF16)
    zsq = carve(100 * K, [128, KC, 512], BF16)
    mean_sb = carve(108 * K, [128, 512], F32)
    rstd_sb = carve(110 * K, [128, 512], F32)
    var_sb = carve(112 * K, [128, 512], F32)
    sg = [carve(114 * K + 1 * K * b, [128, 512], BF16) for b in range(2)]

    PSALL = _ptile(nc, "psall", [128, 4096], F32)
    PS = [PSALL[:, b * 512:(b + 1) * 512] for b in range(8)]

    def PSB(b):
        return PSALL[:, b * 512:(b + 1) * 512].bitcast(BF16)

    S.add("sp", lambda e: e.dma_start(out=lnp[:], in_=lnp_d), w=[("lnp",)], dsem="c_lnp")
    S.add("sp", lambda e: e.dma_start(out=constb[:], in_=cb_d), w=[("constb",)], dsem="c_cb")
    S.add("dve", lambda e: e.memset(epsc[:, 0:1], LN_EPS), w=[("epsc",)])
    S.add("dve", lambda e: e.memset(epsc[:, 1:2], GN_EPS), w=[("epsc",)])

    cnt = {"wgu": 0}

    def tbs(tb):
        return slice(tb * 512, (tb + 1) * 512)

    def Rkeys(tb):
        return [("R", tb, kc) for kc in range(KC)]

    def layer_norm(tb, gi):
        sl = tbs(tb)
        Rb = R[:, :, sl]
        S.add("pool", lambda e: e.tensor_copy(out=zb[:], in_=Rb), r=Rkeys(tb), w=[("zb",)])
        S.add("act", lambda e: e.activation(out=zsq[:], in_=Rb, func=AF.Square), r=Rkeys(tb), w=[("zsq",)])
        pm, pq = PS[6], PS[7]
        for kc in range(KC):
            S.add("pe", lambda e, kc=kc: e.matmul(pm[:], lhsT=ones_div, rhs=zb[:, kc, :], start=(kc == 0), stop=(kc == KC - 1)),
                  r=[("zb",), ("constb",)], w=[("ps", 6)])
        for kc in range(KC):
            S.add("pe", lambda e, kc=kc: e.matmul(pq[:], lhsT=ones_div, rhs=zsq[:, kc, :], start=(kc == 0), stop=(kc == KC - 1)),
                  r=[("zsq",), ("constb",)], w=[("ps", 7)])
        S.add("act", lambda e: e.activation(out=mean_sb[:], in_=pm[:], func=AF.Copy), r=[("ps", 6)], w=[("mean",)])
        S.add("act", lambda e: e.activation(out=var_sb[:], in_=pm[:], func=AF.Square), r=[("ps", 6)], w=[("var",)])
        S.add("dve", lambda e: e.tensor_tensor(out=var_sb[:], in0=pq[:], in1=var_sb[:], op=ALU.subtract),
              r=[("ps", 7), ("var",)], w=[("var",)])
        S.add("act", lambda e: e.activation(out=var_sb[:], in_=var_sb[:], func=AF.Sqrt, bias=epsc[:, 0:1]),
              r=[("var",), ("epsc",)], w=[("var",)])
        S.add("dve", lambda e: e.reciprocal(out=rstd_sb[:], in_=var_sb[:]), r=[("var",)], w=[("rstd",)])
        mb = mean_sb[:].unsqueeze(1).to_broadcast([128, KC, 512])
        rb = rstd_sb[:].unsqueeze(1).to_broadcast([128, KC, 512])
        S.add("dve", lambda e: e.tensor_tensor(out=Rb, in0=Rb, in1=mb, op=ALU.subtract),
              r=Rkeys(tb) + [("mean",)], w=Rkeys(tb))
        S.add("dve", lambda e: e.tensor_tensor(out=Rb, in0=Rb, in1=rb, op=ALU.mult),
              r=Rkeys(tb) + [("rstd",)], w=Rkeys(tb))
        for kc in range(KC):
            S.add("act", lambda e, kc=kc: e.activation(out=R[:, kc, sl], in_=R[:, kc, sl], func=AF.Identity,
                                                      scale=lnp[:, 2 * gi, kc:kc + 1], bias=lnp[:, 2 * gi + 1, kc:kc + 1]),
                  r=[("R", tb, kc), ("lnp",)], w=[("R", tb, kc)])

    def ffn_phase(s, f):
        gi = 0 if f == 0 else 2
        if f == 0:
            for tb in range(4):
                S.add("sp", lambda e, tb=tb: e.dma_start(out=R[:, :, tbs(tb)], in_=xT[:, :, s * T + tb * 512: s * T + (tb + 1) * 512]),
                      w=Rkeys(tb), dsem=("xin", tb))
        for tb in range(4):
            S.add("dve", lambda e, tb=tb: e.tensor_copy(out=Xb[:, :, tbs(tb)], in_=R[:, :, tbs(tb)]), r=Rkeys(tb), w=[("xb", tb)])
            S.add("act", lambda e, tb=tb: e.activation(out=R[:, :, tbs(tb)], in_=R[:, :, tbs(tb)], func=AF.Copy, scale=ALPHA),
                  r=Rkeys(tb), w=Rkeys(tb))
        it = 0
        for (j0, j1) in GROUPS:
            G = j1 - j0
            for jl in range(G):
                S.add("pool", lambda e, jl=jl, j=j0 + jl: e.dma_start(out=Wd[:, jl, :], in_=wd_d[f][j]),
                      w=[("wd", jl)], dsem=("wd", jl))
            for jl in range(G):
                j = j0 + jl
                b = cnt["wgu"] % 3
                cnt["wgu"] += 1
                S.add("pool", lambda e, b=b, j=j: e.dma_start(out=Wgu[b][:], in_=wgu_d[f][j]), w=[("wgu", b)], dsem=("wgu", b))
                for tb in range(4):
                    pg, pu = PS[it % 2], PS[2 + it % 2]
                    kg, ku, ks = ("ps", it % 2), ("ps", 2 + it % 2), ("sg", it % 2)
                    sgt = sg[it % 2]
                    it += 1
                    for kc in range(KC):
                        S.add("pe", lambda e, pg=pg, b=b, kc=kc, tb=tb: e.matmul(pg[:], lhsT=Wgu[b][:, 0, kc, :], rhs=Xb[:, kc, tbs(tb)],
                                                                                start=(kc == 0), stop=(kc == KC - 1)),
                              r=[("wgu", b), ("xb", tb)], w=[kg])
                    for kc in range(KC):
                        S.add("pe", lambda e, pu=pu, b=b, kc=kc, tb=tb: e.matmul(pu[:], lhsT=Wgu[b][:, 1, kc, :], rhs=Xb[:, kc, tbs(tb)],
                                                                                start=(kc == 0), stop=(kc == KC - 1)),
                              r=[("wgu", b), ("xb", tb)], w=[ku])
                    S.add("act", lambda e, pg=pg, sgt=sgt: e.activation(out=sgt[:], in_=pg[:], func=AF.Silu), r=[kg], w=[ks])
                    S.add("dve", lambda e, pu=pu, sgt=sgt, jl=jl, tb=tb: e.tensor_tensor(out=actT[:, jl, tbs(tb)], in0=pu[:], in1=sgt[:], op=ALU.mult),
                          r=[ku, ks], w=[("actT", jl, tb)])
            i2 = 0
            for m in range(KC):
                for tb in range(4):
                    pd = PS[4 + i2 % 2]
                    kd = ("ps", 4 + i2 % 2)
                    i2 += 1
                    for jl in range(G):
                        S.add("pe", lambda e, pd=pd, jl=jl, m=m, tb=tb, G=G: e.matmul(pd[:], lhsT=Wd[:, jl, m * 128:(m + 1) * 128], rhs=actT[:, jl, tbs(tb)],
                                                                                   start=(jl == 0), stop=(jl == G - 1)),
                              r=[("wd", jl), ("actT", jl, tb)], w=[kd])
                    S.add("dve", lambda e, pd=pd, m=m, tb=tb: e.scalar_tensor_tensor(out=R[:, m, tbs(tb)], in0=pd[:], scalar=0.5, in1=R[:, m, tbs(tb)],
                                                                                   op0=ALU.mult, op1=ALU.add),
                          r=[kd, ("R", tb, m)], w=[("R", tb, m)])
        for tb in range(4):
            layer_norm(tb, gi)

    def out_phase(s):
        for tb in range(4):
            S.add("sp", lambda e, tb=tb: e.dma_start(out=outT[:, :, s * T + tb * 512: s * T + (tb + 1) * 512], in_=R[:, :, tbs(tb)]),
                  r=Rkeys(tb), dsem=("xout", tb))


    G_H = [1.0 - 2.0 ** (-5.0 - h) for h in range(4)]
    GC = [g ** 128 for g in G_H]

    def tls(t):
        return slice(t * 128, (t + 1) * 128)

    def cast_dma_cols(dst, src_d, c0, c1, keybase, step=1536):
        for pi, a in enumerate(range(c0, c1, step)):
            b_ = min(a + step, c1)
            k = (keybase, pi)
            S.add("pool", lambda e, a=a, b_=b_: e.dma_start(out=dst[:, :, a - c0:b_ - c0], in_=src_d[:, :, a:b_]), w=[k], dsem=k)

    def xbt_cast(dst, t, key):
        S.add("act", lambda e: e.activation(out=dst[:], in_=R[:, :, tls(t)], func=AF.Copy),
              r=[("R", t // 4, kc) for kc in range(KC)], w=[key])

    def pass_ret(s):
        Wr = carve(0, [128, KC, 3072], BF16)
        yretT = carve(48 * K, [128, KC, T], BF16)
        o = 80 * K
        rcs = carve(o, [128, 16, 2, 64], F32); o += 8 * K
        rdec = carve(o, [128, 1032], F32); o += 4128
        DT = rdec[:, 0:512]
        GQ = rdec[:, 512:1024]
        gk = rdec[:, 1024:1028]
        xbt = [carve(o + 2 * K * i, [128, KC, 128], BF16) for i in range(2)]; o += 4 * K
        qkr = carve(o, [128, 8, 2, 64], BF16); o += 2 * K
        tA = carve(o, [128, 8, 2, 64], F32); o += 4 * K
        tB = carve(o, [128, 8, 64], F32); o += 2 * K
        tC = carve(o, [128, 8, 64], F32); o += 2 * K
        kdec = carve(o, [128, 4, 128], BF16); o += 1 * K
        vbf = carve(o, [128, 1024], BF16); o += 2 * K
        srg = carve(o, [128, 1024], F32); o += 4 * K
        qT = carve(o, [128, 4, 128], BF16); o += 1 * K
        qdT = carve(o, [128, 4, 128], BF16); o += 1 * K
        kT = carve(o, [128, 4, 128], BF16); o += 1 * K
        sT = carve(o, [128, 4, 128], BF16); o += 1 * K
        state = carve(o, [128, 4, 256], F32); o += 4 * K
        stbf = carve(o, [128, 4, 256], BF16); o += 2 * K
        stats = carve(o, [128, 4, 6], F32); o += 128
        mv = carve(o, [128, 4, 2], F32); o += 64
        rs = carve(o, [128, 4], F32); o += 64
        yn = carve(o, [128, 1024], F32); o += 4 * K
        yrt = carve(o, [128, 1024], BF16); o += 2 * K
        assert o <= 142 * K, o

        cast_dma_cols(Wr, win_d, 0, 3072, "wr")
        S.add("sp", lambda e: e.dma_start(out=rcs[:], in_=rcs_d), w=[("rcs",)], dsem="c_rcs")
        S.add("sp", lambda e: e.dma_start(out=rdec[:], in_=rdec_d), w=[("rdec",)], dsem="c_rdec")
        S.add("dve", lambda e: e.memset(state[:].rearrange("p h e -> p (h e)"), 0.0), w=[("state",)])
        S.add("dve", lambda e: e.memset(stbf[:].rearrange("p h e -> p (h e)"), 0.0), w=[("stbf",)])

        for t in range(16):
            xb = xbt[t % 2]
            kx = ("xbt", t % 2)
            xbt_cast(xb, t, kx)
            for cb in range(6):
                for kc in range(KC):
                    S.add("pe", lambda e, cb=cb, kc=kc, xb=xb: e.matmul(PS[cb][:], lhsT=xb[:, kc, :], rhs=Wr[:, kc, cb * 512:(cb + 1) * 512],
                                                                     start=(kc == 0), stop=(kc == KC - 1)),
                          r=[kx, ("wr", cb // 3)], w=[("ps", cb)])
            if dbg is not None and dbg < 2:
                continue
            cos16 = rcs[:, t, 0, :].unsqueeze(1).to_broadcast([128, 16, 64])
            sin8 = rcs[:, t, 1, :].unsqueeze(1).to_broadcast([128, 8, 64])
            P16 = PSALL[:, 0:1024].rearrange("p (g i) -> p g i", g=16, i=64)
            P8 = PSALL[:, 0:1024].rearrange("p (g f i) -> p g f i", g=8, f=2, i=64)
            tA16 = tA[:].rearrange("p g f i -> p (g f) i")
            pk = [("ps", 0), ("ps", 1)]
            S.add("dve", lambda e, cos16=cos16: e.tensor_tensor(out=tA16, in0=P16, in1=cos16, op=ALU.mult), r=pk + [("rcs",)], w=[("tA",)])
            S.add("dve", lambda e, sin8=sin8: e.tensor_tensor(out=tB[:], in0=P8[:, :, 1, :], in1=sin8, op=ALU.mult), r=pk + [("rcs",)], w=[("tB",)])
            S.add("dve", lambda e, sin8=sin8: e.tensor_tensor(out=tC[:], in0=P8[:, :, 0, :], in1=sin8, op=ALU.mult), r=pk + [("rcs",)], w=[("tC",)])
            S.add("pool", lambda e: e.tensor_tensor(out=qkr[:, :, 0, :], in0=tA[:, :, 0, :], in1=tB[:], op=ALU.subtract),
                  r=[("tA",), ("tB",)], w=[("qkr0",)])
            S.add("pool", lambda e: e.tensor_tensor(out=qkr[:, :, 1, :], in0=tA[:, :, 1, :], in1=tC[:], op=ALU.add),
                  r=[("tA",), ("tC",)], w=[("qkr1",)])
            qk_keys = [("qkr0",), ("qkr1",)]
            qflat = qkr[:, 0:4].rearrange("p h f i -> p h (f i)")
            kflat = qkr[:, 4:8].rearrange("p h f i -> p h (f i)")
            if dbg is not None and dbg < 3:
                continue
            S.add("pool", lambda e, kflat=kflat: e.tensor_tensor(out=kdec[:], in0=kflat, in1=gk.unsqueeze(2).to_broadcast([128, 4, 128]), op=ALU.mult),
                  r=qk_keys + [("rdec",)], w=[("kdec",)])
            S.add("act", lambda e: e.activation(out=vbf[:], in_=PSALL[:, 1024:2048], func=AF.Copy), r=[("ps", 2), ("ps", 3)], w=[("vbf",)])
            S.add("act", lambda e: e.activation(out=srg[:], in_=PSALL[:, 2048:3072], func=AF.Silu), r=[("ps", 4), ("ps", 5)], w=[("srg",)])
            if dbg is not None and dbg < 4:
                continue
            for h in range(4):
                S.add("pe", lambda e, h=h, qflat=qflat: e.matmul(PS[6][:, h * 128:(h + 1) * 128], lhsT=qflat[:, h, :], rhs=ident, start=True, stop=True),
                      r=qk_keys + [("constb",)], w=[("ps", 6)])
            for h in range(4):
                S.add("pe", lambda e, h=h, kflat=kflat: e.matmul(PS[7][:, h * 128:(h + 1) * 128], lhsT=kflat[:, h, :], rhs=ident, start=True, stop=True),
                      r=qk_keys + [("constb",)], w=[("ps", 7)])
            S.add("act", lambda e: e.activation(out=qT[:].rearrange("p h c -> p (h c)"), in_=PS[6][:], func=AF.Copy), r=[("ps", 6)], w=[("qT",)])
            S.add("dve", lambda e: e.tensor_tensor(out=qdT[:].rearrange("p h c -> p (h c)"), in0=PS[6][:], in1=GQ, op=ALU.mult),
                  r=[("ps", 6), ("rdec",)], w=[("qdT",)])
            S.add("act", lambda e: e.activation(out=kT[:].rearrange("p h c -> p (h c)"), in_=PS[7][:], func=AF.Copy), r=[("ps", 7)], w=[("kT",)])
            if dbg is not None and dbg < 5:
                continue
            for h in range(4):
                S.add("pe", lambda e, h=h: e.matmul(PS[4][:, h * 128:(h + 1) * 128], lhsT=kT[:, h, :], rhs=qT[:, h, :], start=True, stop=True),
                      r=[("kT",), ("qT",)], w=[("ps", 4)])
            S.add("dve", lambda e: e.tensor_tensor(out=sT[:].rearrange("p h c -> p (h c)"), in0=PS[4][:], in1=DT, op=ALU.mult),
                  r=[("ps", 4), ("rdec",)], w=[("sT",)])
            if dbg is not None and dbg < 6:
                continue
            for h in range(4):
                po = PSALL[:, h * 256:(h + 1) * 256]
                S.add("pe", lambda e, h=h, po=po: e.matmul(po, lhsT=sT[:, h, :], rhs=vbf[:, h * 256:(h + 1) * 256], start=True, stop=False),
                      r=[("sT",), ("vbf",)], w=[("ps", h // 2)])
                S.add("pe", lambda e, h=h, po=po: e.matmul(po, lhsT=qdT[:, h, :], rhs=stbf[:, h, :], start=False, stop=True),
                      r=[("qdT",), ("stbf",)], w=[("ps", h // 2)])
            if dbg is not None and dbg < 7:
                continue
            for h in range(4):
                pkv = PSALL[:, 1024 + h * 256:1024 + (h + 1) * 256]
                S.add("pe", lambda e, h=h, pkv=pkv: e.matmul(pkv, lhsT=kdec[:, h, :], rhs=vbf[:, h * 256:(h + 1) * 256], start=True, stop=True),
                      r=[("kdec",), ("vbf",)], w=[("ps", 2 + h // 2)])
            for h in range(4):
                pkv = PSALL[:, 1024 + h * 256:1024 + (h + 1) * 256]
                S.add("dve", lambda e, h=h, pkv=pkv: e.scalar_tensor_tensor(out=state[:, h, :], in0=state[:, h, :], scalar=GC[h], in1=pkv,
                                                                         op0=ALU.mult, op1=ALU.add),
                      r=[("state",), ("ps", 2 + h // 2)], w=[("state",)])
            S.add("act", lambda e: e.activation(out=stbf[:], in_=state[:], func=AF.Copy), r=[("state",)], w=[("stbf",)])
            if dbg is not None and dbg < 8:
                continue
            pk01 = [("ps", 0), ("ps", 1)]
            for h in range(4):
                po = PSALL[:, h * 256:(h + 1) * 256]
                S.add("dve", lambda e, h=h, po=po: e.bn_stats(out=stats[:, h, :], in_=po), r=pk01, w=[("stats",)])
            for h in range(4):
                S.add("dve", lambda e, h=h: e.bn_aggr(out=mv[:, h, :], in_=stats[:, h, :]), r=[("stats",)], w=[("mv",)])
            S.add("act", lambda e: e.activation(out=rs[:], in_=mv[:, :, 1], func=AF.Sqrt, bias=epsc[:, 1:2]), r=[("mv",), ("epsc",)], w=[("rs",)])
            S.add("dve", lambda e: e.reciprocal(out=rs[:], in_=rs[:]), r=[("rs",)], w=[("rs",)])
            for h in range(4):
                po = PSALL[:, h * 256:(h + 1) * 256]
                S.add("dve", lambda e, h=h, po=po: e.tensor_scalar(out=yn[:, h * 256:(h + 1) * 256], in0=po, scalar1=mv[:, h, 0:1], scalar2=rs[:, h:h + 1],
                                                                op0=ALU.subtract, op1=ALU.mult),
                      r=pk01 + [("mv",), ("rs",)], w=[("yn",)])
            S.add("pool", lambda e: e.tensor_tensor(out=yrt[:], in0=yn[:], in1=srg[:], op=ALU.mult), r=[("yn",), ("srg",)], w=[("yrt",)])
            if dbg is not None and dbg < 9:
                continue
            for c in range(8):
                S.add("pe", lambda e, c=c: e.matmul(PSALL[:, 3072 + c * 128:3072 + (c + 1) * 128], lhsT=yrt[:, c * 128:(c + 1) * 128], rhs=ident, start=True, stop=True),
                      r=[("yrt",), ("constb",)], w=[("ps", 6 + c // 4)])
            S.add("act", lambda e, t=t: e.activation(out=yretT[:, :, tls(t)], in_=PSALL[:, 3072:4096].rearrange("p (c q) -> p c q", c=8), func=AF.Copy),
                  r=[("ps", 6), ("ps", 7)], w=[("yretT", t // 4)])

    def pass_moba(s):
        Wm = carve(0, [128, KC, 1536], BF16)
        kTa = carve(24 * K, [128, 4, T], BF16)
        vaug = carve(96 * K, [128, 16, 8, 66], BF16)
        ymobaT = carve(80 * K, [128, 4, T], BF16)
        o = 96 * K + 16896
        mcs = carve(o, [128, 16, 2, 8], F32); o += 1 * K
        mcb = carve(o, [128, 1152], BF16); o += 2304
        tri01 = mcb[:, 0:128]
        E64 = mcb[0:64, 128:1152].rearrange("p (n k) -> p n k", n=8)
        E64hi = mcb[64:128, 128:1152].rearrange("p (n k) -> p n k", n=8)
        xbt = [carve(o + 2 * K * i, [128, KC, 128], BF16) for i in range(2)]; o += 4 * K
        qktm = carve(o, [128, 16, 64], BF16); o += 2 * K
        r1 = carve(o, [128, 16, 8], F32); o += 512
        r2 = carve(o, [128, 16, 8], F32); o += 512
        qTb = carve(o, [128, 4, 256], BF16); o += 2 * K
        ksum = carve(o, [128, 4, 8], F32); o += 128
        ksb = carve(o, [128, 4, 64], BF16); o += 512
        gs = carve(o, [128, 8, 8], F32); o += 256
        cmp_ = carve(o, [128, 8, 8, 8], F32); o += 2 * K
        cntt = carve(o, [128, 8, 8], F32); o += 256
        negm = carve(o, [128, 8, 32], BF16); o += 512
        negT = carve(o, [128, 8, 256], BF16); o += 4 * K
        expP = [carve(o + 512 * i, [128, 256], BF16) for i in range(3)]; o += 1536
        rec = carve(o, [128, 2, 2, 4], F32); o += 64
        ytm = carve(o, [128, 2, 512], BF16); o += 2 * K
        qkf = carve(o, [128, 1024], F32); o += 4 * K
        assert o <= 142 * K, o

        cast_dma_cols(Wm, win_d, 3072, 4608, "wm")
        S.add("sp", lambda e: e.dma_start(out=mcs[:], in_=mcs_d), w=[("mcs",)], dsem="c_mcs")
        S.add("sp", lambda e: e.dma_start(out=mcb[:], in_=mcb_d), w=[("mcb",)], dsem="c_mcb")
        S.add("dve", lambda e: e.memset(vaug[:].rearrange("p t h d -> p (t h d)"), 1.0), w=[("vaug", t) for t in range(16)])
        S.add("dve", lambda e: e.memset(negT[:].rearrange("p h q -> p (h q)"), 0.0), w=[("negT",)])
        S.add("dve", lambda e: e.memset(negm[:].rearrange("p h n -> p (h n)"), 0.0), w=[("negm",)])
        S.add("dve", lambda e: e.memset(ksum[:].rearrange("p c n -> p (c n)"), 0.0), w=[("ksum",)])
        S.add("dve", lambda e: e.memset(ksb[:].rearrange("p c n -> p (c n)"), 0.0), w=[("ksb",)])

        PQK = PSALL[:, 0:1024].rearrange("p (g d) -> p g d", g=16, d=64)
        sti = 0
        for b in range(8):
            if dbg == 100:
                break
            for tt in range(2):
                t = 2 * b + tt
                xb = xbt[t % 2]
                kx = ("xbt", t % 2)
                xbt_cast(xb, t, kx)
                for cb in range(3):
                    for kc in range(KC):
                        S.add("pe", lambda e, cb=cb, kc=kc, xb=xb: e.matmul(PS[cb][:], lhsT=xb[:, kc, :], rhs=Wm[:, kc, cb * 512:(cb + 1) * 512],
                                                                         start=(kc == 0), stop=(kc == KC - 1)),
                              r=[kx, ("wm", 0)], w=[("ps", cb)])
                if dbg == 101:
                    continue
                pk = [("ps", 0), ("ps", 1)]
                cosb = mcs[:, t, 0, :].unsqueeze(1).to_broadcast([128, 16, 8])
                sinb = mcs[:, t, 1, :].unsqueeze(1).to_broadcast([128, 16, 8])
                qkf16 = qkf[:].rearrange("p (g d) -> p g d", g=16, d=64)
                x1 = qkf16[:, :, 0:8]
                x2 = qkf16[:, :, 8:16]
                S.add("act", lambda e: e.activation(out=qkf[:], in_=PSALL[:, 0:1024], func=AF.Copy), r=pk, w=[("qkf",)])
                S.add("pool", lambda e, qkf16=qkf16: e.tensor_copy(out=qktm[:], in_=qkf16), r=[("qkf",)], w=[("qktm_c",)])
                S.add("pool", lambda e, x1=x1, cosb=cosb: e.tensor_tensor(out=r1[:], in0=x1, in1=cosb, op=ALU.mult), r=[("qkf",), ("mcs",)], w=[("r1",)])
                S.add("pool", lambda e, x2=x2, sinb=sinb: e.tensor_tensor(out=r2[:], in0=x2, in1=sinb, op=ALU.mult), r=[("qkf",), ("mcs",)], w=[("r2",)])
                S.add("pool", lambda e: e.tensor_tensor(out=qktm[:, :, 0:8], in0=r1[:], in1=r2[:], op=ALU.subtract), r=[("r1",), ("r2",), ("qktm_c",)], w=[("qktm_a",)])
                S.add("pool", lambda e, x1=x1, sinb=sinb: e.tensor_tensor(out=r1[:], in0=x1, in1=sinb, op=ALU.mult), r=[("qkf",), ("mcs",)], w=[("r1",)])
                S.add("pool", lambda e, x2=x2, cosb=cosb: e.tensor_tensor(out=r2[:], in0=x2, in1=cosb, op=ALU.mult), r=[("qkf",), ("mcs",)], w=[("r2",)])
                S.add("pool", lambda e: e.tensor_tensor(out=qktm[:, :, 8:16], in0=r1[:], in1=r2[:], op=ALU.add), r=[("r1",), ("r2",), ("qktm_c",)], w=[("qktm_b",)])
                qk_keys = [("qktm_a",), ("qktm_b",), ("qktm_c",)]
                if dbg == 102:
                    continue
                S.add("act", lambda e, t=t: e.activation(out=vaug[:, t, :, 0:64], in_=PS[2][:].rearrange("p (h d) -> p h d", h=8), func=AF.Copy),
                      r=[("ps", 2)], w=[("vaug", t)])
                if dbg == 103:
                    continue
                qf = qktm[:, 0:8, :].rearrange("p h d -> p (h d)")
                kf = qktm[:, 8:16, :].rearrange("p h d -> p (h d)")
                for c in range(4):
                    S.add("pe", lambda e, c=c, qf=qf: e.matmul(PS[3][:, c * 128:(c + 1) * 128], lhsT=qf[:, c * 128:(c + 1) * 128], rhs=ident, start=True, stop=True),
                          r=qk_keys + [("constb",)], w=[("ps", 3)])
                for c in range(4):
                    S.add("pe", lambda e, c=c, kf=kf: e.matmul(PS[4][:, c * 128:(c + 1) * 128], lhsT=kf[:, c * 128:(c + 1) * 128], rhs=ident, start=True, stop=True),
                          r=qk_keys + [("constb",)], w=[("ps", 4)])
                S.add("act", lambda e, tt=tt: e.activation(out=qTb[:, :, tt * 128:(tt + 1) * 128], in_=PS[3][:].rearrange("p (c q) -> p c q", c=4), func=AF.Copy),
                      r=[("ps", 3)], w=[("qTb", tt)])
                S.add("act", lambda e, t=t: e.activation(out=kTa[:, :, tls(t)], in_=PS[4][:].rearrange("p (c q) -> p c q", c=4), func=AF.Copy),
                      r=[("ps", 4)], w=[("kTa", t)])
            if dbg is not None and (dbg < 11 or (100 <= dbg < 120)):
                continue
            glvl = 4 if (dbg is None or dbg < 120) else dbg - 120
            if b >= 4 and not (dbg is not None and dbg < 12):
                for tt in range(2):
                    for c in range(4):
                        S.add("pe", lambda e, c=c, tt=tt: e.matmul(PS[2][:, c * 64:(c + 1) * 64], lhsT=qTb[:, c, tt * 128:(tt + 1) * 128],
                                                                rhs=ksb[:, c, :], start=True, stop=True),
                              r=[("qTb", tt), ("ksb",)], w=[("ps", 2)])
                    S.add("act", lambda e: e.activation(out=gs[:].rearrange("p (c j) n -> p c (j n)", c=4), in_=PS[2][:, 0:256].rearrange("p (c x) -> p c x", c=4)[:, :, 0:16], func=AF.Copy),
                          r=[("ps", 2)], w=[("gs",)])
                    if glvl < 2:
                        continue
                    gm = gs[:, :, 0:b].unsqueeze(2).to_broadcast([128, 8, b, b])
                    gn = gs[:, :, 0:b].unsqueeze(3).to_broadcast([128, 8, b, b])
                    S.add("dve", lambda e, gm=gm, gn=gn, b=b: e.tensor_tensor(out=cmp_[:, :, 0:b, 0:b], in0=gm, in1=gn, op=ALU.is_gt), r=[("gs",)], w=[("cmp",)])
                    S.add("dve", lambda e, b=b: e.tensor_reduce(out=cntt[:, :, 0:b], in_=cmp_[:, :, 0:b, 0:b], axis=AX.X, op=ALU.add), r=[("cmp",)], w=[("cnt",)])
                    S.add("dve", lambda e, b=b: e.tensor_scalar(out=negm[:, :, 0:b], in0=cntt[:, :, 0:b], scalar1=2.5, scalar2=NEG, op0=ALU.is_gt, op1=ALU.mult),
                          r=[("cnt",)], w=[("negm",)])
                    if glvl < 3:
                        continue
                    for h in range(8):
                        S.add("pe", lambda e, h=h: e.matmul(PSALL[0:32, h * 128:(h + 1) * 128], lhsT=negm[:, h, :], rhs=ident, start=True, stop=True),
                              r=[("negm",), ("constb",)], w=[("ps", h // 4)])
                    S.add("act", lambda e, tt=tt: e.activation(out=negT[0:8, :, tt * 128:(tt + 1) * 128], in_=PSALL[0:8, 0:1024].rearrange("p (h q) -> p h q", h=8), func=AF.Copy),
                          r=[("ps", 0), ("ps", 1)], w=[("negT",)])
                    S.add("act", lambda e, tt=tt: e.activation(out=negT[64:72, :, tt * 128:(tt + 1) * 128], in_=PSALL[0:8, 0:1024].rearrange("p (h q) -> p h q", h=8), func=AF.Copy),
                          r=[("ps", 0), ("ps", 1)], w=[("negT",)])
            items = []
            for h in range(8):
                kts = []
                for n in range(b):
                    kts.append((2 * n, 0, 256, n, False))
                    kts.append((2 * n + 1, 0, 256, n, False))
                kts.append((2 * b, 0, 256, None, True))
                kts.append((2 * b + 1, 128, 256, None, True))
                for ki, kk in enumerate(kts):
                    items.append((h, ki, len(kts)) + kk)

            def emit_S(item, slot):
                h, ki, nk, kt, q0, q1, n, own = item
                hp = slice((h % 2) * 64, (h % 2) * 64 + 64)
                c = h // 2
                st = PS[1 + slot]
                kst = ("ps", 1 + slot)
                ex = expP[slot]
                kex = ("expP", slot)
                usemask = (n is not None) and (b >= 4) and not (dbg is not None and dbg < 12) and glvl >= 4
                S.add("pe", lambda e: e.matmul(st[:, q0:q1], lhsT=kTa[hp, c, tls(kt)], rhs=qTb[hp, c, q0:q1], start=True, stop=(not usemask)),
                      r=[("kTa", kt), ("qTb", 0), ("qTb", 1)], w=[kst])
                if usemask:
                    S.add("pe", lambda e: e.matmul(st[:, q0:q1], lhsT=(E64 if h % 2 == 0 else E64hi)[:, n, :], rhs=negT[hp, h, q0:q1], start=False, stop=True),
                          r=[("negT",), ("mcb",)], w=[kst])
                S.add("act", lambda e: e.activation(out=ex[:, q0:q1], in_=st[:, q0:q1], func=AF.Exp, scale=0.125), r=[kst], w=[kex])
                if own:
                    S.add("pool", lambda e: e.tensor_tensor(out=ex[:, q0:q0 + 128], in0=ex[:, q0:q0 + 128], in1=tri01, op=ALU.mult),
                          r=[kex, ("mcb",)], w=[kex])

            def emit_PV(item, slot):
                h, ki, nk, kt, q0, q1, n, own = item
                hg = h // 4
                pos = [PS[4 + hg], PS[6 + hg]]
                okeys = [("ps", 4 + hg), ("ps", 6 + hg)]
                ex = expP[slot]
                kex = ("expP", slot)
                for qt in range(2):
                    if q0 > qt * 128:
                        continue
                    last = (ki == nk - 1) if qt == 1 else (ki == nk - 2)
                    S.add("pe", lambda e, qt=qt, last=last: e.matmul(
                        pos[qt][:, (h % 4) * 66:(h % 4) * 66 + 66], lhsT=ex[:, qt * 128:(qt + 1) * 128], rhs=vaug[:, kt, h, :], start=(ki == 0), stop=last),
                        r=[kex, ("vaug", kt)], w=[okeys[qt]])
                if h % 4 == 3 and ki == nk - 1:
                    for qt in range(2):
                        po = pos[qt][:, 0:264].rearrange("p (h d) -> p h d", h=4)
                        S.add("dve", lambda e, po=po, qt=qt: e.reciprocal(out=rec[:, qt, hg, :], in_=po[:, :, 64]), r=[okeys[qt]], w=[("rec", qt, hg)])
                        S.add("dve", lambda e, po=po, qt=qt: e.tensor_tensor(
                            out=ytm[:, qt, hg * 256:(hg + 1) * 256].rearrange("p (h d) -> p h d", h=4), in0=po[:, :, 0:64],
                            in1=rec[:, qt, hg, :].unsqueeze(2).to_broadcast([128, 4, 64]), op=ALU.mult),
                            r=[okeys[qt], ("rec", qt, hg)], w=[("ytm", qt, hg)])

            SKEW = 2
            for i in range(len(items) + SKEW):
                if i < len(items):
                    emit_S(items[i], (sti + i) % 3)
                if i >= SKEW:
                    emit_PV(items[i - SKEW], (sti + i - SKEW) % 3)
            sti += len(items)
            S.add("dve", lambda e, b=b: e.tensor_reduce(out=ksum[:, :, b], in_=kTa[:, :, b * 256:(b + 1) * 256], axis=AX.X, op=ALU.add),
                  r=[("kTa", 2 * b), ("kTa", 2 * b + 1)], w=[("ksum",)])
            S.add("act", lambda e: e.activation(out=ksb[0:64, :, 0:8], in_=ksum[0:64, :, :], func=AF.Copy), r=[("ksum",)], w=[("ksb",)])
            S.add("act", lambda e: e.activation(out=ksb[64:128, :, 8:16], in_=ksum[64:128, :, :], func=AF.Copy), r=[("ksum",)], w=[("ksb",)])
            for qt in range(2):
                for c in range(4):
                    S.add("pe", lambda e, c=c, qt=qt: e.matmul(PS[3][:, c * 128:(c + 1) * 128], lhsT=ytm[:, qt, c * 128:(c + 1) * 128], rhs=ident, start=True, stop=True),
                          r=[("ytm", qt, 0), ("ytm", qt, 1), ("constb",)], w=[("ps", 3)])
                S.add("act", lambda e, t=2 * b + qt: e.activation(out=ymobaT[:, :, tls(t)], in_=PS[3][:].rearrange("p (c q) -> p c q", c=4), func=AF.Copy),
                      r=[("ps", 3)], w=[("ymobaT", t // 4)])

    def pass_o1(s):
        RP = carve(0, [128, KC, D], BF16)
        WGA = carve(16 * K, [128, KC, D], BF16)
        WGB = carve(32 * K, [128, KC, D], BF16)
        yretT = carve(48 * K, [128, KC, T], BF16)
        ymobaT = carve(80 * K, [128, 4, T], BF16)
        MP = carve(96 * K, [128, 4, D], BF16)
        xblk = carve(104 * K, [128, KC, 512], BF16)
        ufin = carve(112 * K, [128, KC, 512], BF16)
        sga = carve(120 * K, [128, 512], F32)
        sgb = carve(122 * K, [128, 512], F32)
        u1 = carve(124 * K, [128, 512], F32)
        u2 = carve(126 * K, [128, 512], F32)
        cast_dma_cols(RP, rp_d, 0, D, "rp", step=D)
        cast_dma_cols(WGA, win_d, 4608, 5632, "wga", step=D)
        cast_dma_cols(WGB, win_d, 5632, 6656, "wgb", step=D)
        cast_dma_cols(MP, mp_d, 0, D, "mp", step=D)
        for tb in range(4):
            sl = tbs(tb)
            S.add("act", lambda e, sl=sl: e.activation(out=xblk[:], in_=R[:, :, sl], func=AF.Copy), r=Rkeys(tb), w=[("xblk",)])
            for m in range(KC):
                o4 = 4 * (m % 2)
                ms = slice(m * 128, (m + 1) * 128)
                for kc in range(KC):
                    S.add("pe", lambda e, kc=kc, ms=ms, o4=o4, sl=sl: e.matmul(PS[o4][:], lhsT=RP[:, kc, ms], rhs=yretT[:, kc, sl], start=(kc == 0), stop=(kc == KC - 1)),
                          r=[("rp", 0), ("yretT", tb)], w=[("ps", o4)])
                for kc in range(KC):
                    S.add("pe", lambda e, kc=kc, ms=ms, o4=o4: e.matmul(PS[o4 + 1][:], lhsT=WGA[:, kc, ms], rhs=xblk[:, kc, :], start=(kc == 0), stop=(kc == KC - 1)),
                          r=[("wga", 0), ("xblk",)], w=[("ps", o4 + 1)])
                for c in range(4):
                    S.add("pe", lambda e, c=c, ms=ms, o4=o4, sl=sl: e.matmul(PS[o4 + 2][:], lhsT=MP[:, c, ms], rhs=ymobaT[:, c, sl], start=(c == 0), stop=(c == 3)),
                          r=[("mp", 0), ("ymobaT", tb)], w=[("ps", o4 + 2)])
                for kc in range(KC):
                    S.add("pe", lambda e, kc=kc, ms=ms, o4=o4: e.matmul(PS[o4 + 3][:], lhsT=WGB[:, kc, ms], rhs=xblk[:, kc, :], start=(kc == 0), stop=(kc == KC - 1)),
                          r=[("wgb", 0), ("xblk",)], w=[("ps", o4 + 3)])
                S.add("act", lambda e, o4=o4: e.activation(out=sga[:], in_=PS[o4 + 1][:], func=AF.Sigmoid), r=[("ps", o4 + 1)], w=[("sga",)])
                S.add("act", lambda e, o4=o4: e.activation(out=sgb[:], in_=PS[o4 + 3][:], func=AF.Sigmoid), r=[("ps", o4 + 3)], w=[("sgb",)])
                S.add("dve", lambda e, o4=o4: e.tensor_tensor(out=u1[:], in0=PS[o4][:], in1=sga[:], op=ALU.mult), r=[("ps", o4), ("sga",)], w=[("u1",)])
                S.add("dve", lambda e, o4=o4: e.tensor_tensor(out=u2[:], in0=PS[o4 + 2][:], in1=sgb[:], op=ALU.mult), r=[("ps", o4 + 2), ("sgb",)], w=[("u2",)])
                S.add("pool", lambda e, m=m: e.tensor_tensor(out=ufin[:, m, :], in0=u1[:], in1=u2[:], op=ALU.add), r=[("u1",), ("u2",)], w=[("ufin",)])
            S.add("act", lambda e, sl=sl: e.activation(out=yretT[:, :, sl], in_=ufin[:], func=AF.Copy), r=[("ufin",)], w=[("yretT", tb)])

    def pass_o2(s):
        WO = carve(0, [128, KC, D], BF16)
        U = carve(48 * K, [128, KC, T], BF16)
        cast_dma_cols(WO, wo_d, 0, D, "wo", step=D)
        i2 = 0
        for tb in range(4):
            sl = tbs(tb)
            for m in range(KC):
                pb = i2 % 2
                i2 += 1
                ms = slice(m * 128, (m + 1) * 128)
                for kc in range(KC):
                    S.add("pe", lambda e, kc=kc, ms=ms, pb=pb, sl=sl: e.matmul(PS[pb][:], lhsT=WO[:, kc, ms], rhs=U[:, kc, sl], start=(kc == 0), stop=(kc == KC - 1)),
                          r=[("wo", 0), ("yretT", tb)], w=[("ps", pb)])
                S.add("dve", lambda e, m=m, pb=pb, sl=sl: e.scalar_tensor_tensor(out=R[:, m, sl], in0=R[:, m, sl], scalar=ALPHA, in1=PS[pb][:], op0=ALU.mult, op1=ALU.add),
                      r=[("ps", pb), ("R", tb, m)], w=[("R", tb, m)])
            layer_norm(tb, 1)

    def mixer_phase(s):
        if "r" in mixp:
            pass_ret(s)
            S.barrier()
        if "m" in mixp:
            pass_moba(s)
            S.barrier()
        if "o1" in mixp:
            pass_o1(s)
            S.barrier()
        if "o2" in mixp:
            pass_o2(s)

    for s in range(nseq):
        if "load" in phases:
            for tb in range(4):
                S.add("sp", lambda e, tb=tb: e.dma_start(out=R[:, :, tbs(tb)], in_=xT[:, :, s * T + tb * 512: s * T + (tb + 1) * 512]),
                      w=Rkeys(tb), dsem=("xin", tb))
            S.barrier()
        if "ffn1" in phases:
            ffn_phase(s, 0)
            S.barrier()
        if "mix" in phases:
            mixer_phase(s)
            S.barrier()
        if "ffn2" in phases:
            ffn_phase(s, 1)
            S.barrier()
        out_phase(s)
    S.add("sp", lambda e: e.nop(), r=[k for tb in range(4) for k in Rkeys(tb)], w=[k for tb in range(4) for k in Rkeys(tb)])

    engines = {
        "pe": ("tensor", nc.tensor),
        "act": ("scalar", nc.scalar),
        "dve": ("vector", nc.vector),
        "pool": ("gpsimd", nc.gpsimd),
        "sp": ("sync", nc.sync),
    }
    S.emit(nc, engines)
    return nc, S


def _feat_major(v):
    return np.ascontiguousarray(np.asarray(v, np.float32).reshape(KC, 128).T)


def _module_consts():
    c = {}
    pos = np.arange(T, dtype=np.float32)
    inv = (1.0 / (np.float32(10000.0) ** np.linspace(0.0, 1.0, 64, dtype=np.float32))).astype(np.float32)
    ang = (pos[:, None] * inv[None, :]).astype(np.float32)
    rcs = np.stack([np.cos(ang), np.sin(ang)], axis=1).astype(np.float32)
    c["rcs"] = np.ascontiguousarray(rcs.reshape(16, 128, 2, 64).transpose(1, 0, 2, 3))
    inv2 = (1.0 / (np.float32(500000.0) ** (np.arange(0, 16, 2, dtype=np.float32) / np.float32(16)))).astype(np.float32)
    ang2 = (pos[:, None] * inv2[None, :]).astype(np.float32)
    mcs = np.stack([np.cos(ang2), np.sin(ang2)], axis=1).astype(np.float32)
    c["mcs"] = np.ascontiguousarray(mcs.reshape(16, 128, 2, 8).transpose(1, 0, 2, 3))
    g = 1.0 - 2.0 ** (-5.0 - np.arange(4, dtype=np.float64))
    idx = np.arange(128, dtype=np.float64)
    sc = 128.0 ** -0.5
    rdec = np.zeros((128, 1032), np.float32)
    for h in range(4):
        diff = idx[None, :] - idx[:, None]
        dt = np.where(diff >= 0, g[h] ** np.maximum(diff, 0.0), 0.0) * sc
        rdec[:, h * 128:(h + 1) * 128] = dt
        rdec[:, 512 + h * 128:512 + (h + 1) * 128] = (g[h] ** (idx + 1.0))[None, :]
        rdec[:, 1024 + h] = g[h] ** (127.0 - idx) * sc
    c["rdec"] = rdec
    mcb = np.zeros((128, 1152), np.float32)
    mcb[:, 0:128] = (idx[:, None] <= idx[None, :]).astype(np.float32)
    for n in range(8):
        mcb[n, 128 + n * 128:128 + (n + 1) * 128] = 1.0
        mcb[64 + n, 128 + n * 128:128 + (n + 1) * 128] = 1.0
    c["mconstb"] = mcb.astype(ml_dtypes.bfloat16)
    return c


def prep_shared(inp):
    sh = {}
    for f, (gu, dn) in enumerate([("ffn1_w_gu", "ffn1_w_down"), ("ffn2_w_gu", "ffn2_w_down")]):
        w = np.asarray(inp[gu], np.float32)[0]
        w = w.reshape(KC, 128, 2, NJ, 128)
        sh["wgu%d" % (f + 1)] = np.ascontiguousarray(w.transpose(3, 1, 2, 0, 4))
        sh["wd%d" % (f + 1)] = np.ascontiguousarray(np.asarray(inp[dn], np.float32)[0].reshape(NJ, 128, D))
    lnp = np.stack([_feat_major(inp[k][0]) for k in ("ln1_g", "ln1_b", "lnm_g", "lnm_b", "ln2_g", "ln2_b")], axis=1)
    sh["lnp"] = np.ascontiguousarray(lnp)
    def kmajor(w, nk):
        w = np.asarray(w, np.float32)
        return np.ascontiguousarray(w.reshape(nk, 128, w.shape[-1]).transpose(1, 0, 2))
    sh["win"] = kmajor(inp["w_in"][0], KC)
    sh["retp"] = kmajor(inp["ret_proj"][0], KC)
    sh["mobp"] = kmajor(inp["moba_proj"][0], 4)
    sh["wout"] = kmajor(inp["w_out"][0], KC)
    sh.update(_module_consts())
    cb = np.zeros((128, 256), np.float32)
    cb[:, 0:128] = np.eye(128, dtype=np.float32)
    cb[:, 128:256] = 1.0 / 1024.0
    sh["constb"] = cb.astype(ml_dtypes.bfloat16)
    return sh


def shard_x(x, nseq=SEQ_PER_CORE, ncores=NCORES):
    x = np.asarray(x, np.float32)
    maps = []
    for c in range(ncores):
        xc = x[c * nseq:(c + 1) * nseq].reshape(nseq * T, KC, 128)
        maps.append(np.ascontiguousarray(xc.transpose(2, 1, 0)))
    return maps


def unshard(outs, nseq=SEQ_PER_CORE):
    res = []
    for o in outs:
        res.append(np.asarray(o, np.float32).transpose(2, 1, 0).reshape(nseq, T, D))
    return np.ascontiguousarray(np.concatenate(res, axis=0))


_CACHE = {}


def kernel(**inputs):
    if "nc" not in _CACHE:
        _CACHE["nc"] = build_program()[0]
    nc = _CACHE["nc"]
    sh = prep_shared(inputs)
    xs = shard_x(inputs["x"])
    in_maps = []
    for c in range(NCORES):
        m = dict(sh)
        m["xT"] = xs[c]
        in_maps.append(m)
    res = run_bass_kernel_spmd(nc, in_maps, core_ids=list(range(NCORES)))
    return unshard([r["outT"] for r in res.results])
```

```python
import math
import numpy as np
import ml_dtypes
import concourse.bass as bass
import concourse.mybir as mybir
from concourse.bass_utils import run_bass_kernel_spmd

F32 = mybir.dt.float32
BF16 = mybir.dt.bfloat16
AF = mybir.ActivationFunctionType
ALU = mybir.AluOpType
AX = mybir.AxisListType

D = 1024
T = 2048
KC = 8
DFF = 2816
NJ = 22
NCORES = 8
SEQ_PER_CORE = 2
ALPHA = 2.0 ** 0.25
LN_EPS = 1e-5
GN_EPS = 1e-5
WIN = 6656
GROUPS = [(0, 8), (8, 15), (15, 22)]
NEG = -30000.0


class Op:
    __slots__ = ("eng", "fn", "deps", "dsem", "sig", "cnt")

    def __init__(self, eng, fn, deps, dsem):
        self.eng = eng
        self.fn = fn
        self.deps = deps
        self.dsem = dsem
        self.sig = dsem is not None
        self.cnt = 0


class Sched:
    def __init__(self):
        self.ops = []
        self.lastw = {}
        self.readers = {}
        self.bar = set()
        self.last_stream = {}

    def add(self, eng, fn, r=(), w=(), dsem=None):
        i = len(self.ops)
        deps = set(self.bar)
        for k in r:
            j = self.lastw.get(k)
            if j is not None:
                deps.add(j)
        for k in w:
            j = self.lastw.get(k)
            if j is not None:
                deps.add(j)
            rd = self.readers.get(k)
            if rd:
                deps.update(rd.values())
        stream = ("dma", dsem) if dsem is not None else eng
        for k in r:
            self.readers.setdefault(k, {})[stream] = i
        for k in w:
            self.lastw[k] = i
            self.readers[k] = {}
        self.last_stream[stream] = i
        self.ops.append(Op(eng, fn, deps, dsem))
        return i

    def barrier(self):
        self.bar = set(self.last_stream.values())

    def emit(self, nc, engines):
        ops = self.ops

        def stream_of(o):
            return ("dma", o.dsem) if o.dsem is not None else o.eng

        for i, o in enumerate(ops):
            best = {}
            for j in o.deps:
                p = ops[j]
                st = stream_of(p)
                if p.dsem is None and p.eng == "pe" and o.eng == "pe" and o.dsem is None:
                    continue
                if st not in best or best[st] < j:
                    best[st] = j
            o.deps = best
            for j in best.values():
                ops[j].sig = True
        cnt = {}
        for o in ops:
            if o.sig:
                st = stream_of(o)
                inc = 16 if o.dsem is not None else 1
                cnt[st] = cnt.get(st, 0) + inc
                o.cnt = cnt[st]
        sems = {}
        import contextlib
        stack = contextlib.ExitStack()
        for k, st in enumerate(cnt.keys()):
            sems[st] = stack.enter_context(nc.semaphore("s%d" % k))
        self.max_counts = dict(cnt)
        with stack:
            with nc.Block() as block:
                for ename, (deco, _h) in engines.items():
                    my = [(i, o) for i, o in enumerate(ops) if o.eng == ename]
                    if not my:
                        continue

                    def body(eng, my=my, ename=ename):
                        waited = {}
                        for i, o in my:
                            for st, j in o.deps.items():
                                v = ops[j].cnt
                                if waited.get(st, 0) < v:
                                    eng.wait_ge(sems[st], v)
                                    waited[st] = v
                            ins = o.fn(eng)
                            if o.sig:
                                st = stream_of(o)
                                ins.then_inc(sems[st], 16 if o.dsem is not None else 1)
                                if o.dsem is None:
                                    pass
                    getattr(block, deco)(body)


def _tile(nc, name, shape, dt):
    return nc.sbuf_tensor(name, list(shape), dt).__enter__()


def _ptile(nc, name, shape, dt):
    return nc.psum_tensor(name, list(shape), dt).__enter__()


def build_program(nseq=SEQ_PER_CORE, phases=("ffn1", "mix", "ffn2"), dbg=None, mixp=("r", "m", "o1", "o2")):
    nc = bass.Bass("TRN2", target_bir_lowering=False)
    S = Sched()
    NT = nseq * T

    def din(name, shape, dt=F32):
        return nc.dram_tensor(name, list(shape), dt, kind="ExternalInput").ap()

    xT = din("xT", [128, KC, NT])
    outT = nc.dram_tensor("outT", [128, KC, NT], F32, kind="ExternalOutput").ap()
    wgu_d = [din("wgu1", [NJ, 128, 2, KC, 128]), din("wgu2", [NJ, 128, 2, KC, 128])]
    wd_d = [din("wd1", [NJ, 128, D]), din("wd2", [NJ, 128, D])]
    lnp_d = din("lnp", [128, 6, KC])
    cb_d = din("constb", [128, 256], BF16)
    win_d = din("win", [128, KC, WIN])
    rp_d = din("retp", [128, KC, D])
    mp_d = din("mobp", [128, 4, D])
    wo_d = din("wout", [128, KC, D])
    rcs_d = din("rcs", [128, 16, 2, 64])
    mcs_d = din("mcs", [128, 16, 2, 8])
    rdec_d = din("rdec", [128, 1032])
    mcb_d = din("mconstb", [128, 1152], BF16)

    R = _tile(nc, "R", [128, KC, T], F32)
    ARENA = _tile(nc, "ARENA", [128, 73216], BF16)
    lnp = _tile(nc, "lnp_sb", [128, 6, KC], F32)
    constb = _tile(nc, "constb_sb", [128, 256], BF16)
    ident = constb[:, 0:128]
    ones_div = constb[:, 128:256]
    epsc = _tile(nc, "epsc", [128, 2], F32)

    def carve(off_bytes, shape, dt):
        n = int(np.prod(shape[1:]))
        if dt == BF16:
            a = ARENA[:, off_bytes // 2: off_bytes // 2 + n]
        else:
            a = ARENA[:, off_bytes // 2: off_bytes // 2 + 2 * n].bitcast(F32)
        if len(shape) == 2:
            return a
        names = " ".join("d%d" % i for i in range(1, len(shape)))
        kw = {"d%d" % i: shape[i] for i in range(1, len(shape))}
        return a.rearrange("p (%s) -> p %s" % (names, names), **kw)

    K = 1024
    Xb = carve(0, [128, KC, T], BF16)
    actT = carve(32 * K, [128, 8, T], BF16)
    Wgu = [carve(64 * K + 4 * K * b, [128, 2, KC, 128], BF16) for b in range(3)]
    Wd = carve(76 * K, [128, 8, D], BF16)
    zb = carve(92 * K, [128, KC, 512], BF16)
    zsq = carve(100 * K, [128, KC, 512], BF16)
    mean_sb = carve(108 * K, [128, 512], F32)
    rstd_sb = carve(110 * K, [128, 512], F32)
    var_sb = carve(112 * K, [128, 512], F32)
    sg = [carve(114 * K + 1 * K * b, [128, 512], BF16) for b in range(2)]

    PSALL = _ptile(nc, "psall", [128, 4096], F32)
    PS = [PSALL[:, b * 512:(b + 1) * 512] for b in range(8)]

    def PSB(b):
        return PSALL[:, b * 512:(b + 1) * 512].bitcast(BF16)

    S.add("sp", lambda e: e.dma_start(out=lnp[:], in_=lnp_d), w=[("lnp",)], dsem="c_lnp")
    S.add("sp", lambda e: e.dma_start(out=constb[:], in_=cb_d), w=[("constb",)], dsem="c_cb")
    S.add("dve", lambda e: e.memset(epsc[:, 0:1], LN_EPS), w=[("epsc",)])
    S.add("dve", lambda e: e.memset(epsc[:, 1:2], GN_EPS), w=[("epsc",)])

    cnt = {"wgu": 0}

    def tbs(tb):
        return slice(tb * 512, (tb + 1) * 512)

    def Rkeys(tb):
        return [("R", tb, kc) for kc in range(KC)]

    def layer_norm(tb, gi):
        sl = tbs(tb)
        Rb = R[:, :, sl]
        S.add("pool", lambda e: e.tensor_copy(out=zb[:], in_=Rb), r=Rkeys(tb), w=[("zb",)])
        S.add("act", lambda e: e.activation(out=zsq[:], in_=Rb, func=AF.Square), r=Rkeys(tb), w=[("zsq",)])
        pm, pq = PS[6], PS[7]
        for kc in range(KC):
            S.add("pe", lambda e, kc=kc: e.matmul(pm[:], lhsT=ones_div, rhs=zb[:, kc, :], start=(kc == 0), stop=(kc == KC - 1)),
                  r=[("zb",), ("constb",)], w=[("ps", 6)])
        for kc in range(KC):
            S.add("pe", lambda e, kc=kc: e.matmul(pq[:], lhsT=ones_div, rhs=zsq[:, kc, :], start=(kc == 0), stop=(kc == KC - 1)),
                  r=[("zsq",), ("constb",)], w=[("ps", 7)])
        S.add("act", lambda e: e.activation(out=mean_sb[:], in_=pm[:], func=AF.Copy), r=[("ps", 6)], w=[("mean",)])
        S.add("act", lambda e: e.activation(out=var_sb[:], in_=pm[:], func=AF.Square), r=[("ps", 6)], w=[("var",)])
        S.add("dve", lambda e: e.tensor_tensor(out=var_sb[:], in0=pq[:], in1=var_sb[:], op=ALU.subtract),
              r=[("ps", 7), ("var",)], w=[("var",)])
        S.add("act", lambda e: e.activation(out=var_sb[:], in_=var_sb[:], func=AF.Sqrt, bias=epsc[:, 0:1]),
              r=[("var",), ("epsc",)], w=[("var",)])
        S.add("dve", lambda e: e.reciprocal(out=rstd_sb[:], in_=var_sb[:]), r=[("var",)], w=[("rstd",)])
        mb = mean_sb[:].unsqueeze(1).to_broadcast([128, KC, 512])
        rb = rstd_sb[:].unsqueeze(1).to_broadcast([128, KC, 512])
        S.add("dve", lambda e: e.tensor_tensor(out=Rb, in0=Rb, in1=mb, op=ALU.subtract),
              r=Rkeys(tb) + [("mean",)], w=Rkeys(tb))
        S.add("dve", lambda e: e.tensor_tensor(out=Rb, in0=Rb, in1=rb, op=ALU.mult),
              r=Rkeys(tb) + [("rstd",)], w=Rkeys(tb))
        for kc in range(KC):
            S.add("act", lambda e, kc=kc: e.activation(out=R[:, kc, sl], in_=R[:, kc, sl], func=AF.Identity,
                                                      scale=lnp[:, 2 * gi, kc:kc + 1], bias=lnp[:, 2 * gi + 1, kc:kc + 1]),
                  r=[("R", tb, kc), ("lnp",)], w=[("R", tb, kc)])

    def ffn_phase(s, f):
        gi = 0 if f == 0 else 2
        if f == 0:
            for tb in range(4):
                S.add("sp", lambda e, tb=tb: e.dma_start(out=R[:, :, tbs(tb)], in_=xT[:, :, s * T + tb * 512: s * T + (tb + 1) * 512]),
                      w=Rkeys(tb), dsem=("xin", tb))
        for tb in range(4):
            S.add("dve", lambda e, tb=tb: e.tensor_copy(out=Xb[:, :, tbs(tb)], in_=R[:, :, tbs(tb)]), r=Rkeys(tb), w=[("xb", tb)])
            S.add("act", lambda e, tb=tb: e.activation(out=R[:, :, tbs(tb)], in_=R[:, :, tbs(tb)], func=AF.Copy, scale=ALPHA),
                  r=Rkeys(tb), w=Rkeys(tb))
        it = 0
        for (j0, j1) in GROUPS:
            G = j1 - j0
            for jl in range(G):
                S.add("pool", lambda e, jl=jl, j=j0 + jl: e.dma_start(out=Wd[:, jl, :], in_=wd_d[f][j]),
                      w=[("wd", jl)], dsem=("wd", jl))
            for jl in range(G):
                j = j0 + jl
                b = cnt["wgu"] % 3
                cnt["wgu"] += 1
                S.add("pool", lambda e, b=b, j=j: e.dma_start(out=Wgu[b][:], in_=wgu_d[f][j]), w=[("wgu", b)], dsem=("wgu", b))
                for tb in range(4):
                    pg, pu = PS[it % 2], PS[2 + it % 2]
                    kg, ku, ks = ("ps", it % 2), ("ps", 2 + it % 2), ("sg", it % 2)
                    sgt = sg[it % 2]
                    it += 1
                    for kc in range(KC):
                        S.add("pe", lambda e, pg=pg, b=b, kc=kc, tb=tb: e.matmul(pg[:], lhsT=Wgu[b][:, 0, kc, :], rhs=Xb[:, kc, tbs(tb)],
                                                                                start=(kc == 0), stop=(kc == KC - 1)),
                              r=[("wgu", b), ("xb", tb)], w=[kg])
                    for kc in range(KC):
                        S.add("pe", lambda e, pu=pu, b=b, kc=kc, tb=tb: e.matmul(pu[:], lhsT=Wgu[b][:, 1, kc, :], rhs=Xb[:, kc, tbs(tb)],
                                                                                start=(kc == 0), stop=(kc == KC - 1)),
                              r=[("wgu", b), ("xb", tb)], w=[ku])
                    S.add("act", lambda e, pg=pg, sgt=sgt: e.activation(out=sgt[:], in_=pg[:], func=AF.Silu), r=[kg], w=[ks])
                    S.add("dve", lambda e, pu=pu, sgt=sgt, jl=jl, tb=tb: e.tensor_tensor(out=actT[:, jl, tbs(tb)], in0=pu[:], in1=sgt[:], op=ALU.mult),
                          r=[ku, ks], w=[("actT", jl, tb)])
            i2 = 0
            for m in range(KC):
                for tb in range(4):
                    pd = PS[4 + i2 % 2]
                    kd = ("ps", 4 + i2 % 2)
                    i2 += 1
                    for jl in range(G):
                        S.add("pe", lambda e, pd=pd, jl=jl, m=m, tb=tb, G=G: e.matmul(pd[:], lhsT=Wd[:, jl, m * 128:(m + 1) * 128], rhs=actT[:, jl, tbs(tb)],
                                                                                   start=(jl == 0), stop=(jl == G - 1)),
                              r=[("wd", jl), ("actT", jl, tb)], w=[kd])
                    S.add("dve", lambda e, pd=pd, m=m, tb=tb: e.scalar_tensor_tensor(out=R[:, m, tbs(tb)], in0=pd[:], scalar=0.5, in1=R[:, m, tbs(tb)],
                                                                                   op0=ALU.mult, op1=ALU.add),
                          r=[kd, ("R", tb, m)], w=[("R", tb, m)])
        for tb in range(4):
            layer_norm(tb, gi)

    def out_phase(s):
        for tb in range(4):
            S.add("sp", lambda e, tb=tb: e.dma_start(out=outT[:, :, s * T + tb * 512: s * T + (tb + 1) * 512], in_=R[:, :, tbs(tb)]),
                  r=Rkeys(tb), dsem=("xout", tb))


    G_H = [1.0 - 2.0 ** (-5.0 - h) for h in range(4)]
    GC = [g ** 128 for g in G_H]

    def tls(t):
        return slice(t * 128, (t + 1) * 128)

    def cast_dma_cols(dst, src_d, c0, c1, keybase, step=1536):
        for pi, a in enumerate(range(c0, c1, step)):
            b_ = min(a + step, c1)
            k = (keybase, pi)
            S.add("pool", lambda e, a=a, b_=b_: e.dma_start(out=dst[:, :, a - c0:b_ - c0], in_=src_d[:, :, a:b_]), w=[k], dsem=k)

    def xbt_cast(dst, t, key):
        S.add("act", lambda e: e.activation(out=dst[:], in_=R[:, :, tls(t)], func=AF.Copy),
              r=[("R", t // 4, kc) for kc in range(KC)], w=[key])

    def pass_ret(s):
        Wr = carve(0, [128, KC, 3072], BF16)
        yretT = carve(48 * K, [128, KC, T], BF16)
        o = 80 * K
        rcs = carve(o, [128, 16, 2, 64], F32); o += 8 * K
        rdec = carve(o, [128, 1032], F32); o += 4128
        DT = rdec[:, 0:512]
        GQ = rdec[:, 512:1024]
        gk = rdec[:, 1024:1028]
        xbt = [carve(o + 2 * K * i, [128, KC, 128], BF16) for i in range(2)]; o += 4 * K
        qkr = carve(o, [128, 8, 2, 64], BF16); o += 2 * K
        tA = carve(o, [128, 8, 2, 64], F32); o += 4 * K
        tB = carve(o, [128, 8, 64], F32); o += 2 * K
        tC = carve(o, [128, 8, 64], F32); o += 2 * K
        kdec = carve(o, [128, 4, 128], BF16); o += 1 * K
        vbf = carve(o, [128, 1024], BF16); o += 2 * K
        srg = carve(o, [128, 1024], F32); o += 4 * K
        qT = carve(o, [128, 4, 128], BF16); o += 1 * K
        qdT = carve(o, [128, 4, 128], BF16); o += 1 * K
        kT = carve(o, [128, 4, 128], BF16); o += 1 * K
        sT = carve(o, [128, 4, 128], BF16); o += 1 * K
        state = carve(o, [128, 4, 256], F32); o += 4 * K
        stbf = carve(o, [128, 4, 256], BF16); o += 2 * K
        stats = carve(o, [128, 4, 6], F32); o += 128
        mv = carve(o, [128, 4, 2], F32); o += 64
        rs = carve(o, [128, 4], F32); o += 64
        yn = carve(o, [128, 1024], F32); o += 4 * K
        yrt = carve(o, [128, 1024], BF16); o += 2 * K
        assert o <= 143 * K, o

        cast_dma_cols(Wr, win_d, 0, 3072, "wr")
        S.add("sp", lambda e: e.dma_start(out=rcs[:], in_=rcs_d), w=[("rcs",)], dsem="c_rcs")
        S.add("sp", lambda e: e.dma_start(out=rdec[:], in_=rdec_d), w=[("rdec",)], dsem="c_rdec")
        S.add("dve", lambda e: e.memset(state[:].rearrange("p h e -> p (h e)"), 0.0), w=[("state",)])
        S.add("dve", lambda e: e.memset(stbf[:].rearrange("p h e -> p (h e)"), 0.0), w=[("stbf",)])

        for t in range(16):
            xb = xbt[t % 2]
            kx = ("xbt", t % 2)
            xbt_cast(xb, t, kx)
            for cb in range(6):
                for kc in range(KC):
                    S.add("pe", lambda e, cb=cb, kc=kc, xb=xb: e.matmul(PS[cb][:], lhsT=xb[:, kc, :], rhs=Wr[:, kc, cb * 512:(cb + 1) * 512],
                                                                     start=(kc == 0), stop=(kc == KC - 1)),
                          r=[kx, ("wr", cb // 3)], w=[("ps", cb)])
            if dbg is not None and dbg < 2:
                continue
            cos16 = rcs[:, t, 0, :].unsqueeze(1).to_broadcast([128, 16, 64])
            sin8 = rcs[:, t, 1, :].unsqueeze(1).to_broadcast([128, 8, 64])
            P16 = PSALL[:, 0:1024].rearrange("p (g i) -> p g i", g=16, i=64)
            P8 = PSALL[:, 0:1024].rearrange("p (g f i) -> p g f i", g=8, f=2, i=64)
            tA16 = tA[:].rearrange("p g f i -> p (g f) i")
            pk = [("ps", 0), ("ps", 1)]
            S.add("dve", lambda e, cos16=cos16: e.tensor_tensor(out=tA16, in0=P16, in1=cos16, op=ALU.mult), r=pk + [("rcs",)], w=[("tA",)])
            S.add("dve", lambda e, sin8=sin8: e.tensor_tensor(out=tB[:], in0=P8[:, :, 1, :], in1=sin8, op=ALU.mult), r=pk + [("rcs",)], w=[("tB",)])
            S.add("dve", lambda e, sin8=sin8: e.tensor_tensor(out=tC[:], in0=P8[:, :, 0, :], in1=sin8, op=ALU.mult), r=pk + [("rcs",)], w=[("tC",)])
            S.add("pool", lambda e: e.tensor_tensor(out=qkr[:, :, 0, :], in0=tA[:, :, 0, :], in1=tB[:], op=ALU.subtract),
                  r=[("tA",), ("tB",)], w=[("qkr0",)])
            S.add("pool", lambda e: e.tensor_tensor(out=qkr[:, :, 1, :], in0=tA[:, :, 1, :], in1=tC[:], op=ALU.add),
                  r=[("tA",), ("tC",)], w=[("qkr1",)])
            qk_keys = [("qkr0",), ("qkr1",)]
            qflat = qkr[:, 0:4].rearrange("p h f i -> p h (f i)")
            kflat = qkr[:, 4:8].rearrange("p h f i -> p h (f i)")
            if dbg is not None and dbg < 3:
                continue
            S.add("pool", lambda e, kflat=kflat: e.tensor_tensor(out=kdec[:], in0=kflat, in1=gk.unsqueeze(2).to_broadcast([128, 4, 128]), op=ALU.mult),
                  r=qk_keys + [("rdec",)], w=[("kdec",)])
            S.add("act", lambda e: e.activation(out=vbf[:], in_=PSALL[:, 1024:2048], func=AF.Copy), r=[("ps", 2), ("ps", 3)], w=[("vbf",)])
            S.add("act", lambda e: e.activation(out=srg[:], in_=PSALL[:, 2048:3072], func=AF.Silu), r=[("ps", 4), ("ps", 5)], w=[("srg",)])
            if dbg is not None and dbg < 4:
                continue
            for h in range(4):
                S.add("pe", lambda e, h=h, qflat=qflat: e.matmul(PS[6][:, h * 128:(h + 1) * 128], lhsT=qflat[:, h, :], rhs=ident, start=True, stop=True),
                      r=qk_keys + [("constb",)], w=[("ps", 6)])
            for h in range(4):
                S.add("pe", lambda e, h=h, kflat=kflat: e.matmul(PS[7][:, h * 128:(h + 1) * 128], lhsT=kflat[:, h, :], rhs=ident, start=True, stop=True),
                      r=qk_keys + [("constb",)], w=[("ps", 7)])
            S.add("act", lambda e: e.activation(out=qT[:].rearrange("p h c -> p (h c)"), in_=PS[6][:], func=AF.Copy), r=[("ps", 6)], w=[("qT",)])
            S.add("dve", lambda e: e.tensor_tensor(out=qdT[:].rearrange("p h c -> p (h c)"), in0=PS[6][:], in1=GQ, op=ALU.mult),
                  r=[("ps", 6), ("rdec",)], w=[("qdT",)])
            S.add("act", lambda e: e.activation(out=kT[:].rearrange("p h c -> p (h c)"), in_=PS[7][:], func=AF.Copy), r=[("ps", 7)], w=[("kT",)])
            if dbg is not None and dbg < 5:
                continue
            for h in range(4):
                S.add("pe", lambda e, h=h: e.matmul(PS[4][:, h * 128:(h + 1) * 128], lhsT=kT[:, h, :], rhs=qT[:, h, :], start=True, stop=True),
                      r=[("kT",), ("qT",)], w=[("ps", 4)])
            S.add("dve", lambda e: e.tensor_tensor(out=sT[:].rearrange("p h c -> p (h c)"), in0=PS[4][:], in1=DT, op=ALU.mult),
                  r=[("ps", 4), ("rdec",)], w=[("sT",)])
            if dbg is not None and dbg < 6:
                continue
            for h in range(4):
                po = PSALL[:, h * 256:(h + 1) * 256]
                S.add("pe", lambda e, h=h, po=po: e.matmul(po, lhsT=sT[:, h, :], rhs=vbf[:, h * 256:(h + 1) * 256], start=True, stop=False),
                      r=[("sT",), ("vbf",)], w=[("ps", h // 2)])
                S.add("pe", lambda e, h=h, po=po: e.matmul(po, lhsT=qdT[:, h, :], rhs=stbf[:, h, :], start=False, stop=True),
                      r=[("qdT",), ("stbf",)], w=[("ps", h // 2)])
            if dbg is not None and dbg < 7:
                continue
            for h in range(4):
                pkv = PSALL[:, 1024 + h * 256:1024 + (h + 1) * 256]
                S.add("pe", lambda e, h=h, pkv=pkv: e.matmul(pkv, lhsT=kdec[:, h, :], rhs=vbf[:, h * 256:(h + 1) * 256], start=True, stop=True),
                      r=[("kdec",), ("vbf",)], w=[("ps", 2 + h // 2)])
            for h in range(4):
                pkv = PSALL[:, 1024 + h * 256:1024 + (h + 1) * 256]
                S.add("dve", lambda e, h=h, pkv=pkv: e.scalar_tensor_tensor(out=state[:, h, :], in0=state[:, h, :], scalar=GC[h], in1=pkv,
                                                                         op0=ALU.mult, op1=ALU.add),
                      r=[("state",), ("ps", 2 + h // 2)], w=[("state",)])
            S.add("act", lambda e: e.activation(out=stbf[:], in_=state[:], func=AF.Copy), r=[("state",)], w=[("stbf",)])
            if dbg is not None and dbg < 8:
                continue
            pk01 = [("ps", 0), ("ps", 1)]
            for h in range(4):
                po = PSALL[:, h * 256:(h + 1) * 256]
                S.add("dve", lambda e, h=h, po=po: e.bn_stats(out=stats[:, h, :], in_=po), r=pk01, w=[("stats",)])
            for h in range(4):
                S.add("dve", lambda e, h=h: e.bn_aggr(out=mv[:, h, :], in_=stats[:, h, :]), r=[("stats",)], w=[("mv",)])
            S.add("act", lambda e: e.activation(out=rs[:], in_=mv[:, :, 1], func=AF.Sqrt, bias=epsc[:, 1:2]), r=[("mv",), ("epsc",)], w=[("rs",)])
            S.add("dve", lambda e: e.reciprocal(out=rs[:], in_=rs[:]), r=[("rs",)], w=[("rs",)])
            for h in range(4):
                po = PSALL[:, h * 256:(h + 1) * 256]
                S.add("dve", lambda e, h=h, po=po: e.tensor_scalar(out=yn[:, h * 256:(h + 1) * 256], in0=po, scalar1=mv[:, h, 0:1], scalar2=rs[:, h:h + 1],
                                                                op0=ALU.subtract, op1=ALU.mult),
                      r=pk01 + [("mv",), ("rs",)], w=[("yn",)])
            S.add("pool", lambda e: e.tensor_tensor(out=yrt[:], in0=yn[:], in1=srg[:], op=ALU.mult), r=[("yn",), ("srg",)], w=[("yrt",)])
            if dbg is not None and dbg < 9:
                continue
            for c in range(8):
                S.add("pe", lambda e, c=c: e.matmul(PSALL[:, 3072 + c * 128:3072 + (c + 1) * 128], lhsT=yrt[:, c * 128:(c + 1) * 128], rhs=ident, start=True, stop=True),
                      r=[("yrt",), ("constb",)], w=[("ps", 6 + c // 4)])
            S.add("act", lambda e, t=t: e.activation(out=yretT[:, :, tls(t)], in_=PSALL[:, 3072:4096].rearrange("p (c q) -> p c q", c=8), func=AF.Copy),
                  r=[("ps", 6), ("ps", 7)], w=[("yretT", t // 4)])

    def pass_moba(s):
        Wm = carve(0, [128, KC, 1536], BF16)
        kTa = carve(24 * K, [128, 4, T], BF16)
        vaug = carve(96 * K, [128, 16, 8, 66], BF16)
        ymobaT = carve(80 * K, [128, 4, T], BF16)
        o = 96 * K + 16896
        mcs = carve(o, [128, 16, 2, 8], F32); o += 1 * K
        mcb = carve(o, [128, 1152], BF16); o += 2304
        tri01 = mcb[:, 0:128]
        E128 = mcb[:, 128:1152].rearrange("p (n k) -> p n k", n=8)
        xbt = [carve(o + 2 * K * i, [128, KC, 128], BF16) for i in range(2)]; o += 4 * K
        qktm = carve(o, [128, 16, 64], BF16); o += 2 * K
        r1 = carve(o, [128, 16, 8], F32); o += 512
        r2 = carve(o, [128, 16, 8], F32); o += 512
        qTb = carve(o, [128, 4, 256], BF16); o += 2 * K
        qz = carve(44 * K, [128, 4, 2, 256], BF16)
        ksum = carve(o, [128, 4, 8], F32); o += 128
        ksb = carve(o, [128, 4, 64], BF16); o += 512
        gs = carve(o, [128, 8, 8], F32); o += 256
        cmp_ = carve(o, [128, 8, 8, 8], F32); o += 2 * K
        cntt = carve(o, [128, 8, 8], F32); o += 256
        negm = carve(o, [128, 8, 32], BF16); o += 512
        negT = carve(o, [128, 8, 256], BF16); o += 4 * K
        expP = [carve(o + 1024 * i, [128, 512], BF16) for i in range(3)]; o += 3 * K
        rec = carve(o, [128, 2, 2, 4], F32); o += 64
        ytm = carve(o, [128, 2, 512], BF16); o += 2 * K
        qkf = carve(40 * K, [128, 1024], F32)
        assert o <= 143 * K, o

        cast_dma_cols(Wm, win_d, 3072, 4608, "wm")
        S.add("sp", lambda e: e.dma_start(out=mcs[:], in_=mcs_d), w=[("mcs",)], dsem="c_mcs")
        S.add("sp", lambda e: e.dma_start(out=mcb[:], in_=mcb_d), w=[("mcb",)], dsem="c_mcb")
        S.add("dve", lambda e: e.memset(vaug[:].rearrange("p t h d -> p (t h d)"), 1.0), w=[("vaug", t) for t in range(16)])
        S.add("dve", lambda e: e.memset(negT[:].rearrange("p h q -> p (h q)"), 0.0), w=[("negT",)])
        S.add("dve", lambda e: e.memset(qz[:].rearrange("p c j q -> p (c j q)"), 0.0), w=[("qz", 0), ("qz", 1)])
        S.add("dve", lambda e: e.memset(negm[:].rearrange("p h n -> p (h n)"), 0.0), w=[("negm",)])
        S.add("dve", lambda e: e.memset(ksum[:].rearrange("p c n -> p (c n)"), 0.0), w=[("ksum",)])
        S.add("dve", lambda e: e.memset(ksb[:].rearrange("p c n -> p (c n)"), 0.0), w=[("ksb",)])

        PQK = PSALL[:, 0:1024].rearrange("p (g d) -> p g d", g=16, d=64)
        sti = 0
        for b in range(8):
            if dbg == 100:
                break
            for tt in range(2):
                t = 2 * b + tt
                xb = xbt[t % 2]
                kx = ("xbt", t % 2)
                xbt_cast(xb, t, kx)
                for cb in range(3):
                    for kc in range(KC):
                        S.add("pe", lambda e, cb=cb, kc=kc, xb=xb: e.matmul(PS[cb][:], lhsT=xb[:, kc, :], rhs=Wm[:, kc, cb * 512:(cb + 1) * 512],
                                                                         start=(kc == 0), stop=(kc == KC - 1)),
                              r=[kx, ("wm", 0)], w=[("ps", cb)])
                if dbg == 101:
                    continue
                pk = [("ps", 0), ("ps", 1)]
                cosb = mcs[:, t, 0, :].unsqueeze(1).to_broadcast([128, 16, 8])
                sinb = mcs[:, t, 1, :].unsqueeze(1).to_broadcast([128, 16, 8])
                qkf16 = qkf[:].rearrange("p (g d) -> p g d", g=16, d=64)
                x1 = qkf16[:, :, 0:8]
                x2 = qkf16[:, :, 8:16]
                S.add("act", lambda e: e.activation(out=qkf[:], in_=PSALL[:, 0:1024], func=AF.Copy), r=pk, w=[("qkf",)])
                S.add("pool", lambda e, qkf16=qkf16: e.tensor_copy(out=qktm[:], in_=qkf16), r=[("qkf",)], w=[("qktm_c",)])
                S.add("pool", lambda e, x1=x1, cosb=cosb: e.tensor_tensor(out=r1[:], in0=x1, in1=cosb, op=ALU.mult), r=[("qkf",), ("mcs",)], w=[("r1",)])
                S.add("pool", lambda e, x2=x2, sinb=sinb: e.tensor_tensor(out=r2[:], in0=x2, in1=sinb, op=ALU.mult), r=[("qkf",), ("mcs",)], w=[("r2",)])
                S.add("pool", lambda e: e.tensor_tensor(out=qktm[:, :, 0:8], in0=r1[:], in1=r2[:], op=ALU.subtract), r=[("r1",), ("r2",), ("qktm_c",)], w=[("qktm_a",)])
                S.add("pool", lambda e, x1=x1, sinb=sinb: e.tensor_tensor(out=r1[:], in0=x1, in1=sinb, op=ALU.mult), r=[("qkf",), ("mcs",)], w=[("r1",)])
                S.add("pool", lambda e, x2=x2, cosb=cosb: e.tensor_tensor(out=r2[:], in0=x2, in1=cosb, op=ALU.mult), r=[("qkf",), ("mcs",)], w=[("r2",)])
                S.add("pool", lambda e: e.tensor_tensor(out=qktm[:, :, 8:16], in0=r1[:], in1=r2[:], op=ALU.add), r=[("r1",), ("r2",), ("qktm_c",)], w=[("qktm_b",)])
                qk_keys = [("qktm_a",), ("qktm_b",), ("qktm_c",)]
                if dbg == 102:
                    continue
                S.add("act", lambda e, t=t: e.activation(out=vaug[:, t, :, 0:64], in_=PS[2][:].rearrange("p (h d) -> p h d", h=8), func=AF.Copy),
                      r=[("ps", 2)], w=[("vaug", t)])
                if dbg == 103:
                    continue
                qf = qktm[:, 0:8, :].rearrange("p h d -> p (h d)")
                kf = qktm[:, 8:16, :].rearrange("p h d -> p (h d)")
                for c in range(4):
                    S.add("pe", lambda e, c=c, qf=qf: e.matmul(PS[3][:, c * 128:(c + 1) * 128], lhsT=qf[:, c * 128:(c + 1) * 128], rhs=ident, start=True, stop=True),
                          r=qk_keys + [("constb",)], w=[("ps", 3)])
                for c in range(4):
                    S.add("pe", lambda e, c=c, kf=kf: e.matmul(PS[4][:, c * 128:(c + 1) * 128], lhsT=kf[:, c * 128:(c + 1) * 128], rhs=ident, start=True, stop=True),
                          r=qk_keys + [("constb",)], w=[("ps", 4)])
                S.add("act", lambda e, tt=tt: e.activation(out=qTb[:, :, tt * 128:(tt + 1) * 128], in_=PS[3][:].rearrange("p (c q) -> p c q", c=4), func=AF.Copy),
                      r=[("ps", 3)], w=[("qTb", tt)])
                S.add("act", lambda e, tt=tt: e.activation(out=qz[0:64, :, 0, tt * 128:(tt + 1) * 128], in_=PS[3][0:64, :].rearrange("p (c q) -> p c q", c=4), func=AF.Copy),
                      r=[("ps", 3)], w=[("qz", tt)])
                S.add("act", lambda e, tt=tt: e.activation(out=qz[64:128, :, 1, tt * 128:(tt + 1) * 128], in_=PS[3][64:128, :].rearrange("p (c q) -> p c q", c=4), func=AF.Copy),
                      r=[("ps", 3)], w=[("qz", tt)])
                S.add("act", lambda e, t=t: e.activation(out=kTa[:, :, tls(t)], in_=PS[4][:].rearrange("p (c q) -> p c q", c=4), func=AF.Copy),
                      r=[("ps", 4)], w=[("kTa", t)])
            if dbg is not None and (dbg < 11 or (100 <= dbg < 120)):
                continue
            glvl = 4 if (dbg is None or dbg < 120) else dbg - 120
            if b >= 4 and not (dbg is not None and dbg < 12):
                for tt in range(2):
                    for c in range(4):
                        S.add("pe", lambda e, c=c, tt=tt: e.matmul(PS[2][:, c * 64:(c + 1) * 64], lhsT=qTb[:, c, tt * 128:(tt + 1) * 128],
                                                                rhs=ksb[:, c, :], start=True, stop=True),
                              r=[("qTb", tt), ("ksb",)], w=[("ps", 2)])
                    S.add("act", lambda e: e.activation(out=gs[:].rearrange("p (c j) n -> p c (j n)", c=4), in_=PS[2][:, 0:256].rearrange("p (c x) -> p c x", c=4)[:, :, 0:16], func=AF.Copy),
                          r=[("ps", 2)], w=[("gs",)])
                    if glvl < 2:
                        continue
                    gm = gs[:, :, 0:b].unsqueeze(2).to_broadcast([128, 8, b, b])
                    gn = gs[:, :, 0:b].unsqueeze(3).to_broadcast([128, 8, b, b])
                    S.add("dve", lambda e, gm=gm, gn=gn, b=b: e.tensor_tensor(out=cmp_[:, :, 0:b, 0:b], in0=gm, in1=gn, op=ALU.is_gt), r=[("gs",)], w=[("cmp",)])
                    S.add("dve", lambda e, b=b: e.tensor_reduce(out=cntt[:, :, 0:b], in_=cmp_[:, :, 0:b, 0:b], axis=AX.X, op=ALU.add), r=[("cmp",)], w=[("cnt",)])
                    S.add("dve", lambda e, b=b: e.tensor_scalar(out=negm[:, :, 0:b], in0=cntt[:, :, 0:b], scalar1=2.5, scalar2=NEG, op0=ALU.is_gt, op1=ALU.mult),
                          r=[("cnt",)], w=[("negm",)])
                    if glvl < 3:
                        continue
                    for h in range(8):
                        S.add("pe", lambda e, h=h: e.matmul(PSALL[0:32, h * 128:(h + 1) * 128], lhsT=negm[:, h, :], rhs=ident, start=True, stop=True),
                              r=[("negm",), ("constb",)], w=[("ps", h // 4)])
                    S.add("act", lambda e, tt=tt: e.activation(out=negT[0:8, :, tt * 128:(tt + 1) * 128], in_=PSALL[0:8, 0:1024].rearrange("p (h q) -> p h q", h=8), func=AF.Copy),
                          r=[("ps", 0), ("ps", 1)], w=[("negT",)])
            items = [(h, n) for h in range(8) for n in list(range(b)) + [None]]
            usemask_b = (b >= 4) and not (dbg is not None and dbg < 12) and glvl >= 4

            def emit_S(item, slot, b=b, usemask_b=usemask_b):
                h, n = item
                c, j = h // 2, h % 2
                st = PS[1 + slot]
                kst = ("ps", 1 + slot)
                ex = expP[slot]
                kex = ("expP", slot)
                qh = qz[:, c, j, :]
                rq = [("qz", 0), ("qz", 1)]
                if n is not None:
                    for jj in range(2):
                        kt = 2 * n + jj
                        so = st[:, jj * 256:(jj + 1) * 256]
                        S.add("pe", lambda e, kt=kt, so=so: e.matmul(so, lhsT=kTa[:, c, tls(kt)], rhs=qh, start=True, stop=(not usemask_b)),
                              r=[("kTa", kt)] + rq, w=[kst])
                        if usemask_b:
                            S.add("pe", lambda e, so=so: e.matmul(so, lhsT=E128[:, n, :], rhs=negT[:, h, :], start=False, stop=True),
                                  r=[("negT",), ("mcb",)], w=[kst])
                    S.add("act", lambda e: e.activation(out=ex[:], in_=st[:], func=AF.Exp, scale=0.125), r=[kst], w=[kex])
                else:
                    S.add("pe", lambda e: e.matmul(st[:, 0:256], lhsT=kTa[:, c, tls(2 * b)], rhs=qh, start=True, stop=True),
                          r=[("kTa", 2 * b)] + rq, w=[kst])
                    S.add("pe", lambda e: e.matmul(st[:, 384:512], lhsT=kTa[:, c, tls(2 * b + 1)], rhs=qh[:, 128:256], start=True, stop=True),
                          r=[("kTa", 2 * b + 1)] + rq, w=[kst])
                    S.add("act", lambda e: e.activation(out=ex[:, 0:256], in_=st[:, 0:256], func=AF.Exp, scale=0.125), r=[kst], w=[kex])
                    S.add("act", lambda e: e.activation(out=ex[:, 384:512], in_=st[:, 384:512], func=AF.Exp, scale=0.125), r=[kst], w=[kex])
                    S.add("pool", lambda e: e.tensor_tensor(out=ex[:, 0:128], in0=ex[:, 0:128], in1=tri01, op=ALU.mult), r=[kex, ("mcb",)], w=[kex])
                    S.add("pool", lambda e: e.tensor_tensor(out=ex[:, 384:512], in0=ex[:, 384:512], in1=tri01, op=ALU.mult), r=[kex, ("mcb",)], w=[kex])

            def emit_PV(item, slot, b=b):
                h, n = item
                hg = h // 4
                pos = [PS[4 + hg], PS[6 + hg]]
                okeys = [("ps", 4 + hg), ("ps", 6 + hg)]
                ex = expP[slot]
                kex = ("expP", slot)
                hs = slice((h % 4) * 66, (h % 4) * 66 + 66)
                if n is not None:
                    for jj in range(2):
                        kt = 2 * n + jj
                        for qt in range(2):
                            S.add("pe", lambda e, kt=kt, qt=qt, jj=jj: e.matmul(pos[qt][:, hs], lhsT=ex[:, jj * 256 + qt * 128:jj * 256 + (qt + 1) * 128],
                                                                              rhs=vaug[:, kt, h, :], start=(n == 0 and jj == 0), stop=False),
                                  r=[kex, ("vaug", kt)], w=[okeys[qt]])
                else:
                    S.add("pe", lambda e: e.matmul(pos[0][:, hs], lhsT=ex[:, 0:128], rhs=vaug[:, 2 * b, h, :], start=(b == 0), stop=True),
                          r=[kex, ("vaug", 2 * b)], w=[okeys[0]])
                    S.add("pe", lambda e: e.matmul(pos[1][:, hs], lhsT=ex[:, 128:256], rhs=vaug[:, 2 * b, h, :], start=(b == 0), stop=False),
                          r=[kex, ("vaug", 2 * b)], w=[okeys[1]])
                    S.add("pe", lambda e: e.matmul(pos[1][:, hs], lhsT=ex[:, 384:512], rhs=vaug[:, 2 * b + 1, h, :], start=False, stop=True),
                          r=[kex, ("vaug", 2 * b + 1)], w=[okeys[1]])
                    if h % 4 == 3:
                        for qt in range(2):
                            po = pos[qt][:, 0:264].rearrange("p (h d) -> p h d", h=4)
                            S.add("dve", lambda e, po=po, qt=qt: e.reciprocal(out=rec[:, qt, hg, :], in_=po[:, :, 64]), r=[okeys[qt]], w=[("rec", qt, hg)])
                            S.add("dve", lambda e, po=po, qt=qt: e.tensor_tensor(
                                out=ytm[:, qt, hg * 256:(hg + 1) * 256].rearrange("p (h d) -> p h d", h=4), in0=po[:, :, 0:64],
                                in1=rec[:, qt, hg, :].unsqueeze(2).to_broadcast([128, 4, 64]), op=ALU.mult),
                                r=[okeys[qt], ("rec", qt, hg)], w=[("ytm", qt, hg)])

            SKEW = 2
            for i in range(len(items) + SKEW):
                if i < len(items):
                    emit_S(items[i], (sti + i) % 3)
                if i >= SKEW:
                    emit_PV(items[i - SKEW], (sti + i - SKEW) % 3)
            sti += len(items)
            S.add("dve", lambda e, b=b: e.tensor_reduce(out=ksum[:, :, b], in_=kTa[:, :, b * 256:(b + 1) * 256], axis=AX.X, op=ALU.add),
                  r=[("kTa", 2 * b), ("kTa", 2 * b + 1)], w=[("ksum",)])
            S.add("act", lambda e: e.activation(out=ksb[0:64, :, 0:8], in_=ksum[0:64, :, :], func=AF.Copy), r=[("ksum",)], w=[("ksb",)])
            S.add("act", lambda e: e.activation(out=ksb[64:128, :, 8:16], in_=ksum[64:128, :, :], func=AF.Copy), r=[("ksum",)], w=[("ksb",)])
            for qt in range(2):
                for c in range(4):
                    S.add("pe", lambda e, c=c, qt=qt: e.matmul(PS[3][:, c * 128:(c + 1) * 128], lhsT=ytm[:, qt, c * 128:(c + 1) * 128], rhs=ident, start=True, stop=True),
                          r=[("ytm", qt, 0), ("ytm", qt, 1), ("constb",)], w=[("ps", 3)])
                S.add("act", lambda e, t=2 * b + qt: e.activation(out=ymobaT[:, :, tls(t)], in_=PS[3][:].rearrange("p (c q) -> p c q", c=4), func=AF.Copy),
                      r=[("ps", 3)], w=[("ymobaT", t // 4)])

    def pass_o1(s):
        RP = carve(0, [128, KC, D], BF16)
        WGA = carve(16 * K, [128, KC, D], BF16)
        WGB = carve(32 * K, [128, KC, D], BF16)
        yretT = carve(48 * K, [128, KC, T], BF16)
        ymobaT = carve(80 * K, [128, 4, T], BF16)
        MP = carve(96 * K, [128, 4, D], BF16)
        xblk = carve(104 * K, [128, KC, 512], BF16)
        ufin = carve(112 * K, [128, KC, 512], BF16)
        sga = carve(120 * K, [128, 512], F32)
        sgb = carve(122 * K, [128, 512], F32)
        u1 = carve(124 * K, [128, 512], F32)
        u2 = carve(126 * K, [128, 512], F32)
        cast_dma_cols(RP, rp_d, 0, D, "rp", step=D)
        cast_dma_cols(WGA, win_d, 4608, 5632, "wga", step=D)
        cast_dma_cols(WGB, win_d, 5632, 6656, "wgb", step=D)
        cast_dma_cols(MP, mp_d, 0, D, "mp", step=D)
        for tb in range(4):
            sl = tbs(tb)
            S.add("act", lambda e, sl=sl: e.activation(out=xblk[:], in_=R[:, :, sl], func=AF.Copy), r=Rkeys(tb), w=[("xblk",)])
            for m in range(KC):
                o4 = 4 * (m % 2)
                ms = slice(m * 128, (m + 1) * 128)
                for kc in range(KC):
                    S.add("pe", lambda e, kc=kc, ms=ms, o4=o4, sl=sl: e.matmul(PS[o4][:], lhsT=RP[:, kc, ms], rhs=yretT[:, kc, sl], start=(kc == 0), stop=(kc == KC - 1)),
                          r=[("rp", 0), ("yretT", tb)], w=[("ps", o4)])
                for kc in range(KC):
                    S.add("pe", lambda e, kc=kc, ms=ms, o4=o4: e.matmul(PS[o4 + 1][:], lhsT=WGA[:, kc, ms], rhs=xblk[:, kc, :], start=(kc == 0), stop=(kc == KC - 1)),
                          r=[("wga", 0), ("xblk",)], w=[("ps", o4 + 1)])
                for c in range(4):
                    S.add("pe", lambda e, c=c, ms=ms, o4=o4, sl=sl: e.matmul(PS[o4 + 2][:], lhsT=MP[:, c, ms], rhs=ymobaT[:, c, sl], start=(c == 0), stop=(c == 3)),
                          r=[("mp", 0), ("ymobaT", tb)], w=[("ps", o4 + 2)])
                for kc in range(KC):
                    S.add("pe", lambda e, kc=kc, ms=ms, o4=o4: e.matmul(PS[o4 + 3][:], lhsT=WGB[:, kc, ms], rhs=xblk[:, kc, :], start=(kc == 0), stop=(kc == KC - 1)),
                          r=[("wgb", 0), ("xblk",)], w=[("ps", o4 + 3)])
                S.add("act", lambda e, o4=o4: e.activation(out=sga[:], in_=PS[o4 + 1][:], func=AF.Sigmoid), r=[("ps", o4 + 1)], w=[("sga",)])
                S.add("act", lambda e, o4=o4: e.activation(out=sgb[:], in_=PS[o4 + 3][:], func=AF.Sigmoid), r=[("ps", o4 + 3)], w=[("sgb",)])
                S.add("dve", lambda e, o4=o4: e.tensor_tensor(out=u1[:], in0=PS[o4][:], in1=sga[:], op=ALU.mult), r=[("ps", o4), ("sga",)], w=[("u1",)])
                S.add("dve", lambda e, o4=o4: e.tensor_tensor(out=u2[:], in0=PS[o4 + 2][:], in1=sgb[:], op=ALU.mult), r=[("ps", o4 + 2), ("sgb",)], w=[("u2",)])
                S.add("pool", lambda e, m=m: e.tensor_tensor(out=ufin[:, m, :], in0=u1[:], in1=u2[:], op=ALU.add), r=[("u1",), ("u2",)], w=[("ufin",)])
            S.add("act", lambda e, sl=sl: e.activation(out=yretT[:, :, sl], in_=ufin[:], func=AF.Copy), r=[("ufin",)], w=[("yretT", tb)])

    def pass_o2(s):
        WO = carve(0, [128, KC, D], BF16)
        U = carve(48 * K, [128, KC, T], BF16)
        cast_dma_cols(WO, wo_d, 0, D, "wo", step=D)
        i2 = 0
        for tb in range(4):
            sl = tbs(tb)
            for m in range(KC):
                pb = i2 % 2
                i2 += 1
                ms = slice(m * 128, (m + 1) * 128)
                for kc in range(KC):
                    S.add("pe", lambda e, kc=kc, ms=ms, pb=pb, sl=sl: e.matmul(PS[pb][:], lhsT=WO[:, kc, ms], rhs=U[:, kc, sl], start=(kc == 0), stop=(kc == KC - 1)),
                          r=[("wo", 0), ("yretT", tb)], w=[("ps", pb)])
                S.add("dve", lambda e, m=m, pb=pb, sl=sl: e.scalar_tensor_tensor(out=R[:, m, sl], in0=R[:, m, sl], scalar=ALPHA, in1=PS[pb][:], op0=ALU.mult, op1=ALU.add),
                      r=[("ps", pb), ("R", tb, m)], w=[("R", tb, m)])
            layer_norm(tb, 1)

    def mixer_phase(s):
        if "r" in mixp:
            pass_ret(s)
            S.barrier()
        if "m" in mixp:
            pass_moba(s)
            S.barrier()
        if "o1" in mixp:
            pass_o1(s)
            S.barrier()
        if "o2" in mixp:
            pass_o2(s)

    for s in range(nseq):
        if "load" in phases:
            for tb in range(4):
                S.add("sp", lambda e, tb=tb: e.dma_start(out=R[:, :, tbs(tb)], in_=xT[:, :, s * T + tb * 512: s * T + (tb + 1) * 512]),
                      w=Rkeys(tb), dsem=("xin", tb))
            S.barrier()
        if "ffn1" in phases:
            ffn_phase(s, 0)
            S.barrier()
        if "mix" in phases:
            mixer_phase(s)
            S.barrier()
        if "ffn2" in phases:
            ffn_phase(s, 1)
            S.barrier()
        out_phase(s)
    S.add("sp", lambda e: e.nop(), r=[k for tb in range(4) for k in Rkeys(tb)], w=[k for tb in range(4) for k in Rkeys(tb)])

    engines = {
        "pe": ("tensor", nc.tensor),
        "act": ("scalar", nc.scalar),
        "dve": ("vector", nc.vector),
        "pool": ("gpsimd", nc.gpsimd),
        "sp": ("sync", nc.sync),
    }
    S.emit(nc, engines)
    return nc, S


def _feat_major(v):
    return np.ascontiguousarray(np.asarray(v, np.float32).reshape(KC, 128).T)


def _module_consts():
    c = {}
    pos = np.arange(T, dtype=np.float32)
    inv = (1.0 / (np.float32(10000.0) ** np.linspace(0.0, 1.0, 64, dtype=np.float32))).astype(np.float32)
    ang = (pos[:, None] * inv[None, :]).astype(np.float32)
    rcs = np.stack([np.cos(ang), np.sin(ang)], axis=1).astype(np.float32)
    c["rcs"] = np.ascontiguousarray(rcs.reshape(16, 128, 2, 64).transpose(1, 0, 2, 3))
    inv2 = (1.0 / (np.float32(500000.0) ** (np.arange(0, 16, 2, dtype=np.float32) / np.float32(16)))).astype(np.float32)
    ang2 = (pos[:, None] * inv2[None, :]).astype(np.float32)
    mcs = np.stack([np.cos(ang2), np.sin(ang2)], axis=1).astype(np.float32)
    c["mcs"] = np.ascontiguousarray(mcs.reshape(16, 128, 2, 8).transpose(1, 0, 2, 3))
    g = 1.0 - 2.0 ** (-5.0 - np.arange(4, dtype=np.float64))
    idx = np.arange(128, dtype=np.float64)
    sc = 128.0 ** -0.5
    rdec = np.zeros((128, 1032), np.float32)
    for h in range(4):
        diff = idx[None, :] - idx[:, None]
        dt = np.where(diff >= 0, g[h] ** np.maximum(diff, 0.0), 0.0) * sc
        rdec[:, h * 128:(h + 1) * 128] = dt
        rdec[:, 512 + h * 128:512 + (h + 1) * 128] = (g[h] ** (idx + 1.0))[None, :]
        rdec[:, 1024 + h] = g[h] ** (127.0 - idx) * sc
    c["rdec"] = rdec
    mcb = np.zeros((128, 1152), np.float32)
    mcb[:, 0:128] = (idx[:, None] <= idx[None, :]).astype(np.float32)
    for n in range(8):
        mcb[n, 128 + n * 128:128 + (n + 1) * 128] = 1.0
    c["mconstb"] = mcb.astype(ml_dtypes.bfloat16)
    return c


def prep_shared(inp):
    sh = {}
    for f, (gu, dn) in enumerate([("ffn1_w_gu", "ffn1_w_down"), ("ffn2_w_gu", "ffn2_w_down")]):
        w = np.asarray(inp[gu], np.float32)[0]
        w = w.reshape(KC, 128, 2, NJ, 128)
        sh["wgu%d" % (f + 1)] = np.ascontiguousarray(w.transpose(3, 1, 2, 0, 4))
        sh["wd%d" % (f + 1)] = np.ascontiguousarray(np.asarray(inp[dn], np.float32)[0].reshape(NJ, 128, D))
    lnp = np.stack([_feat_major(inp[k][0]) for k in ("ln1_g", "ln1_b", "lnm_g", "lnm_b", "ln2_g", "ln2_b")], axis=1)
    sh["lnp"] = np.ascontiguousarray(lnp)
    def kmajor(w, nk):
        w = np.asarray(w, np.float32)
        return np.ascontiguousarray(w.reshape(nk, 128, w.shape[-1]).transpose(1, 0, 2))
    sh["win"] = kmajor(inp["w_in"][0], KC)
    sh["retp"] = kmajor(inp["ret_proj"][0], KC)
    sh["mobp"] = kmajor(inp["moba_proj"][0], 4)
    sh["wout"] = kmajor(inp["w_out"][0], KC)
    sh.update(_module_consts())
    cb = np.zeros((128, 256), np.float32)
    cb[:, 0:128] = np.eye(128, dtype=np.float32)
    cb[:, 128:256] = 1.0 / 1024.0
    sh["constb"] = cb.astype(ml_dtypes.bfloat16)
    return sh


def shard_x(x, nseq=SEQ_PER_CORE, ncores=NCORES):
    x = np.asarray(x, np.float32)
    maps = []
    for c in range(ncores):
        xc = x[c * nseq:(c + 1) * nseq].reshape(nseq * T, KC, 128)
        maps.append(np.ascontiguousarray(xc.transpose(2, 1, 0)))
    return maps


def unshard(outs, nseq=SEQ_PER_CORE):
    res = []
    for o in outs:
        res.append(np.asarray(o, np.float32).transpose(2, 1, 0).reshape(nseq, T, D))
    return np.ascontiguousarray(np.concatenate(res, axis=0))


_CACHE = {}


def kernel(**inputs):
    if "nc" not in _CACHE:
        _CACHE["nc"] = build_program()[0]
    nc = _CACHE["nc"]
    sh = prep_shared(inputs)
    xs = shard_x(inputs["x"])
    in_maps = []
    for c in range(NCORES):
        m = dict(sh)
        m["xT"] = xs[c]
        in_maps.append(m)
    res = run_bass_kernel_spmd(nc, in_maps, core_ids=list(range(NCORES)))
    return unshard([r["outT"] for r in res.results])
```

```python
import math
import numpy as np
import ml_dtypes
import concourse.bass as bass
import concourse.mybir as mybir
from concourse.bass_utils import run_bass_kernel_spmd

F32 = mybir.dt.float32
BF16 = mybir.dt.bfloat16
AF = mybir.ActivationFunctionType
ALU = mybir.AluOpType
AX = mybir.AxisListType

D = 1024
T = 2048
KC = 8
DFF = 2816
NJ = 22
NCORES = 8
SEQ_PER_CORE = 2
ALPHA = 2.0 ** 0.25
LN_EPS = 1e-5
GN_EPS = 1e-5
WIN = 6656
GROUPS = [(0, 8), (8, 15), (15, 22)]
NEG = -30000.0


class Op:
    __slots__ = ("eng", "fn", "deps", "dsem", "sig", "cnt")

    def __init__(self, eng, fn, deps, dsem):
        self.eng = eng
        self.fn = fn
        self.deps = deps
        self.dsem = dsem
        self.sig = dsem is not None
        self.cnt = 0


class Sched:
    def __init__(self):
        self.ops = []
        self.lastw = {}
        self.readers = {}
        self.bar = set()
        self.last_stream = {}

    def add(self, eng, fn, r=(), w=(), dsem=None):
        i = len(self.ops)
        deps = set(self.bar)
        for k in r:
            j = self.lastw.get(k)
            if j is not None:
                deps.add(j)
        for k in w:
            j = self.lastw.get(k)
            if j is not None:
                deps.add(j)
            rd = self.readers.get(k)
            if rd:
                deps.update(rd.values())
        stream = ("dma", dsem) if dsem is not None else eng
        for k in r:
            self.readers.setdefault(k, {})[stream] = i
        for k in w:
            self.lastw[k] = i
            self.readers[k] = {}
        self.last_stream[stream] = i
        self.ops.append(Op(eng, fn, deps, dsem))
        return i

    def barrier(self):
        self.bar = set(self.last_stream.values())

    def emit(self, nc, engines):
        ops = self.ops

        def stream_of(o):
            return ("dma", o.dsem) if o.dsem is not None else o.eng

        for i, o in enumerate(ops):
            best = {}
            for j in o.deps:
                p = ops[j]
                st = stream_of(p)
                if p.dsem is None and p.eng == "pe" and o.eng == "pe" and o.dsem is None:
                    continue
                if st not in best or best[st] < j:
                    best[st] = j
            o.deps = best
            for j in best.values():
                ops[j].sig = True
        cnt = {}
        for o in ops:
            if o.sig:
                st = stream_of(o)
                inc = 16 if o.dsem is not None else 1
                cnt[st] = cnt.get(st, 0) + inc
                o.cnt = cnt[st]
        sems = {}
        import contextlib
        stack = contextlib.ExitStack()
        for k, st in enumerate(cnt.keys()):
            sems[st] = stack.enter_context(nc.semaphore("s%d" % k))
        self.max_counts = dict(cnt)
        with stack:
            with nc.Block() as block:
                for ename, (deco, _h) in engines.items():
                    my = [(i, o) for i, o in enumerate(ops) if o.eng == ename]
                    if not my:
                        continue

                    def body(eng, my=my, ename=ename):
                        waited = {}
                        for i, o in my:
                            for st, j in o.deps.items():
                                v = ops[j].cnt
                                if waited.get(st, 0) < v:
                                    eng.wait_ge(sems[st], v)
                                    waited[st] = v
                            ins = o.fn(eng)
                            if o.sig:
                                st = stream_of(o)
                                ins.then_inc(sems[st], 16 if o.dsem is not None else 1)
                                if o.dsem is None:
                                    pass
                    getattr(block, deco)(body)


def _tile(nc, name, shape, dt):
    return nc.sbuf_tensor(name, list(shape), dt).__enter__()


def _ptile(nc, name, shape, dt):
    return nc.psum_tensor(name, list(shape), dt).__enter__()


def build_program(nseq=SEQ_PER_CORE, phases=("ffn1", "mix", "ffn2"), dbg=None, mixp=("r", "m", "o1", "o2")):
    nc = bass.Bass("TRN2", target_bir_lowering=False)
    S = Sched()
    NT = nseq * T

    def din(name, shape, dt=F32):
        return nc.dram_tensor(name, list(shape), dt, kind="ExternalInput").ap()

    xT = din("xT", [128, KC, NT])
    outT = nc.dram_tensor("outT", [128, KC, NT], F32, kind="ExternalOutput").ap()
    wgu_d = [din("wgu1", [NJ, 128, 2, KC, 128]), din("wgu2", [NJ, 128, 2, KC, 128])]
    wd_d = [din("wd1", [NJ, 128, D]), din("wd2", [NJ, 128, D])]
    lnp_d = din("lnp", [128, 6, KC])
    cb_d = din("constb", [128, 256], BF16)
    win_d = din("win", [128, KC, WIN])
    rp_d = din("retp", [128, KC, D])
    mp_d = din("mobp", [128, 4, D])
    wo_d = din("wout", [128, KC, D])
    rcs_d = din("rcs", [128, 16, 2, 64])
    mcs_d = din("mcs", [128, 16, 2, 8])
    rdec_d = din("rdec", [128, 1032])
    mcb_d = din("mconstb", [128, 1152], BF16)

    R = _tile(nc, "R", [128, KC, T], F32)
    ARENA = _tile(nc, "ARENA", [128, 73216], BF16)
    lnp = _tile(nc, "lnp_sb", [128, 6, KC], F32)
    constb = _tile(nc, "constb_sb", [128, 256], BF16)
    ident = constb[:, 0:128]
    ones_div = constb[:, 128:256]
    epsc = _tile(nc, "epsc", [128, 2], F32)

    def carve(off_bytes, shape, dt):
        n = int(np.prod(shape[1:]))
        if dt == BF16:
            a = ARENA[:, off_bytes // 2: off_bytes // 2 + n]
        else:
            a = ARENA[:, off_bytes // 2: off_bytes // 2 + 2 * n].bitcast(F32)
        if len(shape) == 2:
            return a
        names = " ".join("d%d" % i for i in range(1, len(shape)))
        kw = {"d%d" % i: shape[i] for i in range(1, len(shape))}
        return a.rearrange("p (%s) -> p %s" % (names, names), **kw)

    K = 1024
    Xb = carve(0, [128, KC, T], BF16)
    actT = carve(32 * K, [128, 8, T], BF16)
    Wgu = [carve(64 * K + 4 * K * b, [128, 2, KC, 128], BF16) for b in range(3)]
    Wd = carve(76 * K, [128, 8, D], BF16)
    zb = carve(92 * K, [128, KC, 512], BF16)
    zsq = carve(100 * K, [128, KC, 512], BF16)
    mean_sb = carve(108 * K, [128, 512], F32)
    rstd_sb = carve(110 * K, [128, 512], F32)
    var_sb = carve(112 * K, [128, 512], F32)
    sg = [carve(114 * K + 1 * K * b, [128, 512], BF16) for b in range(2)]

    PSALL = _ptile(nc, "psall", [128, 4096], F32)
    PS = [PSALL[:, b * 512:(b + 1) * 512] for b in range(8)]

    def PSB(b):
        return PSALL[:, b * 512:(b + 1) * 512].bitcast(BF16)

    S.add("sp", lambda e: e.dma_start(out=lnp[:], in_=lnp_d), w=[("lnp",)], dsem="c_lnp")
    S.add("sp", lambda e: e.dma_start(out=constb[:], in_=cb_d), w=[("constb",)], dsem="c_cb")
    S.add("dve", lambda e: e.memset(epsc[:, 0:1], LN_EPS), w=[("epsc",)])
    S.add("dve", lambda e: e.memset(epsc[:, 1:2], GN_EPS), w=[("epsc",)])

    cnt = {"wgu": 0}

    def tbs(tb):
        return slice(tb * 512, (tb + 1) * 512)

    def Rkeys(tb):
        return [("R", tb, kc) for kc in range(KC)]

    def layer_norm(tb, gi):
        sl = tbs(tb)
        Rb = R[:, :, sl]
        S.add("dve", lambda e: e.tensor_copy(out=zb[:], in_=Rb), r=Rkeys(tb), w=[("zb",)])
        S.add("act", lambda e: e.activation(out=zsq[:], in_=Rb, func=AF.Square), r=Rkeys(tb), w=[("zsq",)])
        pm, pq = PS[6], PS[7]
        for kc in range(KC):
            S.add("pe", lambda e, kc=kc: e.matmul(pm[:], lhsT=ones_div, rhs=zb[:, kc, :], start=(kc == 0), stop=(kc == KC - 1)),
                  r=[("zb",), ("constb",)], w=[("ps", 6)])
        for kc in range(KC):
            S.add("pe", lambda e, kc=kc: e.matmul(pq[:], lhsT=ones_div, rhs=zsq[:, kc, :], start=(kc == 0), stop=(kc == KC - 1)),
                  r=[("zsq",), ("constb",)], w=[("ps", 7)])
        S.add("act", lambda e: e.activation(out=mean_sb[:], in_=pm[:], func=AF.Copy), r=[("ps", 6)], w=[("mean",)])
        S.add("act", lambda e: e.activation(out=var_sb[:], in_=pm[:], func=AF.Square), r=[("ps", 6)], w=[("var",)])
        S.add("dve", lambda e: e.tensor_tensor(out=var_sb[:], in0=pq[:], in1=var_sb[:], op=ALU.subtract),
              r=[("ps", 7), ("var",)], w=[("var",)])
        S.add("act", lambda e: e.activation(out=var_sb[:], in_=var_sb[:], func=AF.Sqrt, bias=epsc[:, 0:1]),
              r=[("var",), ("epsc",)], w=[("var",)])
        S.add("dve", lambda e: e.reciprocal(out=rstd_sb[:], in_=var_sb[:]), r=[("var",)], w=[("rstd",)])
        mb = mean_sb[:].unsqueeze(1).to_broadcast([128, KC, 512])
        rb = rstd_sb[:].unsqueeze(1).to_broadcast([128, KC, 512])
        S.add("dve", lambda e: e.tensor_tensor(out=Rb, in0=Rb, in1=mb, op=ALU.subtract),
              r=Rkeys(tb) + [("mean",)], w=Rkeys(tb))
        S.add("dve", lambda e: e.tensor_tensor(out=Rb, in0=Rb, in1=rb, op=ALU.mult),
              r=Rkeys(tb) + [("rstd",)], w=Rkeys(tb))
        for kc in range(KC):
            S.add("act", lambda e, kc=kc: e.activation(out=R[:, kc, sl], in_=R[:, kc, sl], func=AF.Identity,
                                                      scale=lnp[:, 2 * gi, kc:kc + 1], bias=lnp[:, 2 * gi + 1, kc:kc + 1]),
                  r=[("R", tb, kc), ("lnp",)], w=[("R", tb, kc)])

    def ffn_phase(s, f):
        gi = 0 if f == 0 else 2
        if f == 0:
            for tb in range(4):
                S.add("sp", lambda e, tb=tb: e.dma_start(out=R[:, :, tbs(tb)], in_=xT[:, :, s * T + tb * 512: s * T + (tb + 1) * 512]),
                      w=Rkeys(tb), dsem=("xin", tb))
        for tb in range(4):
            S.add("dve", lambda e, tb=tb: e.tensor_copy(out=Xb[:, :, tbs(tb)], in_=R[:, :, tbs(tb)]), r=Rkeys(tb), w=[("xb", tb)])
            S.add("act", lambda e, tb=tb: e.activation(out=R[:, :, tbs(tb)], in_=R[:, :, tbs(tb)], func=AF.Copy, scale=ALPHA),
                  r=Rkeys(tb), w=Rkeys(tb))
        it = 0
        for (j0, j1) in GROUPS:
            G = j1 - j0
            for jl in range(G):
                S.add("pool", lambda e, jl=jl, j=j0 + jl: e.dma_start(out=Wd[:, jl, :], in_=wd_d[f][j]),
                      w=[("wd", jl)], dsem=("wd", jl))
            for jl in range(G):
                j = j0 + jl
                b = cnt["wgu"] % 3
                cnt["wgu"] += 1
                S.add("pool", lambda e, b=b, j=j: e.dma_start(out=Wgu[b][:], in_=wgu_d[f][j]), w=[("wgu", b)], dsem=("wgu", b))
                for tb in range(4):
                    pg, pu = PS[it % 2], PS[2 + it % 2]
                    kg, ku, ks = ("ps", it % 2), ("ps", 2 + it % 2), ("sg", it % 2)
                    sgt = sg[it % 2]
                    it += 1
                    for kc in range(KC):
                        S.add("pe", lambda e, pg=pg, b=b, kc=kc, tb=tb: e.matmul(pg[:], lhsT=Wgu[b][:, 0, kc, :], rhs=Xb[:, kc, tbs(tb)],
                                                                                start=(kc == 0), stop=(kc == KC - 1)),
                              r=[("wgu", b), ("xb", tb)], w=[kg])
                    for kc in range(KC):
                        S.add("pe", lambda e, pu=pu, b=b, kc=kc, tb=tb: e.matmul(pu[:], lhsT=Wgu[b][:, 1, kc, :], rhs=Xb[:, kc, tbs(tb)],
                                                                                start=(kc == 0), stop=(kc == KC - 1)),
                              r=[("wgu", b), ("xb", tb)], w=[ku])
                    S.add("act", lambda e, pg=pg, sgt=sgt: e.activation(out=sgt[:], in_=pg[:], func=AF.Silu), r=[kg], w=[ks])
                    S.add("dve", lambda e, pu=pu, sgt=sgt, jl=jl, tb=tb: e.tensor_tensor(out=actT[:, jl, tbs(tb)], in0=pu[:], in1=sgt[:], op=ALU.mult),
                          r=[ku, ks], w=[("actT", jl, tb)])
            lastg = (j1 == NJ)
            order = [(m, tb) for tb in range(4) for m in range(KC)] if lastg else [(m, tb) for m in range(KC) for tb in range(4)]
            i2 = 0
            for (m, tb) in order:
                pd = PS[4 + i2 % 2]
                kd = ("ps", 4 + i2 % 2)
                i2 += 1
                for jl in range(G):
                    S.add("pe", lambda e, pd=pd, jl=jl, m=m, tb=tb, G=G: e.matmul(pd[:], lhsT=Wd[:, jl, m * 128:(m + 1) * 128], rhs=actT[:, jl, tbs(tb)],
                                                                               start=(jl == 0), stop=(jl == G - 1)),
                          r=[("wd", jl), ("actT", jl, tb)], w=[kd])
                S.add("dve", lambda e, pd=pd, m=m, tb=tb: e.scalar_tensor_tensor(out=R[:, m, tbs(tb)], in0=pd[:], scalar=0.5, in1=R[:, m, tbs(tb)],
                                                                               op0=ALU.mult, op1=ALU.add),
                      r=[kd, ("R", tb, m)], w=[("R", tb, m)])
                if lastg and m == KC - 1:
                    layer_norm(tb, gi)

    def out_phase(s):
        for tb in range(4):
            S.add("sp", lambda e, tb=tb: e.dma_start(out=outT[:, :, s * T + tb * 512: s * T + (tb + 1) * 512], in_=R[:, :, tbs(tb)]),
                  r=Rkeys(tb), dsem=("xout", tb))


    G_H = [1.0 - 2.0 ** (-5.0 - h) for h in range(4)]
    GC = [g ** 128 for g in G_H]

    def tls(t):
        return slice(t * 128, (t + 1) * 128)

    def cast_dma_cols(dst, src_d, c0, c1, keybase, step=1536):
        for pi, a in enumerate(range(c0, c1, step)):
            b_ = min(a + step, c1)
            k = (keybase, pi)
            S.add("pool", lambda e, a=a, b_=b_: e.dma_start(out=dst[:, :, a - c0:b_ - c0], in_=src_d[:, :, a:b_]), w=[k], dsem=k)

    def xbt_cast(dst, t, key):
        S.add("act", lambda e: e.activation(out=dst[:], in_=R[:, :, tls(t)], func=AF.Copy),
              r=[("R", t // 4, kc) for kc in range(KC)], w=[key])

    def pass_ret(s):
        Wr = carve(0, [128, KC, 3072], BF16)
        yretT = carve(48 * K, [128, KC, T], BF16)
        o = 80 * K
        rcs = carve(o, [128, 16, 2, 64], F32); o += 8 * K
        rdec = carve(o, [128, 1032], F32); o += 4128
        DT = rdec[:, 0:512]
        GQ = rdec[:, 512:1024]
        gk = rdec[:, 1024:1028]
        xbt = [carve(o + 2 * K * i, [128, KC, 128], BF16) for i in range(2)]; o += 4 * K
        qkr = carve(o, [128, 8, 2, 64], BF16); o += 2 * K
        tA = carve(o, [128, 8, 2, 64], F32); o += 4 * K
        tB = carve(o, [128, 8, 64], F32); o += 2 * K
        tC = carve(o, [128, 8, 64], F32); o += 2 * K
        kdec = carve(o, [128, 4, 128], BF16); o += 1 * K
        vbf = carve(o, [128, 1024], BF16); o += 2 * K
        srg = carve(o, [128, 1024], F32); o += 4 * K
        qT = carve(o, [128, 4, 128], BF16); o += 1 * K
        qdT = carve(o, [128, 4, 128], BF16); o += 1 * K
        kT = carve(o, [128, 4, 128], BF16); o += 1 * K
        sT = carve(o, [128, 4, 128], BF16); o += 1 * K
        state = carve(o, [128, 4, 256], F32); o += 4 * K
        stbf = carve(o, [128, 4, 256], BF16); o += 2 * K
        stats = carve(o, [128, 4, 6], F32); o += 128
        mv = carve(o, [128, 4, 2], F32); o += 64
        rs = carve(o, [128, 4], F32); o += 64
        yn = carve(o, [128, 1024], F32); o += 4 * K
        yrt = carve(o, [128, 1024], BF16); o += 2 * K
        assert o <= 143 * K, o

        cast_dma_cols(Wr, win_d, 0, 3072, "wr")
        S.add("sp", lambda e: e.dma_start(out=rcs[:], in_=rcs_d), w=[("rcs",)], dsem="c_rcs")
        S.add("sp", lambda e: e.dma_start(out=rdec[:], in_=rdec_d), w=[("rdec",)], dsem="c_rdec")
        S.add("dve", lambda e: e.memset(state[:].rearrange("p h e -> p (h e)"), 0.0), w=[("state",)])
        S.add("dve", lambda e: e.memset(stbf[:].rearrange("p h e -> p (h e)"), 0.0), w=[("stbf",)])

        for t in range(16):
            xb = xbt[t % 2]
            kx = ("xbt", t % 2)
            xbt_cast(xb, t, kx)
            for cb in range(6):
                for kc in range(KC):
                    S.add("pe", lambda e, cb=cb, kc=kc, xb=xb: e.matmul(PS[cb][:], lhsT=xb[:, kc, :], rhs=Wr[:, kc, cb * 512:(cb + 1) * 512],
                                                                     start=(kc == 0), stop=(kc == KC - 1)),
                          r=[kx, ("wr", cb // 3)], w=[("ps", cb)])
            if dbg is not None and dbg < 2:
                continue
            cos16 = rcs[:, t, 0, :].unsqueeze(1).to_broadcast([128, 16, 64])
            sin8 = rcs[:, t, 1, :].unsqueeze(1).to_broadcast([128, 8, 64])
            P16 = PSALL[:, 0:1024].rearrange("p (g i) -> p g i", g=16, i=64)
            P8 = PSALL[:, 0:1024].rearrange("p (g f i) -> p g f i", g=8, f=2, i=64)
            tA16 = tA[:].rearrange("p g f i -> p (g f) i")
            pk = [("ps", 0), ("ps", 1)]
            S.add("dve", lambda e, cos16=cos16: e.tensor_tensor(out=tA16, in0=P16, in1=cos16, op=ALU.mult), r=pk + [("rcs",)], w=[("tA",)])
            S.add("dve", lambda e, sin8=sin8: e.tensor_tensor(out=tB[:], in0=P8[:, :, 1, :], in1=sin8, op=ALU.mult), r=pk + [("rcs",)], w=[("tB",)])
            S.add("dve", lambda e, sin8=sin8: e.tensor_tensor(out=tC[:], in0=P8[:, :, 0, :], in1=sin8, op=ALU.mult), r=pk + [("rcs",)], w=[("tC",)])
            S.add("pool", lambda e: e.tensor_tensor(out=qkr[:, :, 0, :], in0=tA[:, :, 0, :], in1=tB[:], op=ALU.subtract),
                  r=[("tA",), ("tB",)], w=[("qkr0",)])
            S.add("pool", lambda e: e.tensor_tensor(out=qkr[:, :, 1, :], in0=tA[:, :, 1, :], in1=tC[:], op=ALU.add),
                  r=[("tA",), ("tC",)], w=[("qkr1",)])
            qk_keys = [("qkr0",), ("qkr1",)]
            qflat = qkr[:, 0:4].rearrange("p h f i -> p h (f i)")
            kflat = qkr[:, 4:8].rearrange("p h f i -> p h (f i)")
            if dbg is not None and dbg < 3:
                continue
            S.add("pool", lambda e, kflat=kflat: e.tensor_tensor(out=kdec[:], in0=kflat, in1=gk.unsqueeze(2).to_broadcast([128, 4, 128]), op=ALU.mult),
                  r=qk_keys + [("rdec",)], w=[("kdec",)])
            S.add("act", lambda e: e.activation(out=vbf[:], in_=PSALL[:, 1024:2048], func=AF.Copy), r=[("ps", 2), ("ps", 3)], w=[("vbf",)])
            S.add("act", lambda e: e.activation(out=srg[:], in_=PSALL[:, 2048:3072], func=AF.Silu), r=[("ps", 4), ("ps", 5)], w=[("srg",)])
            if dbg is not None and dbg < 4:
                continue
            for h in range(4):
                S.add("pe", lambda e, h=h, qflat=qflat: e.matmul(PS[6][:, h * 128:(h + 1) * 128], lhsT=qflat[:, h, :], rhs=ident, start=True, stop=True),
                      r=qk_keys + [("constb",)], w=[("ps", 6)])
            for h in range(4):
                S.add("pe", lambda e, h=h, kflat=kflat: e.matmul(PS[7][:, h * 128:(h + 1) * 128], lhsT=kflat[:, h, :], rhs=ident, start=True, stop=True),
                      r=qk_keys + [("constb",)], w=[("ps", 7)])
            S.add("act", lambda e: e.activation(out=qT[:].rearrange("p h c -> p (h c)"), in_=PS[6][:], func=AF.Copy), r=[("ps", 6)], w=[("qT",)])
            S.add("dve", lambda e: e.tensor_tensor(out=qdT[:].rearrange("p h c -> p (h c)"), in0=PS[6][:], in1=GQ, op=ALU.mult),
                  r=[("ps", 6), ("rdec",)], w=[("qdT",)])
            S.add("act", lambda e: e.activation(out=kT[:].rearrange("p h c -> p (h c)"), in_=PS[7][:], func=AF.Copy), r=[("ps", 7)], w=[("kT",)])
            if dbg is not None and dbg < 5:
                continue
            for h in range(4):
                S.add("pe", lambda e, h=h: e.matmul(PS[4][:, h * 128:(h + 1) * 128], lhsT=kT[:, h, :], rhs=qT[:, h, :], start=True, stop=True),
                      r=[("kT",), ("qT",)], w=[("ps", 4)])
            S.add("dve", lambda e: e.tensor_tensor(out=sT[:].rearrange("p h c -> p (h c)"), in0=PS[4][:], in1=DT, op=ALU.mult),
                  r=[("ps", 4), ("rdec",)], w=[("sT",)])
            if dbg is not None and dbg < 6:
                continue
            for h in range(4):
                po = PSALL[:, h * 256:(h + 1) * 256]
                S.add("pe", lambda e, h=h, po=po: e.matmul(po, lhsT=sT[:, h, :], rhs=vbf[:, h * 256:(h + 1) * 256], start=True, stop=False),
                      r=[("sT",), ("vbf",)], w=[("ps", h // 2)])
                S.add("pe", lambda e, h=h, po=po: e.matmul(po, lhsT=qdT[:, h, :], rhs=stbf[:, h, :], start=False, stop=True),
                      r=[("qdT",), ("stbf",)], w=[("ps", h // 2)])
            if dbg is not None and dbg < 7:
                continue
            for h in range(4):
                pkv = PSALL[:, 1024 + h * 256:1024 + (h + 1) * 256]
                S.add("pe", lambda e, h=h, pkv=pkv: e.matmul(pkv, lhsT=kdec[:, h, :], rhs=vbf[:, h * 256:(h + 1) * 256], start=True, stop=True),
                      r=[("kdec",), ("vbf",)], w=[("ps", 2 + h // 2)])
            for h in range(4):
                pkv = PSALL[:, 1024 + h * 256:1024 + (h + 1) * 256]
                S.add("dve", lambda e, h=h, pkv=pkv: e.scalar_tensor_tensor(out=state[:, h, :], in0=state[:, h, :], scalar=GC[h], in1=pkv,
                                                                         op0=ALU.mult, op1=ALU.add),
                      r=[("state",), ("ps", 2 + h // 2)], w=[("state",)])
            S.add("act", lambda e: e.activation(out=stbf[:], in_=state[:], func=AF.Copy), r=[("state",)], w=[("stbf",)])
            if dbg is not None and dbg < 8:
                continue
            pk01 = [("ps", 0), ("ps", 1)]
            for h in range(4):
                po = PSALL[:, h * 256:(h + 1) * 256]
                S.add("dve", lambda e, h=h, po=po: e.bn_stats(out=stats[:, h, :], in_=po), r=pk01, w=[("stats",)])
            for h in range(4):
                S.add("dve", lambda e, h=h: e.bn_aggr(out=mv[:, h, :], in_=stats[:, h, :]), r=[("stats",)], w=[("mv",)])
            S.add("act", lambda e: e.activation(out=rs[:], in_=mv[:, :, 1], func=AF.Sqrt, bias=epsc[:, 1:2]), r=[("mv",), ("epsc",)], w=[("rs",)])
            S.add("dve", lambda e: e.reciprocal(out=rs[:], in_=rs[:]), r=[("rs",)], w=[("rs",)])
            for h in range(4):
                po = PSALL[:, h * 256:(h + 1) * 256]
                S.add("dve", lambda e, h=h, po=po: e.tensor_scalar(out=yn[:, h * 256:(h + 1) * 256], in0=po, scalar1=mv[:, h, 0:1], scalar2=rs[:, h:h + 1],
                                                                op0=ALU.subtract, op1=ALU.mult),
                      r=pk01 + [("mv",), ("rs",)], w=[("yn",)])
            S.add("pool", lambda e: e.tensor_tensor(out=yrt[:], in0=yn[:], in1=srg[:], op=ALU.mult), r=[("yn",), ("srg",)], w=[("yrt",)])
            if dbg is not None and dbg < 9:
                continue
            for c in range(8):
                S.add("pe", lambda e, c=c: e.matmul(PSALL[:, 3072 + c * 128:3072 + (c + 1) * 128], lhsT=yrt[:, c * 128:(c + 1) * 128], rhs=ident, start=True, stop=True),
                      r=[("yrt",), ("constb",)], w=[("ps", 6 + c // 4)])
            S.add("act", lambda e, t=t: e.activation(out=yretT[:, :, tls(t)], in_=PSALL[:, 3072:4096].rearrange("p (c q) -> p c q", c=8), func=AF.Copy),
                  r=[("ps", 6), ("ps", 7)], w=[("yretT", t // 4)])

    def pass_moba(s):
        Wm = carve(0, [128, KC, 1536], BF16)
        kTa = carve(24 * K, [128, 4, T], BF16)
        vaug = carve(96 * K, [128, 16, 8, 66], BF16)
        ymobaT = carve(80 * K, [128, 4, T], BF16)
        o = 96 * K + 16896
        mcs = carve(o, [128, 16, 2, 8], F32); o += 1 * K
        mcb = carve(o, [128, 1152], BF16); o += 2304
        tri01 = mcb[:, 0:128]
        E128 = mcb[:, 128:1152].rearrange("p (n k) -> p n k", n=8)
        xbt = [carve(o + 2 * K * i, [128, KC, 128], BF16) for i in range(2)]; o += 4 * K
        qktm = carve(o, [128, 16, 64], BF16); o += 2 * K
        r1 = carve(o, [128, 16, 8], F32); o += 512
        r2 = carve(o, [128, 16, 8], F32); o += 512
        qTb = carve(o, [128, 4, 256], BF16); o += 2 * K
        qz = carve(44 * K, [128, 4, 2, 256], BF16)
        ksum = carve(o, [128, 4, 8], F32); o += 128
        ksb = carve(o, [128, 4, 64], BF16); o += 512
        gs = carve(o, [128, 8, 8], F32); o += 256
        cmp_ = carve(o, [128, 8, 8, 8], F32); o += 2 * K
        cntt = carve(o, [128, 8, 8], F32); o += 256
        negm = carve(o, [128, 8, 32], BF16); o += 512
        negT = carve(o, [128, 8, 256], BF16); o += 4 * K
        expP = [carve(o + 1024 * i, [128, 512], BF16) for i in range(3)]; o += 3 * K
        rec = carve(o, [128, 2, 2, 4], F32); o += 64
        ytm = carve(o, [128, 2, 512], BF16); o += 2 * K
        qkf = carve(40 * K, [128, 1024], F32)
        assert o <= 143 * K, o

        cast_dma_cols(Wm, win_d, 3072, 4608, "wm")
        S.add("sp", lambda e: e.dma_start(out=mcs[:], in_=mcs_d), w=[("mcs",)], dsem="c_mcs")
        S.add("sp", lambda e: e.dma_start(out=mcb[:], in_=mcb_d), w=[("mcb",)], dsem="c_mcb")
        S.add("dve", lambda e: e.memset(vaug[:].rearrange("p t h d -> p (t h d)"), 1.0), w=[("vaug", t) for t in range(16)])
        S.add("dve", lambda e: e.memset(negT[:].rearrange("p h q -> p (h q)"), 0.0), w=[("negT",)])
        S.add("dve", lambda e: e.memset(qz[:].rearrange("p c j q -> p (c j q)"), 0.0), w=[("qz", 0), ("qz", 1)])
        S.add("dve", lambda e: e.memset(negm[:].rearrange("p h n -> p (h n)"), 0.0), w=[("negm",)])
        S.add("dve", lambda e: e.memset(ksum[:].rearrange("p c n -> p (c n)"), 0.0), w=[("ksum",)])
        S.add("dve", lambda e: e.memset(ksb[:].rearrange("p c n -> p (c n)"), 0.0), w=[("ksb",)])

        PQK = PSALL[:, 0:1024].rearrange("p (g d) -> p g d", g=16, d=64)
        sti = 0
        for b in range(8):
            if dbg == 100:
                break
            for tt in range(2):
                t = 2 * b + tt
                xb = xbt[t % 2]
                kx = ("xbt", t % 2)
                xbt_cast(xb, t, kx)
                for cb in range(3):
                    for kc in range(KC):
                        S.add("pe", lambda e, cb=cb, kc=kc, xb=xb: e.matmul(PS[cb][:], lhsT=xb[:, kc, :], rhs=Wm[:, kc, cb * 512:(cb + 1) * 512],
                                                                         start=(kc == 0), stop=(kc == KC - 1)),
                              r=[kx, ("wm", 0)], w=[("ps", cb)])
                if dbg == 101:
                    continue
                pk = [("ps", 0), ("ps", 1)]
                cosb = mcs[:, t, 0, :].unsqueeze(1).to_broadcast([128, 16, 8])
                sinb = mcs[:, t, 1, :].unsqueeze(1).to_broadcast([128, 16, 8])
                qkf16 = qkf[:].rearrange("p (g d) -> p g d", g=16, d=64)
                x1 = qkf16[:, :, 0:8]
                x2 = qkf16[:, :, 8:16]
                S.add("act", lambda e: e.activation(out=qkf[:], in_=PSALL[:, 0:1024], func=AF.Copy), r=pk, w=[("qkf",)])
                S.add("pool", lambda e, qkf16=qkf16: e.tensor_copy(out=qktm[:], in_=qkf16), r=[("qkf",)], w=[("qktm_c",)])
                S.add("pool", lambda e, x1=x1, cosb=cosb: e.tensor_tensor(out=r1[:], in0=x1, in1=cosb, op=ALU.mult), r=[("qkf",), ("mcs",)], w=[("r1",)])
                S.add("pool", lambda e, x2=x2, sinb=sinb: e.tensor_tensor(out=r2[:], in0=x2, in1=sinb, op=ALU.mult), r=[("qkf",), ("mcs",)], w=[("r2",)])
                S.add("pool", lambda e: e.tensor_tensor(out=qktm[:, :, 0:8], in0=r1[:], in1=r2[:], op=ALU.subtract), r=[("r1",), ("r2",), ("qktm_c",)], w=[("qktm_a",)])
                S.add("pool", lambda e, x1=x1, sinb=sinb: e.tensor_tensor(out=r1[:], in0=x1, in1=sinb, op=ALU.mult), r=[("qkf",), ("mcs",)], w=[("r1",)])
                S.add("pool", lambda e, x2=x2, cosb=cosb: e.tensor_tensor(out=r2[:], in0=x2, in1=cosb, op=ALU.mult), r=[("qkf",), ("mcs",)], w=[("r2",)])
                S.add("pool", lambda e: e.tensor_tensor(out=qktm[:, :, 8:16], in0=r1[:], in1=r2[:], op=ALU.add), r=[("r1",), ("r2",), ("qktm_c",)], w=[("qktm_b",)])
                qk_keys = [("qktm_a",), ("qktm_b",), ("qktm_c",)]
                if dbg == 102:
                    continue
                S.add("act", lambda e, t=t: e.activation(out=vaug[:, t, :, 0:64], in_=PS[2][:].rearrange("p (h d) -> p h d", h=8), func=AF.Copy),
                      r=[("ps", 2)], w=[("vaug", t)])
                if dbg == 103:
                    continue
                qf = qktm[:, 0:8, :].rearrange("p h d -> p (h d)")
                kf = qktm[:, 8:16, :].rearrange("p h d -> p (h d)")
                for c in range(4):
                    S.add("pe", lambda e, c=c, qf=qf: e.matmul(PS[3][:, c * 128:(c + 1) * 128], lhsT=qf[:, c * 128:(c + 1) * 128], rhs=ident, start=True, stop=True),
                          r=qk_keys + [("constb",)], w=[("ps", 3)])
                for c in range(4):
                    S.add("pe", lambda e, c=c, kf=kf: e.matmul(PS[4][:, c * 128:(c + 1) * 128], lhsT=kf[:, c * 128:(c + 1) * 128], rhs=ident, start=True, stop=True),
                          r=qk_keys + [("constb",)], w=[("ps", 4)])
                S.add("act", lambda e, tt=tt: e.activation(out=qTb[:, :, tt * 128:(tt + 1) * 128], in_=PS[3][:].rearrange("p (c q) -> p c q", c=4), func=AF.Copy),
                      r=[("ps", 3)], w=[("qTb", tt)])
                S.add("act", lambda e, tt=tt: e.activation(out=qz[0:64, :, 0, tt * 128:(tt + 1) * 128], in_=PS[3][0:64, :].rearrange("p (c q) -> p c q", c=4), func=AF.Copy),
                      r=[("ps", 3)], w=[("qz", tt)])
                S.add("act", lambda e, tt=tt: e.activation(out=qz[64:128, :, 1, tt * 128:(tt + 1) * 128], in_=PS[3][64:128, :].rearrange("p (c q) -> p c q", c=4), func=AF.Copy),
                      r=[("ps", 3)], w=[("qz", tt)])
                S.add("act", lambda e, t=t: e.activation(out=kTa[:, :, tls(t)], in_=PS[4][:].rearrange("p (c q) -> p c q", c=4), func=AF.Copy),
                      r=[("ps", 4)], w=[("kTa", t)])
            if dbg is not None and (dbg < 11 or (100 <= dbg < 120)):
                continue
            glvl = 4 if (dbg is None or dbg < 120) else dbg - 120
            if b >= 4 and not (dbg is not None and dbg < 12):
                for tt in range(2):
                    for c in range(4):
                        S.add("pe", lambda e, c=c, tt=tt: e.matmul(PS[2][:, c * 64:(c + 1) * 64], lhsT=qTb[:, c, tt * 128:(tt + 1) * 128],
                                                                rhs=ksb[:, c, :], start=True, stop=True),
                              r=[("qTb", tt), ("ksb",)], w=[("ps", 2)])
                    S.add("act", lambda e: e.activation(out=gs[:].rearrange("p (c j) n -> p c (j n)", c=4), in_=PS[2][:, 0:256].rearrange("p (c x) -> p c x", c=4)[:, :, 0:16], func=AF.Copy),
                          r=[("ps", 2)], w=[("gs",)])
                    if glvl < 2:
                        continue
                    gm = gs[:, :, 0:b].unsqueeze(2).to_broadcast([128, 8, b, b])
                    gn = gs[:, :, 0:b].unsqueeze(3).to_broadcast([128, 8, b, b])
                    S.add("dve", lambda e, gm=gm, gn=gn, b=b: e.tensor_tensor(out=cmp_[:, :, 0:b, 0:b], in0=gm, in1=gn, op=ALU.is_gt), r=[("gs",)], w=[("cmp",)])
                    S.add("dve", lambda e, b=b: e.tensor_reduce(out=cntt[:, :, 0:b], in_=cmp_[:, :, 0:b, 0:b], axis=AX.X, op=ALU.add), r=[("cmp",)], w=[("cnt",)])
                    S.add("dve", lambda e, b=b: e.tensor_scalar(out=negm[:, :, 0:b], in0=cntt[:, :, 0:b], scalar1=2.5, scalar2=NEG, op0=ALU.is_gt, op1=ALU.mult),
                          r=[("cnt",)], w=[("negm",)])
                    if glvl < 3:
                        continue
                    for h in range(8):
                        S.add("pe", lambda e, h=h: e.matmul(PSALL[0:32, h * 128:(h + 1) * 128], lhsT=negm[:, h, :], rhs=ident, start=True, stop=True),
                              r=[("negm",), ("constb",)], w=[("ps", h // 4)])
                    S.add("act", lambda e, tt=tt: e.activation(out=negT[0:8, :, tt * 128:(tt + 1) * 128], in_=PSALL[0:8, 0:1024].rearrange("p (h q) -> p h q", h=8), func=AF.Copy),
                          r=[("ps", 0), ("ps", 1)], w=[("negT",)])
            items = [(h, n) for h in range(8) for n in list(range(b)) + [None]]
            usemask_b = (b >= 4) and not (dbg is not None and dbg < 12) and glvl >= 4

            def emit_S(item, slot, b=b, usemask_b=usemask_b):
                h, n = item
                c, j = h // 2, h % 2
                st = PS[1 + slot]
                kst = ("ps", 1 + slot)
                ex = expP[slot]
                kex = ("expP", slot)
                qh = qz[:, c, j, :]
                rq = [("qz", 0), ("qz", 1)]
                if n is not None:
                    for jj in range(2):
                        kt = 2 * n + jj
                        so = st[:, jj * 256:(jj + 1) * 256]
                        S.add("pe", lambda e, kt=kt, so=so: e.matmul(so, lhsT=kTa[:, c, tls(kt)], rhs=qh, start=True, stop=(not usemask_b)),
                              r=[("kTa", kt)] + rq, w=[kst])
                        if usemask_b:
                            S.add("pe", lambda e, so=so: e.matmul(so, lhsT=E128[:, n, :], rhs=negT[:, h, :], start=False, stop=True),
                                  r=[("negT",), ("mcb",)], w=[kst])
                    S.add("act", lambda e: e.activation(out=ex[:], in_=st[:], func=AF.Exp, scale=0.125), r=[kst], w=[kex])
                else:
                    S.add("pe", lambda e: e.matmul(st[:, 0:256], lhsT=kTa[:, c, tls(2 * b)], rhs=qh, start=True, stop=True),
                          r=[("kTa", 2 * b)] + rq, w=[kst])
                    S.add("pe", lambda e: e.matmul(st[:, 384:512], lhsT=kTa[:, c, tls(2 * b + 1)], rhs=qh[:, 128:256], start=True, stop=True),
                          r=[("kTa", 2 * b + 1)] + rq, w=[kst])
                    S.add("act", lambda e: e.activation(out=ex[:, 0:256], in_=st[:, 0:256], func=AF.Exp, scale=0.125), r=[kst], w=[kex])
                    S.add("act", lambda e: e.activation(out=ex[:, 384:512], in_=st[:, 384:512], func=AF.Exp, scale=0.125), r=[kst], w=[kex])
                    S.add("pool", lambda e: e.tensor_tensor(out=ex[:, 0:128], in0=ex[:, 0:128], in1=tri01, op=ALU.mult), r=[kex, ("mcb",)], w=[kex])
                    S.add("pool", lambda e: e.tensor_tensor(out=ex[:, 384:512], in0=ex[:, 384:512], in1=tri01, op=ALU.mult), r=[kex, ("mcb",)], w=[kex])

            def emit_PV(item, slot, b=b):
                h, n = item
                hg = h // 4
                pos = [PS[4 + hg], PS[6 + hg]]
                okeys = [("ps", 4 + hg), ("ps", 6 + hg)]
                ex = expP[slot]
                kex = ("expP", slot)
                hs = slice((h % 4) * 66, (h % 4) * 66 + 66)
                if n is not None:
                    for jj in range(2):
                        kt = 2 * n + jj
                        for qt in range(2):
                            S.add("pe", lambda e, kt=kt, qt=qt, jj=jj: e.matmul(pos[qt][:, hs], lhsT=ex[:, jj * 256 + qt * 128:jj * 256 + (qt + 1) * 128],
                                                                              rhs=vaug[:, kt, h, :], start=(n == 0 and jj == 0), stop=False),
                                  r=[kex, ("vaug", kt)], w=[okeys[qt]])
                else:
                    S.add("pe", lambda e: e.matmul(pos[0][:, hs], lhsT=ex[:, 0:128], rhs=vaug[:, 2 * b, h, :], start=(b == 0), stop=True),
                          r=[kex, ("vaug", 2 * b)], w=[okeys[0]])
                    S.add("pe", lambda e: e.matmul(pos[1][:, hs], lhsT=ex[:, 128:256], rhs=vaug[:, 2 * b, h, :], start=(b == 0), stop=False),
                          r=[kex, ("vaug", 2 * b)], w=[okeys[1]])
                    S.add("pe", lambda e: e.matmul(pos[1][:, hs], lhsT=ex[:, 384:512], rhs=vaug[:, 2 * b + 1, h, :], start=False, stop=True),
                          r=[kex, ("vaug", 2 * b + 1)], w=[okeys[1]])
                    if h % 4 == 3:
                        for qt in range(2):
                            po = pos[qt][:, 0:264].rearrange("p (h d) -> p h d", h=4)
                            S.add("dve", lambda e, po=po, qt=qt: e.reciprocal(out=rec[:, qt, hg, :], in_=po[:, :, 64]), r=[okeys[qt]], w=[("rec", qt, hg)])
                            S.add("dve", lambda e, po=po, qt=qt: e.tensor_tensor(
                                out=ytm[:, qt, hg * 256:(hg + 1) * 256].rearrange("p (h d) -> p h d", h=4), in0=po[:, :, 0:64],
                                in1=rec[:, qt, hg, :].unsqueeze(2).to_broadcast([128, 4, 64]), op=ALU.mult),
                                r=[okeys[qt], ("rec", qt, hg)], w=[("ytm", qt, hg)])

            SKEW = 2
            for i in range(len(items) + SKEW):
                if i < len(items):
                    emit_S(items[i], (sti + i) % 3)
                if i >= SKEW:
                    emit_PV(items[i - SKEW], (sti + i - SKEW) % 3)
            sti += len(items)
            S.add("dve", lambda e, b=b: e.tensor_reduce(out=ksum[:, :, b], in_=kTa[:, :, b * 256:(b + 1) * 256], axis=AX.X, op=ALU.add),
                  r=[("kTa", 2 * b), ("kTa", 2 * b + 1)], w=[("ksum",)])
            S.add("act", lambda e: e.activation(out=ksb[0:64, :, 0:8], in_=ksum[0:64, :, :], func=AF.Copy), r=[("ksum",)], w=[("ksb",)])
            S.add("act", lambda e: e.activation(out=ksb[64:128, :, 8:16], in_=ksum[64:128, :, :], func=AF.Copy), r=[("ksum",)], w=[("ksb",)])
            for qt in range(2):
                for c in range(4):
                    S.add("pe", lambda e, c=c, qt=qt: e.matmul(PS[3][:, c * 128:(c + 1) * 128], lhsT=ytm[:, qt, c * 128:(c + 1) * 128], rhs=ident, start=True, stop=True),
                          r=[("ytm", qt, 0), ("ytm", qt, 1), ("constb",)], w=[("ps", 3)])
                S.add("act", lambda e, t=2 * b + qt: e.activation(out=ymobaT[:, :, tls(t)], in_=PS[3][:].rearrange("p (c q) -> p c q", c=4), func=AF.Copy),
                      r=[("ps", 3)], w=[("ymobaT", t // 4)])

    def pass_o1(s):
        RP = carve(0, [128, KC, D], BF16)
        WGA = carve(16 * K, [128, KC, D], BF16)
        WGB = carve(32 * K, [128, KC, D], BF16)
        yretT = carve(48 * K, [128, KC, T], BF16)
        ymobaT = carve(80 * K, [128, 4, T], BF16)
        MP = carve(96 * K, [128, 4, D], BF16)
        xblk = carve(104 * K, [128, KC, 512], BF16)
        ufin = carve(112 * K, [128, KC, 512], BF16)
        sga = carve(120 * K, [128, 512], F32)
        sgb = carve(122 * K, [128, 512], F32)
        u1 = carve(124 * K, [128, 512], F32)
        u2 = carve(126 * K, [128, 512], F32)
        cast_dma_cols(RP, rp_d, 0, D, "rp", step=D)
        cast_dma_cols(WGA, win_d, 4608, 5632, "wga", step=D)
        cast_dma_cols(WGB, win_d, 5632, 6656, "wgb", step=D)
        cast_dma_cols(MP, mp_d, 0, D, "mp", step=D)
        for tb in range(4):
            sl = tbs(tb)
            S.add("act", lambda e, sl=sl: e.activation(out=xblk[:], in_=R[:, :, sl], func=AF.Copy), r=Rkeys(tb), w=[("xblk",)])
            for m in range(KC):
                o4 = 4 * (m % 2)
                ms = slice(m * 128, (m + 1) * 128)
                for kc in range(KC):
                    S.add("pe", lambda e, kc=kc, ms=ms, o4=o4, sl=sl: e.matmul(PS[o4][:], lhsT=RP[:, kc, ms], rhs=yretT[:, kc, sl], start=(kc == 0), stop=(kc == KC - 1)),
                          r=[("rp", 0), ("yretT", tb)], w=[("ps", o4)])
                for kc in range(KC):
                    S.add("pe", lambda e, kc=kc, ms=ms, o4=o4: e.matmul(PS[o4 + 1][:], lhsT=WGA[:, kc, ms], rhs=xblk[:, kc, :], start=(kc == 0), stop=(kc == KC - 1)),
                          r=[("wga", 0), ("xblk",)], w=[("ps", o4 + 1)])
                for c in range(4):
                    S.add("pe", lambda e, c=c, ms=ms, o4=o4, sl=sl: e.matmul(PS[o4 + 2][:], lhsT=MP[:, c, ms], rhs=ymobaT[:, c, sl], start=(c == 0), stop=(c == 3)),
                          r=[("mp", 0), ("ymobaT", tb)], w=[("ps", o4 + 2)])
                for kc in range(KC):
                    S.add("pe", lambda e, kc=kc, ms=ms, o4=o4: e.matmul(PS[o4 + 3][:], lhsT=WGB[:, kc, ms], rhs=xblk[:, kc, :], start=(kc == 0), stop=(kc == KC - 1)),
                          r=[("wgb", 0), ("xblk",)], w=[("ps", o4 + 3)])
                S.add("act", lambda e, o4=o4: e.activation(out=sga[:], in_=PS[o4 + 1][:], func=AF.Sigmoid), r=[("ps", o4 + 1)], w=[("sga",)])
                S.add("act", lambda e, o4=o4: e.activation(out=sgb[:], in_=PS[o4 + 3][:], func=AF.Sigmoid), r=[("ps", o4 + 3)], w=[("sgb",)])
                S.add("dve", lambda e, o4=o4: e.tensor_tensor(out=u1[:], in0=PS[o4][:], in1=sga[:], op=ALU.mult), r=[("ps", o4), ("sga",)], w=[("u1",)])
                S.add("dve", lambda e, o4=o4: e.tensor_tensor(out=u2[:], in0=PS[o4 + 2][:], in1=sgb[:], op=ALU.mult), r=[("ps", o4 + 2), ("sgb",)], w=[("u2",)])
                S.add("pool", lambda e, m=m: e.tensor_tensor(out=ufin[:, m, :], in0=u1[:], in1=u2[:], op=ALU.add), r=[("u1",), ("u2",)], w=[("ufin",)])
            S.add("act", lambda e, sl=sl: e.activation(out=yretT[:, :, sl], in_=ufin[:], func=AF.Copy), r=[("ufin",)], w=[("yretT", tb)])

    def pass_o2(s):
        WO = carve(0, [128, KC, D], BF16)
        U = carve(48 * K, [128, KC, T], BF16)
        cast_dma_cols(WO, wo_d, 0, D, "wo", step=D)
        i2 = 0
        for tb in range(4):
            sl = tbs(tb)
            for m in range(KC):
                pb = i2 % 2
                i2 += 1
                ms = slice(m * 128, (m + 1) * 128)
                for kc in range(KC):
                    S.add("pe", lambda e, kc=kc, ms=ms, pb=pb, sl=sl: e.matmul(PS[pb][:], lhsT=WO[:, kc, ms], rhs=U[:, kc, sl], start=(kc == 0), stop=(kc == KC - 1)),
                          r=[("wo", 0), ("yretT", tb)], w=[("ps", pb)])
                S.add("dve", lambda e, m=m, pb=pb, sl=sl: e.scalar_tensor_tensor(out=R[:, m, sl], in0=R[:, m, sl], scalar=ALPHA, in1=PS[pb][:], op0=ALU.mult, op1=ALU.add),
                      r=[("ps", pb), ("R", tb, m)], w=[("R", tb, m)])
            layer_norm(tb, 1)

    def mixer_phase(s):
        if "r" in mixp:
            pass_ret(s)
            S.barrier()
        if "m" in mixp:
            pass_moba(s)
            S.barrier()
        if "o1" in mixp:
            pass_o1(s)
            S.barrier()
        if "o2" in mixp:
            pass_o2(s)

    for s in range(nseq):
        if "load" in phases:
            for tb in range(4):
                S.add("sp", lambda e, tb=tb: e.dma_start(out=R[:, :, tbs(tb)], in_=xT[:, :, s * T + tb * 512: s * T + (tb + 1) * 512]),
                      w=Rkeys(tb), dsem=("xin", tb))
            S.barrier()
        if "ffn1" in phases:
            ffn_phase(s, 0)
            S.barrier()
        if "mix" in phases:
            mixer_phase(s)
            S.barrier()
        if "ffn2" in phases:
            ffn_phase(s, 1)
            S.barrier()
        out_phase(s)
    S.add("sp", lambda e: e.nop(), r=[k for tb in range(4) for k in Rkeys(tb)], w=[k for tb in range(4) for k in Rkeys(tb)])

    engines = {
        "pe": ("tensor", nc.tensor),
        "act": ("scalar", nc.scalar),
        "dve": ("vector", nc.vector),
        "pool": ("gpsimd", nc.gpsimd),
        "sp": ("sync", nc.sync),
    }
    S.emit(nc, engines)
    return nc, S


def _feat_major(v):
    return np.ascontiguousarray(np.asarray(v, np.float32).reshape(KC, 128).T)


def _module_consts():
    c = {}
    pos = np.arange(T, dtype=np.float32)
    inv = (1.0 / (np.float32(10000.0) ** np.linspace(0.0, 1.0, 64, dtype=np.float32))).astype(np.float32)
    ang = (pos[:, None] * inv[None, :]).astype(np.float32)
    rcs = np.stack([np.cos(ang), np.sin(ang)], axis=1).astype(np.float32)
    c["rcs"] = np.ascontiguousarray(rcs.reshape(16, 128, 2, 64).transpose(1, 0, 2, 3))
    inv2 = (1.0 / (np.float32(500000.0) ** (np.arange(0, 16, 2, dtype=np.float32) / np.float32(16)))).astype(np.float32)
    ang2 = (pos[:, None] * inv2[None, :]).astype(np.float32)
    mcs = np.stack([np.cos(ang2), np.sin(ang2)], axis=1).astype(np.float32)
    c["mcs"] = np.ascontiguousarray(mcs.reshape(16, 128, 2, 8).transpose(1, 0, 2, 3))
    g = 1.0 - 2.0 ** (-5.0 - np.arange(4, dtype=np.float64))
    idx = np.arange(128, dtype=np.float64)
    sc = 128.0 ** -0.5
    rdec = np.zeros((128, 1032), np.float32)
    for h in range(4):
        diff = idx[None, :] - idx[:, None]
        dt = np.where(diff >= 0, g[h] ** np.maximum(diff, 0.0), 0.0) * sc
        rdec[:, h * 128:(h + 1) * 128] = dt
        rdec[:, 512 + h * 128:512 + (h + 1) * 128] = (g[h] ** (idx + 1.0))[None, :]
        rdec[:, 1024 + h] = g[h] ** (127.0 - idx) * sc
    c["rdec"] = rdec
    mcb = np.zeros((128, 1152), np.float32)
    mcb[:, 0:128] = (idx[:, None] <= idx[None, :]).astype(np.float32)
    for n in range(8):
        mcb[n, 128 + n * 128:128 + (n + 1) * 128] = 1.0
    c["mconstb"] = mcb.astype(ml_dtypes.bfloat16)
    return c


def prep_shared(inp):
    sh = {}
    for f, (gu, dn) in enumerate([("ffn1_w_gu", "ffn1_w_down"), ("ffn2_w_gu", "ffn2_w_down")]):
        w = np.asarray(inp[gu], np.float32)[0]
        w = w.reshape(KC, 128, 2, NJ, 128)
        sh["wgu%d" % (f + 1)] = np.ascontiguousarray(w.transpose(3, 1, 2, 0, 4))
        sh["wd%d" % (f + 1)] = np.ascontiguousarray(np.asarray(inp[dn], np.float32)[0].reshape(NJ, 128, D))
    lnp = np.stack([_feat_major(inp[k][0]) for k in ("ln1_g", "ln1_b", "lnm_g", "lnm_b", "ln2_g", "ln2_b")], axis=1)
    sh["lnp"] = np.ascontiguousarray(lnp)
    def kmajor(w, nk):
        w = np.asarray(w, np.float32)
        return np.ascontiguousarray(w.reshape(nk, 128, w.shape[-1]).transpose(1, 0, 2))
    sh["win"] = kmajor(inp["w_in"][0], KC)
    sh["retp"] = kmajor(inp["ret_proj"][0], KC)
    sh["mobp"] = kmajor(inp["moba_proj"][0], 4)
    sh["wout"] = kmajor(inp["w_out"][0], KC)
    sh.update(_module_consts())
    cb = np.zeros((128, 256), np.float32)
    cb[:, 0:128] = np.eye(128, dtype=np.float32)
    cb[:, 128:256] = 1.0 / 1024.0
    sh["constb"] = cb.astype(ml_dtypes.bfloat16)
    return sh


def shard_x(x, nseq=SEQ_PER_CORE, ncores=NCORES):
    x = np.asarray(x, np.float32)
    maps = []
    for c in range(ncores):
        xc = x[c * nseq:(c + 1) * nseq].reshape(nseq * T, KC, 128)
        maps.append(np.ascontiguousarray(xc.transpose(2, 1, 0)))
    return maps


def unshard(outs, nseq=SEQ_PER_CORE):
    res = []
    for o in outs:
        res.append(np.asarray(o, np.float32).transpose(2, 1, 0).reshape(nseq, T, D))
    return np.ascontiguousarray(np.concatenate(res, axis=0))


_CACHE = {}


def kernel(**inputs):
    if "nc" not in _CACHE:
        _CACHE["nc"] = build_program()[0]
    nc = _CACHE["nc"]
    sh = prep_shared(inputs)
    xs = shard_x(inputs["x"])
    in_maps = []
    for c in range(NCORES):
        m = dict(sh)
        m["xT"] = xs[c]
        in_maps.append(m)
    res = run_bass_kernel_spmd(nc, in_maps, core_ids=list(range(NCORES)))
    return unshard([r["outT"] for r in res.results])
```

```python
import math
import numpy as np
import ml_dtypes
import concourse.bass as bass
import concourse.mybir as mybir
from concourse.bass_utils import run_bass_kernel_spmd

F32 = mybir.dt.float32
BF16 = mybir.dt.bfloat16
AF = mybir.ActivationFunctionType
ALU = mybir.AluOpType
AX = mybir.AxisListType

D = 1024
T = 2048
KC = 8
DFF = 2816
NJ = 22
NCORES = 8
SEQ_PER_CORE = 2
ALPHA = 2.0 ** 0.25
LN_EPS = 1e-5
GN_EPS = 1e-5
WIN = 6656
GROUPS = [(0, 8), (8, 15), (15, 22)]
NEG = -30000.0


class Op:
    __slots__ = ("eng", "fn", "deps", "dsem", "sig", "cnt")

    def __init__(self, eng, fn, deps, dsem):
        self.eng = eng
        self.fn = fn
        self.deps = deps
        self.dsem = dsem
        self.sig = dsem is not None
        self.cnt = 0


class Sched:
    def __init__(self):
        self.ops = []
        self.lastw = {}
        self.readers = {}
        self.bar = set()
        self.last_stream = {}

    def add(self, eng, fn, r=(), w=(), dsem=None):
        i = len(self.ops)
        deps = set(self.bar)
        for k in r:
            j = self.lastw.get(k)
            if j is not None:
                deps.add(j)
        for k in w:
            j = self.lastw.get(k)
            if j is not None:
                deps.add(j)
            rd = self.readers.get(k)
            if rd:
                deps.update(rd.values())
        stream = ("dma", dsem) if dsem is not None else eng
        for k in r:
            self.readers.setdefault(k, {})[stream] = i
        for k in w:
            self.lastw[k] = i
            self.readers[k] = {}
        self.last_stream[stream] = i
        self.ops.append(Op(eng, fn, deps, dsem))
        return i

    def barrier(self):
        self.bar = set(self.last_stream.values())

    def emit(self, nc, engines):
        ops = self.ops

        def stream_of(o):
            return ("dma", o.dsem) if o.dsem is not None else o.eng

        for i, o in enumerate(ops):
            best = {}
            for j in o.deps:
                p = ops[j]
                st = stream_of(p)
                if p.dsem is None and p.eng == "pe" and o.eng == "pe" and o.dsem is None:
                    continue
                if st not in best or best[st] < j:
                    best[st] = j
            o.deps = best
            for j in best.values():
                ops[j].sig = True
        cnt = {}
        for o in ops:
            if o.sig:
                st = stream_of(o)
                inc = 16 if o.dsem is not None else 1
                cnt[st] = cnt.get(st, 0) + inc
                o.cnt = cnt[st]
        sems = {}
        import contextlib
        stack = contextlib.ExitStack()
        for k, st in enumerate(cnt.keys()):
            sems[st] = stack.enter_context(nc.semaphore("s%d" % k))
        self.max_counts = dict(cnt)
        with stack:
            with nc.Block() as block:
                for ename, (deco, _h) in engines.items():
                    my = [(i, o) for i, o in enumerate(ops) if o.eng == ename]
                    if not my:
                        continue

                    def body(eng, my=my, ename=ename):
                        waited = {}
                        for i, o in my:
                            for st, j in o.deps.items():
                                v = ops[j].cnt
                                if waited.get(st, 0) < v:
                                    eng.wait_ge(sems[st], v)
                                    waited[st] = v
                            ins = o.fn(eng)
                            if o.sig:
                                st = stream_of(o)
                                ins.then_inc(sems[st], 16 if o.dsem is not None else 1)
                                if o.dsem is None:
                                    pass
                    getattr(block, deco)(body)


def _tile(nc, name, shape, dt):
    return nc.sbuf_tensor(name, list(shape), dt).__enter__()


def _ptile(nc, name, shape, dt):
    return nc.psum_tensor(name, list(shape), dt).__enter__()


def build_program(nseq=SEQ_PER_CORE, phases=("ffn1", "mix", "ffn2"), dbg=None, mixp=("r", "m", "o1", "o2")):
    nc = bass.Bass("TRN2", target_bir_lowering=False)
    S = Sched()
    NT = nseq * T

    def din(name, shape, dt=F32):
        return nc.dram_tensor(name, list(shape), dt, kind="ExternalInput").ap()

    xT = din("xT", [128, KC, NT])
    outT = nc.dram_tensor("outT", [128, KC, NT], F32, kind="ExternalOutput").ap()
    wgu_d = [din("wgu1", [NJ, 128, 2, KC, 128]), din("wgu2", [NJ, 128, 2, KC, 128])]
    wd_d = [din("wd1", [NJ, 128, D]), din("wd2", [NJ, 128, D])]
    lnp_d = din("lnp", [128, 6, KC])
    cb_d = din("constb", [128, 256], BF16)
    win_d = din("win", [128, KC, WIN])
    rp_d = din("retp", [128, KC, D])
    mp_d = din("mobp", [128, 4, D])
    wo_d = din("wout", [128, KC, D])
    rcs_d = din("rcs", [128, 16, 2, 64])
    mcs_d = din("mcs", [128, 16, 2, 8])
    rdec_d = din("rdec", [128, 1032])
    mcb_d = din("mconstb", [128, 1152], BF16)

    R = _tile(nc, "R", [128, KC, T], F32)
    ARENA = _tile(nc, "ARENA", [128, 73216], BF16)
    lnp = _tile(nc, "lnp_sb", [128, 6, KC], F32)
    constb = _tile(nc, "constb_sb", [128, 256], BF16)
    ident = constb[:, 0:128]
    ones_div = constb[:, 128:256]
    epsc = _tile(nc, "epsc", [128, 2], F32)

    def carve(off_bytes, shape, dt):
        n = int(np.prod(shape[1:]))
        if dt == BF16:
            a = ARENA[:, off_bytes // 2: off_bytes // 2 + n]
        else:
            a = ARENA[:, off_bytes // 2: off_bytes // 2 + 2 * n].bitcast(F32)
        if len(shape) == 2:
            return a
        names = " ".join("d%d" % i for i in range(1, len(shape)))
        kw = {"d%d" % i: shape[i] for i in range(1, len(shape))}
        return a.rearrange("p (%s) -> p %s" % (names, names), **kw)

    K = 1024
    Xb = carve(0, [128, KC, T], BF16)
    actT = carve(32 * K, [128, 8, T], BF16)
    Wgu = [carve(64 * K + 4 * K * b, [128, 2, KC, 128], BF16) for b in range(3)]
    Wd = carve(76 * K, [128, 8, D], BF16)
    zb = carve(92 * K, [128, KC, 512], BF16)
    zsq = carve(100 * K, [128, KC, 512], BF16)
    mean_sb = carve(108 * K, [128, 512], F32)
    rstd_sb = carve(110 * K, [128, 512], F32)
    var_sb = carve(112 * K, [128, 512], F32)
    sg = [carve(114 * K + 1 * K * b, [128, 512], BF16) for b in range(2)]

    PSALL = _ptile(nc, "psall", [128, 4096], F32)
    PS = [PSALL[:, b * 512:(b + 1) * 512] for b in range(8)]

    def PSB(b):
        return PSALL[:, b * 512:(b + 1) * 512].bitcast(BF16)

    S.add("sp", lambda e: e.dma_start(out=lnp[:], in_=lnp_d), w=[("lnp",)], dsem="c_lnp")
    S.add("sp", lambda e: e.dma_start(out=constb[:], in_=cb_d), w=[("constb",)], dsem="c_cb")
    S.add("dve", lambda e: e.memset(epsc[:, 0:1], LN_EPS), w=[("epsc",)])
    S.add("dve", lambda e: e.memset(epsc[:, 1:2], GN_EPS), w=[("epsc",)])

    cnt = {"wgu": 0}

    def tbs(tb):
        return slice(tb * 512, (tb + 1) * 512)

    def Rkeys(tb):
        return [("R", tb, kc) for kc in range(KC)]

    def layer_norm(tb, gi):
        sl = tbs(tb)
        Rb = R[:, :, sl]
        S.add("dve", lambda e: e.tensor_copy(out=zb[:], in_=Rb), r=Rkeys(tb), w=[("zb",)])
        S.add("act", lambda e: e.activation(out=zsq[:], in_=Rb, func=AF.Square), r=Rkeys(tb), w=[("zsq",)])
        pm, pq = PS[6], PS[7]
        for kc in range(KC):
            S.add("pe", lambda e, kc=kc: e.matmul(pm[:], lhsT=ones_div, rhs=zb[:, kc, :], start=(kc == 0), stop=(kc == KC - 1)),
                  r=[("zb",), ("constb",)], w=[("ps", 6)])
        for kc in range(KC):
            S.add("pe", lambda e, kc=kc: e.matmul(pq[:], lhsT=ones_div, rhs=zsq[:, kc, :], start=(kc == 0), stop=(kc == KC - 1)),
                  r=[("zsq",), ("constb",)], w=[("ps", 7)])
        S.add("act", lambda e: e.activation(out=mean_sb[:], in_=pm[:], func=AF.Copy), r=[("ps", 6)], w=[("mean",)])
        S.add("act", lambda e: e.activation(out=var_sb[:], in_=pm[:], func=AF.Square), r=[("ps", 6)], w=[("var",)])
        S.add("dve", lambda e: e.tensor_tensor(out=var_sb[:], in0=pq[:], in1=var_sb[:], op=ALU.subtract),
              r=[("ps", 7), ("var",)], w=[("var",)])
        S.add("act", lambda e: e.activation(out=var_sb[:], in_=var_sb[:], func=AF.Sqrt, bias=epsc[:, 0:1]),
              r=[("var",), ("epsc",)], w=[("var",)])
        S.add("dve", lambda e: e.reciprocal(out=rstd_sb[:], in_=var_sb[:]), r=[("var",)], w=[("rstd",)])
        mb = mean_sb[:].unsqueeze(1).to_broadcast([128, KC, 512])
        rb = rstd_sb[:].unsqueeze(1).to_broadcast([128, KC, 512])
        S.add("dve", lambda e: e.tensor_tensor(out=Rb, in0=Rb, in1=mb, op=ALU.subtract),
              r=Rkeys(tb) + [("mean",)], w=Rkeys(tb))
        S.add("dve", lambda e: e.tensor_tensor(out=Rb, in0=Rb, in1=rb, op=ALU.mult),
              r=Rkeys(tb) + [("rstd",)], w=Rkeys(tb))
        for kc in range(KC):
            S.add("act", lambda e, kc=kc: e.activation(out=R[:, kc, sl], in_=R[:, kc, sl], func=AF.Identity,
                                                      scale=lnp[:, 2 * gi, kc:kc + 1], bias=lnp[:, 2 * gi + 1, kc:kc + 1]),
                  r=[("R", tb, kc), ("lnp",)], w=[("R", tb, kc)])

    def ffn_phase(s, f):
        gi = 0 if f == 0 else 2
        if f == 0:
            for tb in range(4):
                S.add("sp", lambda e, tb=tb: e.dma_start(out=R[:, :, tbs(tb)], in_=xT[:, :, s * T + tb * 512: s * T + (tb + 1) * 512]),
                      w=Rkeys(tb), dsem=("xin", tb))
        for tb in range(4):
            S.add("dve", lambda e, tb=tb: e.tensor_copy(out=Xb[:, :, tbs(tb)], in_=R[:, :, tbs(tb)]), r=Rkeys(tb), w=[("xb", tb)])
            S.add("act", lambda e, tb=tb: e.activation(out=R[:, :, tbs(tb)], in_=R[:, :, tbs(tb)], func=AF.Copy, scale=ALPHA),
                  r=Rkeys(tb), w=Rkeys(tb))
        it = 0
        for (j0, j1) in GROUPS:
            G = j1 - j0
            for jl in range(G):
                S.add("pool", lambda e, jl=jl, j=j0 + jl: e.dma_start(out=Wd[:, jl, :], in_=wd_d[f][j]),
                      w=[("wd", jl)], dsem=("wd", jl))
            for jl in range(G):
                j = j0 + jl
                b = cnt["wgu"] % 3
                cnt["wgu"] += 1
                S.add("pool", lambda e, b=b, j=j: e.dma_start(out=Wgu[b][:], in_=wgu_d[f][j]), w=[("wgu", b)], dsem=("wgu", b))
                for tb in range(4):
                    pg, pu = PS[it % 2], PS[2 + it % 2]
                    kg, ku, ks = ("ps", it % 2), ("ps", 2 + it % 2), ("sg", it % 2)
                    sgt = sg[it % 2]
                    it += 1
                    for kc in range(KC):
                        S.add("pe", lambda e, pg=pg, b=b, kc=kc, tb=tb: e.matmul(pg[:], lhsT=Wgu[b][:, 0, kc, :], rhs=Xb[:, kc, tbs(tb)],
                                                                                start=(kc == 0), stop=(kc == KC - 1)),
                              r=[("wgu", b), ("xb", tb)], w=[kg])
                    for kc in range(KC):
                        S.add("pe", lambda e, pu=pu, b=b, kc=kc, tb=tb: e.matmul(pu[:], lhsT=Wgu[b][:, 1, kc, :], rhs=Xb[:, kc, tbs(tb)],
                                                                                start=(kc == 0), stop=(kc == KC - 1)),
                              r=[("wgu", b), ("xb", tb)], w=[ku])
                    S.add("act", lambda e, pg=pg, sgt=sgt: e.activation(out=sgt[:], in_=pg[:], func=AF.Silu), r=[kg], w=[ks])
                    S.add("dve", lambda e, pu=pu, sgt=sgt, jl=jl, tb=tb: e.tensor_tensor(out=actT[:, jl, tbs(tb)], in0=pu[:], in1=sgt[:], op=ALU.mult),
                          r=[ku, ks], w=[("actT", jl, tb)])
            lastg = (j1 == NJ)
            order = [(m, tb) for tb in range(4) for m in range(KC)] if lastg else [(m, tb) for m in range(KC) for tb in range(4)]
            i2 = 0
            for (m, tb) in order:
                pd = PS[4 + i2 % 2]
                kd = ("ps", 4 + i2 % 2)
                i2 += 1
                for jl in range(G):
                    S.add("pe", lambda e, pd=pd, jl=jl, m=m, tb=tb, G=G: e.matmul(pd[:], lhsT=Wd[:, jl, m * 128:(m + 1) * 128], rhs=actT[:, jl, tbs(tb)],
                                                                               start=(jl == 0), stop=(jl == G - 1)),
                          r=[("wd", jl), ("actT", jl, tb)], w=[kd])
                S.add("dve", lambda e, pd=pd, m=m, tb=tb: e.scalar_tensor_tensor(out=R[:, m, tbs(tb)], in0=pd[:], scalar=0.5, in1=R[:, m, tbs(tb)],
                                                                               op0=ALU.mult, op1=ALU.add),
                      r=[kd, ("R", tb, m)], w=[("R", tb, m)])
                if lastg and m == KC - 1:
                    layer_norm(tb, gi)

    def out_phase(s):
        for tb in range(4):
            S.add("sp", lambda e, tb=tb: e.dma_start(out=outT[:, :, s * T + tb * 512: s * T + (tb + 1) * 512], in_=R[:, :, tbs(tb)]),
                  r=Rkeys(tb), dsem=("xout", tb))


    G_H = [1.0 - 2.0 ** (-5.0 - h) for h in range(4)]
    GC = [g ** 128 for g in G_H]

    def tls(t):
        return slice(t * 128, (t + 1) * 128)

    def cast_dma_cols(dst, src_d, c0, c1, keybase, step=1536):
        for pi, a in enumerate(range(c0, c1, step)):
            b_ = min(a + step, c1)
            k = (keybase, pi)
            S.add("pool", lambda e, a=a, b_=b_: e.dma_start(out=dst[:, :, a - c0:b_ - c0], in_=src_d[:, :, a:b_]), w=[k], dsem=k)

    def xbt_cast(dst, t, key):
        S.add("act", lambda e: e.activation(out=dst[:], in_=R[:, :, tls(t)], func=AF.Copy),
              r=[("R", t // 4, kc) for kc in range(KC)], w=[key])

    def pass_ret(s):
        Wr = carve(0, [128, KC, 3072], BF16)
        yretT = carve(48 * K, [128, KC, T], BF16)
        o = 80 * K
        rcs = carve(o, [128, 16, 2, 64], F32); o += 8 * K
        rdec = carve(o, [128, 1032], F32); o += 4128
        DT = rdec[:, 0:512]
        GQ = rdec[:, 512:1024]
        gk = rdec[:, 1024:1028]
        xbt = [carve(o + 2 * K * i, [128, KC, 128], BF16) for i in range(2)]; o += 4 * K
        qkr = carve(o, [128, 8, 2, 64], BF16); o += 2 * K
        tA = carve(o, [128, 8, 2, 64], F32); o += 4 * K
        tB = carve(o, [128, 8, 64], F32); o += 2 * K
        tC = carve(o, [128, 8, 64], F32); o += 2 * K
        kdec2 = [carve(o + 1 * K * i, [128, 4, 128], BF16) for i in range(2)]; o += 2 * K
        vbf2 = [carve(o + 2 * K * i, [128, 1024], BF16) for i in range(2)]; o += 4 * K
        srg2 = [carve(o + 4 * K * i, [128, 1024], F32) for i in range(2)]; o += 8 * K
        qT2 = [carve(o + 1 * K * i, [128, 4, 128], BF16) for i in range(2)]; o += 2 * K
        qdT2 = [carve(o + 1 * K * i, [128, 4, 128], BF16) for i in range(2)]; o += 2 * K
        kT2 = [carve(o + 1 * K * i, [128, 4, 128], BF16) for i in range(2)]; o += 2 * K
        sT = carve(o, [128, 4, 128], BF16); o += 1 * K
        state = carve(o, [128, 4, 256], F32); o += 4 * K
        stbf = carve(o, [128, 4, 256], BF16); o += 2 * K
        stats = carve(o, [128, 4, 6], F32); o += 128
        mv = carve(o, [128, 4, 2], F32); o += 64
        rs = carve(o, [128, 4], F32); o += 64
        yn = carve(o, [128, 1024], F32); o += 4 * K
        yrt2 = [carve(o + 2 * K * i, [128, 1024], BF16) for i in range(2)]; o += 4 * K
        assert o <= 143 * K, o

        cast_dma_cols(Wr, win_d, 0, 3072, "wr")
        S.add("sp", lambda e: e.dma_start(out=rcs[:], in_=rcs_d), w=[("rcs",)], dsem="c_rcs")
        S.add("sp", lambda e: e.dma_start(out=rdec[:], in_=rdec_d), w=[("rdec",)], dsem="c_rdec")
        S.add("dve", lambda e: e.memset(state[:].rearrange("p h e -> p (h e)"), 0.0), w=[("state",)])
        S.add("dve", lambda e: e.memset(stbf[:].rearrange("p h e -> p (h e)"), 0.0), w=[("stbf",)])

        def stage_A(t):
            i = t % 2
            xb = xbt[i]
            kx = ("xbt", i)
            kdec, vbf, srg, qT, qdT, kT = kdec2[i], vbf2[i], srg2[i], qT2[i], qdT2[i], kT2[i]
            xbt_cast(xb, t, kx)
            for cb in range(6):
                for kc in range(KC):
                    S.add("pe", lambda e, cb=cb, kc=kc: e.matmul(PS[cb][:], lhsT=xb[:, kc, :], rhs=Wr[:, kc, cb * 512:(cb + 1) * 512],
                                                              start=(kc == 0), stop=(kc == KC - 1)),
                          r=[kx, ("wr", cb // 3)], w=[("ps", cb)])
            cos16 = rcs[:, t, 0, :].unsqueeze(1).to_broadcast([128, 16, 64])
            sin8 = rcs[:, t, 1, :].unsqueeze(1).to_broadcast([128, 8, 64])
            P16 = PSALL[:, 0:1024].rearrange("p (g i) -> p g i", g=16, i=64)
            P8 = PSALL[:, 0:1024].rearrange("p (g f i) -> p g f i", g=8, f=2, i=64)
            tA16 = tA[:].rearrange("p g f i -> p (g f) i")
            pk = [("ps", 0), ("ps", 1)]
            S.add("dve", lambda e: e.tensor_tensor(out=tA16, in0=P16, in1=cos16, op=ALU.mult), r=pk + [("rcs",)], w=[("tA",)])
            S.add("dve", lambda e: e.tensor_tensor(out=tB[:], in0=P8[:, :, 1, :], in1=sin8, op=ALU.mult), r=pk + [("rcs",)], w=[("tB",)])
            S.add("dve", lambda e: e.tensor_tensor(out=tC[:], in0=P8[:, :, 0, :], in1=sin8, op=ALU.mult), r=pk + [("rcs",)], w=[("tC",)])
            S.add("pool", lambda e: e.tensor_tensor(out=qkr[:, :, 0, :], in0=tA[:, :, 0, :], in1=tB[:], op=ALU.subtract),
                  r=[("tA",), ("tB",)], w=[("qkr0",)])
            S.add("pool", lambda e: e.tensor_tensor(out=qkr[:, :, 1, :], in0=tA[:, :, 1, :], in1=tC[:], op=ALU.add),
                  r=[("tA",), ("tC",)], w=[("qkr1",)])
            qk_keys = [("qkr0",), ("qkr1",)]
            qflat = qkr[:, 0:4].rearrange("p h f i -> p h (f i)")
            kflat = qkr[:, 4:8].rearrange("p h f i -> p h (f i)")
            S.add("pool", lambda e: e.tensor_tensor(out=kdec[:], in0=kflat, in1=gk.unsqueeze(2).to_broadcast([128, 4, 128]), op=ALU.mult),
                  r=qk_keys + [("rdec",)], w=[("kdec", i)])
            S.add("act", lambda e: e.activation(out=vbf[:], in_=PSALL[:, 1024:2048], func=AF.Copy), r=[("ps", 2), ("ps", 3)], w=[("vbf", i)])
            S.add("act", lambda e: e.activation(out=srg[:], in_=PSALL[:, 2048:3072], func=AF.Silu), r=[("ps", 4), ("ps", 5)], w=[("srg", i)])
            for h in range(4):
                S.add("pe", lambda e, h=h: e.matmul(PS[6][:, h * 128:(h + 1) * 128], lhsT=qflat[:, h, :], rhs=ident, start=True, stop=True),
                      r=qk_keys + [("constb",)], w=[("ps", 6)])
            for h in range(4):
                S.add("pe", lambda e, h=h: e.matmul(PS[7][:, h * 128:(h + 1) * 128], lhsT=kflat[:, h, :], rhs=ident, start=True, stop=True),
                      r=qk_keys + [("constb",)], w=[("ps", 7)])
            S.add("act", lambda e: e.activation(out=qT[:].rearrange("p h c -> p (h c)"), in_=PS[6][:], func=AF.Copy), r=[("ps", 6)], w=[("qT", i)])
            S.add("dve", lambda e: e.tensor_tensor(out=qdT[:].rearrange("p h c -> p (h c)"), in0=PS[6][:], in1=GQ, op=ALU.mult),
                  r=[("ps", 6), ("rdec",)], w=[("qdT", i)])
            S.add("act", lambda e: e.activation(out=kT[:].rearrange("p h c -> p (h c)"), in_=PS[7][:], func=AF.Copy), r=[("ps", 7)], w=[("kT", i)])

        def stage_B1(t):
            i = t % 2
            kdec, vbf, srg, qT, qdT, kT, yrt = kdec2[i], vbf2[i], srg2[i], qT2[i], qdT2[i], kT2[i], yrt2[i]
            for h in range(4):
                S.add("pe", lambda e, h=h: e.matmul(PS[6][:, h * 128:(h + 1) * 128], lhsT=kT[:, h, :], rhs=qT[:, h, :], start=True, stop=True),
                      r=[("kT", i), ("qT", i)], w=[("ps", 6)])
            S.add("dve", lambda e: e.tensor_tensor(out=sT[:].rearrange("p h c -> p (h c)"), in0=PS[6][:], in1=DT, op=ALU.mult),
                  r=[("ps", 6), ("rdec",)], w=[("sT",)])
            for h in range(4):
                po = PSALL[:, 2048 + h * 256:2048 + (h + 1) * 256]
                S.add("pe", lambda e, h=h, po=po: e.matmul(po, lhsT=sT[:, h, :], rhs=vbf[:, h * 256:(h + 1) * 256], start=True, stop=False),
                      r=[("sT",), ("vbf", i)], w=[("ps", 4 + h // 2)])
                S.add("pe", lambda e, h=h, po=po: e.matmul(po, lhsT=qdT[:, h, :], rhs=stbf[:, h, :], start=False, stop=True),
                      r=[("qdT", i), ("stbf",)], w=[("ps", 4 + h // 2)])
            for h in range(4):
                pkv = PSALL[:, 1024 + h * 256:1024 + (h + 1) * 256]
                S.add("pe", lambda e, h=h, pkv=pkv: e.matmul(pkv, lhsT=kdec[:, h, :], rhs=vbf[:, h * 256:(h + 1) * 256], start=True, stop=True),
                      r=[("kdec", i), ("vbf", i)], w=[("ps", 2 + h // 2)])
            for h in range(4):
                pkv = PSALL[:, 1024 + h * 256:1024 + (h + 1) * 256]
                S.add("dve", lambda e, h=h, pkv=pkv: e.scalar_tensor_tensor(out=state[:, h, :], in0=state[:, h, :], scalar=GC[h], in1=pkv,
                                                                         op0=ALU.mult, op1=ALU.add),
                      r=[("state",), ("ps", 2 + h // 2)], w=[("state",)])
            S.add("act", lambda e: e.activation(out=stbf[:], in_=state[:], func=AF.Copy), r=[("state",)], w=[("stbf",)])
            pk45 = [("ps", 4), ("ps", 5)]
            for h in range(4):
                po = PSALL[:, 2048 + h * 256:2048 + (h + 1) * 256]
                S.add("dve", lambda e, h=h, po=po: e.bn_stats(out=stats[:, h, :], in_=po), r=pk45, w=[("stats",)])
            for h in range(4):
                S.add("dve", lambda e, h=h: e.bn_aggr(out=mv[:, h, :], in_=stats[:, h, :]), r=[("stats",)], w=[("mv",)])
            S.add("act", lambda e: e.activation(out=rs[:], in_=mv[:, :, 1], func=AF.Sqrt, bias=epsc[:, 1:2]), r=[("mv",), ("epsc",)], w=[("rs",)])
            S.add("dve", lambda e: e.reciprocal(out=rs[:], in_=rs[:]), r=[("rs",)], w=[("rs",)])
            for h in range(4):
                po = PSALL[:, 2048 + h * 256:2048 + (h + 1) * 256]
                S.add("dve", lambda e, h=h, po=po: e.tensor_scalar(out=yn[:, h * 256:(h + 1) * 256], in0=po, scalar1=mv[:, h, 0:1], scalar2=rs[:, h:h + 1],
                                                                op0=ALU.subtract, op1=ALU.mult),
                      r=pk45 + [("mv",), ("rs",)], w=[("yn",)])
            S.add("pool", lambda e: e.tensor_tensor(out=yrt[:], in0=yn[:], in1=srg[:], op=ALU.mult), r=[("yn",), ("srg", i)], w=[("yrt", i)])

        def stage_B2(t):
            i = t % 2
            yrt = yrt2[i]
            for c in range(8):
                S.add("pe", lambda e, c=c: e.matmul(PSALL[:, 3072 + c * 128:3072 + (c + 1) * 128], lhsT=yrt[:, c * 128:(c + 1) * 128], rhs=ident, start=True, stop=True),
                      r=[("yrt", i), ("constb",)], w=[("ps", 6 + c // 4)])
            S.add("act", lambda e: e.activation(out=yretT[:, :, tls(t)], in_=PSALL[:, 3072:4096].rearrange("p (c q) -> p c q", c=8), func=AF.Copy),
                  r=[("ps", 6), ("ps", 7)], w=[("yretT", t // 4)])

        for it in range(-2, 16):
            if 0 <= it + 2 < 16:
                stage_A(it + 2)
            if 0 <= it + 1 < 16:
                stage_B1(it + 1)
            if 0 <= it < 16:
                stage_B2(it)

    def pass_moba(s):
        Wm = carve(0, [128, KC, 1536], BF16)
        kTa = carve(24 * K, [128, 4, T], BF16)
        vaug = carve(96 * K, [128, 16, 8, 66], BF16)
        ymobaT = carve(80 * K, [128, 4, T], BF16)
        o = 96 * K + 16896
        mcs = carve(o, [128, 16, 2, 8], F32); o += 1 * K
        mcb = carve(o, [128, 1152], BF16); o += 2304
        tri01 = mcb[:, 0:128]
        E128 = mcb[:, 128:1152].rearrange("p (n k) -> p n k", n=8)
        xbt = [carve(o + 2 * K * i, [128, KC, 128], BF16) for i in range(2)]; o += 4 * K
        qktm = carve(o, [128, 16, 64], BF16); o += 2 * K
        r1 = carve(o, [128, 16, 8], F32); o += 512
        r2 = carve(o, [128, 16, 8], F32); o += 512
        qTb = carve(o, [128, 4, 256], BF16); o += 2 * K
        qz = carve(44 * K, [128, 4, 2, 256], BF16)
        ksum = carve(o, [128, 4, 8], F32); o += 128
        ksb = carve(o, [128, 4, 64], BF16); o += 512
        gs = carve(o, [128, 8, 8], F32); o += 256
        cmp_ = carve(o, [128, 8, 8, 8], F32); o += 2 * K
        cntt = carve(o, [128, 8, 8], F32); o += 256
        negm = carve(o, [128, 8, 32], BF16); o += 512
        negT = carve(o, [128, 8, 256], BF16); o += 4 * K
        expP = [carve(o + 1024 * i, [128, 512], BF16) for i in range(3)]; o += 3 * K
        rec = carve(o, [128, 2, 2, 4], F32); o += 64
        ytm = carve(o, [128, 2, 512], BF16); o += 2 * K
        qkf = carve(40 * K, [128, 1024], F32)
        assert o <= 143 * K, o

        cast_dma_cols(Wm, win_d, 3072, 4608, "wm")
        S.add("sp", lambda e: e.dma_start(out=mcs[:], in_=mcs_d), w=[("mcs",)], dsem="c_mcs")
        S.add("sp", lambda e: e.dma_start(out=mcb[:], in_=mcb_d), w=[("mcb",)], dsem="c_mcb")
        S.add("dve", lambda e: e.memset(vaug[:].rearrange("p t h d -> p (t h d)"), 1.0), w=[("vaug", t) for t in range(16)])
        S.add("dve", lambda e: e.memset(negT[:].rearrange("p h q -> p (h q)"), 0.0), w=[("negT",)])
        S.add("dve", lambda e: e.memset(qz[:].rearrange("p c j q -> p (c j q)"), 0.0), w=[("qz", 0), ("qz", 1)])
        S.add("dve", lambda e: e.memset(negm[:].rearrange("p h n -> p (h n)"), 0.0), w=[("negm",)])
        S.add("dve", lambda e: e.memset(ksum[:].rearrange("p c n -> p (c n)"), 0.0), w=[("ksum",)])
        S.add("dve", lambda e: e.memset(ksb[:].rearrange("p c n -> p (c n)"), 0.0), w=[("ksb",)])

        PQK = PSALL[:, 0:1024].rearrange("p (g d) -> p g d", g=16, d=64)
        sti = 0
        for b in range(8):
            if dbg == 100:
                break
            for tt in range(2):
                t = 2 * b + tt
                xb = xbt[t % 2]
                kx = ("xbt", t % 2)
                xbt_cast(xb, t, kx)
                for cb in range(3):
                    for kc in range(KC):
                        S.add("pe", lambda e, cb=cb, kc=kc, xb=xb: e.matmul(PS[cb][:], lhsT=xb[:, kc, :], rhs=Wm[:, kc, cb * 512:(cb + 1) * 512],
                                                                         start=(kc == 0), stop=(kc == KC - 1)),
                              r=[kx, ("wm", 0)], w=[("ps", cb)])
                if dbg == 101:
                    continue
                pk = [("ps", 0), ("ps", 1)]
                cosb = mcs[:, t, 0, :].unsqueeze(1).to_broadcast([128, 16, 8])
                sinb = mcs[:, t, 1, :].unsqueeze(1).to_broadcast([128, 16, 8])
                qkf16 = qkf[:].rearrange("p (g d) -> p g d", g=16, d=64)
                x1 = qkf16[:, :, 0:8]
                x2 = qkf16[:, :, 8:16]
                S.add("act", lambda e: e.activation(out=qkf[:], in_=PSALL[:, 0:1024], func=AF.Copy), r=pk, w=[("qkf",)])
                S.add("pool", lambda e, qkf16=qkf16: e.tensor_copy(out=qktm[:], in_=qkf16), r=[("qkf",)], w=[("qktm_c",)])
                S.add("pool", lambda e, x1=x1, cosb=cosb: e.tensor_tensor(out=r1[:], in0=x1, in1=cosb, op=ALU.mult), r=[("qkf",), ("mcs",)], w=[("r1",)])
                S.add("pool", lambda e, x2=x2, sinb=sinb: e.tensor_tensor(out=r2[:], in0=x2, in1=sinb, op=ALU.mult), r=[("qkf",), ("mcs",)], w=[("r2",)])
                S.add("pool", lambda e: e.tensor_tensor(out=qktm[:, :, 0:8], in0=r1[:], in1=r2[:], op=ALU.subtract), r=[("r1",), ("r2",), ("qktm_c",)], w=[("qktm_a",)])
                S.add("pool", lambda e, x1=x1, sinb=sinb: e.tensor_tensor(out=r1[:], in0=x1, in1=sinb, op=ALU.mult), r=[("qkf",), ("mcs",)], w=[("r1",)])
                S.add("pool", lambda e, x2=x2, cosb=cosb: e.tensor_tensor(out=r2[:], in0=x2, in1=cosb, op=ALU.mult), r=[("qkf",), ("mcs",)], w=[("r2",)])
                S.add("pool", lambda e: e.tensor_tensor(out=qktm[:, :, 8:16], in0=r1[:], in1=r2[:], op=ALU.add), r=[("r1",), ("r2",), ("qktm_c",)], w=[("qktm_b",)])
                qk_keys = [("qktm_a",), ("qktm_b",), ("qktm_c",)]
                if dbg == 102:
                    continue
                S.add("act", lambda e, t=t: e.activation(out=vaug[:, t, :, 0:64], in_=PS[2][:].rearrange("p (h d) -> p h d", h=8), func=AF.Copy),
                      r=[("ps", 2)], w=[("vaug", t)])
                if dbg == 103:
                    continue
                qf = qktm[:, 0:8, :].rearrange("p h d -> p (h d)")
                kf = qktm[:, 8:16, :].rearrange("p h d -> p (h d)")
                for c in range(4):
                    S.add("pe", lambda e, c=c, qf=qf: e.matmul(PS[3][:, c * 128:(c + 1) * 128], lhsT=qf[:, c * 128:(c + 1) * 128], rhs=ident, start=True, stop=True),
                          r=qk_keys + [("constb",)], w=[("ps", 3)])
                for c in range(4):
                    S.add("pe", lambda e, c=c, kf=kf: e.matmul(PS[4][:, c * 128:(c + 1) * 128], lhsT=kf[:, c * 128:(c + 1) * 128], rhs=ident, start=True, stop=True),
                          r=qk_keys + [("constb",)], w=[("ps", 4)])
                S.add("act", lambda e, tt=tt: e.activation(out=qTb[:, :, tt * 128:(tt + 1) * 128], in_=PS[3][:].rearrange("p (c q) -> p c q", c=4), func=AF.Copy),
                      r=[("ps", 3)], w=[("qTb", tt)])
                S.add("act", lambda e, tt=tt: e.activation(out=qz[0:64, :, 0, tt * 128:(tt + 1) * 128], in_=PS[3][0:64, :].rearrange("p (c q) -> p c q", c=4), func=AF.Copy),
                      r=[("ps", 3)], w=[("qz", tt)])
                S.add("act", lambda e, tt=tt: e.activation(out=qz[64:128, :, 1, tt * 128:(tt + 1) * 128], in_=PS[3][64:128, :].rearrange("p (c q) -> p c q", c=4), func=AF.Copy),
                      r=[("ps", 3)], w=[("qz", tt)])
                S.add("act", lambda e, t=t: e.activation(out=kTa[:, :, tls(t)], in_=PS[4][:].rearrange("p (c q) -> p c q", c=4), func=AF.Copy),
                      r=[("ps", 4)], w=[("kTa", t)])
            if dbg is not None and (dbg < 11 or (100 <= dbg < 120)):
                continue
            glvl = 4 if (dbg is None or dbg < 120) else dbg - 120
            if b >= 4 and not (dbg is not None and dbg < 12):
                for tt in range(2):
                    for c in range(4):
                        S.add("pe", lambda e, c=c, tt=tt: e.matmul(PS[2][:, c * 64:(c + 1) * 64], lhsT=qTb[:, c, tt * 128:(tt + 1) * 128],
                                                                rhs=ksb[:, c, :], start=True, stop=True),
                              r=[("qTb", tt), ("ksb",)], w=[("ps", 2)])
                    S.add("act", lambda e: e.activation(out=gs[:].rearrange("p (c j) n -> p c (j n)", c=4), in_=PS[2][:, 0:256].rearrange("p (c x) -> p c x", c=4)[:, :, 0:16], func=AF.Copy),
                          r=[("ps", 2)], w=[("gs",)])
                    if glvl < 2:
                        continue
                    gm = gs[:, :, 0:b].unsqueeze(2).to_broadcast([128, 8, b, b])
                    gn = gs[:, :, 0:b].unsqueeze(3).to_broadcast([128, 8, b, b])
                    S.add("dve", lambda e, gm=gm, gn=gn, b=b: e.tensor_tensor(out=cmp_[:, :, 0:b, 0:b], in0=gm, in1=gn, op=ALU.is_gt), r=[("gs",)], w=[("cmp",)])
                    S.add("dve", lambda e, b=b: e.tensor_reduce(out=cntt[:, :, 0:b], in_=cmp_[:, :, 0:b, 0:b], axis=AX.X, op=ALU.add), r=[("cmp",)], w=[("cnt",)])
                    S.add("dve", lambda e, b=b: e.tensor_scalar(out=negm[:, :, 0:b], in0=cntt[:, :, 0:b], scalar1=2.5, scalar2=NEG, op0=ALU.is_gt, op1=ALU.mult),
                          r=[("cnt",)], w=[("negm",)])
                    if glvl < 3:
                        continue
                    for h in range(8):
                        S.add("pe", lambda e, h=h: e.matmul(PSALL[0:32, h * 128:(h + 1) * 128], lhsT=negm[:, h, :], rhs=ident, start=True, stop=True),
                              r=[("negm",), ("constb",)], w=[("ps", h // 4)])
                    S.add("act", lambda e, tt=tt: e.activation(out=negT[0:8, :, tt * 128:(tt + 1) * 128], in_=PSALL[0:8, 0:1024].rearrange("p (h q) -> p h q", h=8), func=AF.Copy),
                          r=[("ps", 0), ("ps", 1)], w=[("negT",)])
            items = [(h, n) for h in range(8) for n in list(range(b)) + [None]]
            usemask_b = (b >= 4) and not (dbg is not None and dbg < 12) and glvl >= 4

            def emit_S(item, slot, b=b, usemask_b=usemask_b):
                h, n = item
                c, j = h // 2, h % 2
                st = PS[1 + slot]
                kst = ("ps", 1 + slot)
                ex = expP[slot]
                kex = ("expP", slot)
                qh = qz[:, c, j, :]
                rq = [("qz", 0), ("qz", 1)]
                if n is not None:
                    for jj in range(2):
                        kt = 2 * n + jj
                        so = st[:, jj * 256:(jj + 1) * 256]
                        S.add("pe", lambda e, kt=kt, so=so: e.matmul(so, lhsT=kTa[:, c, tls(kt)], rhs=qh, start=True, stop=(not usemask_b)),
                              r=[("kTa", kt)] + rq, w=[kst])
                        if usemask_b:
                            S.add("pe", lambda e, so=so: e.matmul(so, lhsT=E128[:, n, :], rhs=negT[:, h, :], start=False, stop=True),
                                  r=[("negT",), ("mcb",)], w=[kst])
                    S.add("act", lambda e: e.activation(out=ex[:], in_=st[:], func=AF.Exp, scale=0.125), r=[kst], w=[kex])
                else:
                    S.add("pe", lambda e: e.matmul(st[:, 0:256], lhsT=kTa[:, c, tls(2 * b)], rhs=qh, start=True, stop=True),
                          r=[("kTa", 2 * b)] + rq, w=[kst])
                    S.add("pe", lambda e: e.matmul(st[:, 384:512], lhsT=kTa[:, c, tls(2 * b + 1)], rhs=qh[:, 128:256], start=True, stop=True),
                          r=[("kTa", 2 * b + 1)] + rq, w=[kst])
                    S.add("act", lambda e: e.activation(out=ex[:, 0:256], in_=st[:, 0:256], func=AF.Exp, scale=0.125), r=[kst], w=[kex])
                    S.add("act", lambda e: e.activation(out=ex[:, 384:512], in_=st[:, 384:512], func=AF.Exp, scale=0.125), r=[kst], w=[kex])
                    S.add("pool", lambda e: e.tensor_tensor(out=ex[:, 0:128], in0=ex[:, 0:128], in1=tri01, op=ALU.mult), r=[kex, ("mcb",)], w=[kex])
                    S.add("pool", lambda e: e.tensor_tensor(out=ex[:, 384:512], in0=ex[:, 384:512], in1=tri01, op=ALU.mult), r=[kex, ("mcb",)], w=[kex])

            def emit_PV(item, slot, b=b):
                h, n = item
                hg = h // 4
                pos = [PS[4 + hg], PS[6 + hg]]
                okeys = [("ps", 4 + hg), ("ps", 6 + hg)]
                ex = expP[slot]
                kex = ("expP", slot)
                hs = slice((h % 4) * 66, (h % 4) * 66 + 66)
                if n is not None:
                    for jj in range(2):
                        kt = 2 * n + jj
                        for qt in range(2):
                            S.add("pe", lambda e, kt=kt, qt=qt, jj=jj: e.matmul(pos[qt][:, hs], lhsT=ex[:, jj * 256 + qt * 128:jj * 256 + (qt + 1) * 128],
                                                                              rhs=vaug[:, kt, h, :], start=(n == 0 and jj == 0), stop=False),
                                  r=[kex, ("vaug", kt)], w=[okeys[qt]])
                else:
                    S.add("pe", lambda e: e.matmul(pos[0][:, hs], lhsT=ex[:, 0:128], rhs=vaug[:, 2 * b, h, :], start=(b == 0), stop=True),
                          r=[kex, ("vaug", 2 * b)], w=[okeys[0]])
                    S.add("pe", lambda e: e.matmul(pos[1][:, hs], lhsT=ex[:, 128:256], rhs=vaug[:, 2 * b, h, :], start=(b == 0), stop=False),
                          r=[kex, ("vaug", 2 * b)], w=[okeys[1]])
                    S.add("pe", lambda e: e.matmul(pos[1][:, hs], lhsT=ex[:, 384:512], rhs=vaug[:, 2 * b + 1, h, :], start=False, stop=True),
                          r=[kex, ("vaug", 2 * b + 1)], w=[okeys[1]])
                    if h % 4 == 3:
                        for qt in range(2):
                            po = pos[qt][:, 0:264].rearrange("p (h d) -> p h d", h=4)
                            S.add("dve", lambda e, po=po, qt=qt: e.reciprocal(out=rec[:, qt, hg, :], in_=po[:, :, 64]), r=[okeys[qt]], w=[("rec", qt, hg)])
                            S.add("dve", lambda e, po=po, qt=qt: e.tensor_tensor(
                                out=ytm[:, qt, hg * 256:(hg + 1) * 256].rearrange("p (h d) -> p h d", h=4), in0=po[:, :, 0:64],
                                in1=rec[:, qt, hg, :].unsqueeze(2).to_broadcast([128, 4, 64]), op=ALU.mult),
                                r=[okeys[qt], ("rec", qt, hg)], w=[("ytm", qt, hg)])

            SKEW = 2
            for i in range(len(items) + SKEW):
                if i < len(items):
                    emit_S(items[i], (sti + i) % 3)
                if i >= SKEW:
                    emit_PV(items[i - SKEW], (sti + i - SKEW) % 3)
            sti += len(items)
            S.add("dve", lambda e, b=b: e.tensor_reduce(out=ksum[:, :, b], in_=kTa[:, :, b * 256:(b + 1) * 256], axis=AX.X, op=ALU.add),
                  r=[("kTa", 2 * b), ("kTa", 2 * b + 1)], w=[("ksum",)])
            S.add("act", lambda e: e.activation(out=ksb[0:64, :, 0:8], in_=ksum[0:64, :, :], func=AF.Copy), r=[("ksum",)], w=[("ksb",)])
            S.add("act", lambda e: e.activation(out=ksb[64:128, :, 8:16], in_=ksum[64:128, :, :], func=AF.Copy), r=[("ksum",)], w=[("ksb",)])
            for qt in range(2):
                for c in range(4):
                    S.add("pe", lambda e, c=c, qt=qt: e.matmul(PS[3][:, c * 128:(c + 1) * 128], lhsT=ytm[:, qt, c * 128:(c + 1) * 128], rhs=ident, start=True, stop=True),
                          r=[("ytm", qt, 0), ("ytm", qt, 1), ("constb",)], w=[("ps", 3)])
                S.add("act", lambda e, t=2 * b + qt: e.activation(out=ymobaT[:, :, tls(t)], in_=PS[3][:].rearrange("p (c q) -> p c q", c=4), func=AF.Copy),
                      r=[("ps", 3)], w=[("ymobaT", t // 4)])

    def pass_o1(s):
        RP = carve(0, [128, KC, D], BF16)
        WGA = carve(16 * K, [128, KC, D], BF16)
        WGB = carve(32 * K, [128, KC, D], BF16)
        yretT = carve(48 * K, [128, KC, T], BF16)
        ymobaT = carve(80 * K, [128, 4, T], BF16)
        MP = carve(96 * K, [128, 4, D], BF16)
        xblk = carve(104 * K, [128, KC, 512], BF16)
        ufin = carve(112 * K, [128, KC, 512], BF16)
        sga = carve(120 * K, [128, 512], F32)
        sgb = carve(122 * K, [128, 512], F32)
        u1 = carve(124 * K, [128, 512], F32)
        u2 = carve(126 * K, [128, 512], F32)
        cast_dma_cols(RP, rp_d, 0, D, "rp", step=D)
        cast_dma_cols(WGA, win_d, 4608, 5632, "wga", step=D)
        cast_dma_cols(WGB, win_d, 5632, 6656, "wgb", step=D)
        cast_dma_cols(MP, mp_d, 0, D, "mp", step=D)
        for tb in range(4):
            sl = tbs(tb)
            S.add("act", lambda e, sl=sl: e.activation(out=xblk[:], in_=R[:, :, sl], func=AF.Copy), r=Rkeys(tb), w=[("xblk",)])
            for m in range(KC):
                o4 = 4 * (m % 2)
                ms = slice(m * 128, (m + 1) * 128)
                for kc in range(KC):
                    S.add("pe", lambda e, kc=kc, ms=ms, o4=o4, sl=sl: e.matmul(PS[o4][:], lhsT=RP[:, kc, ms], rhs=yretT[:, kc, sl], start=(kc == 0), stop=(kc == KC - 1)),
                          r=[("rp", 0), ("yretT", tb)], w=[("ps", o4)])
                for kc in range(KC):
                    S.add("pe", lambda e, kc=kc, ms=ms, o4=o4: e.matmul(PS[o4 + 1][:], lhsT=WGA[:, kc, ms], rhs=xblk[:, kc, :], start=(kc == 0), stop=(kc == KC - 1)),
                          r=[("wga", 0), ("xblk",)], w=[("ps", o4 + 1)])
                for c in range(4):
                    S.add("pe", lambda e, c=c, ms=ms, o4=o4, sl=sl: e.matmul(PS[o4 + 2][:], lhsT=MP[:, c, ms], rhs=ymobaT[:, c, sl], start=(c == 0), stop=(c == 3)),
                          r=[("mp", 0), ("ymobaT", tb)], w=[("ps", o4 + 2)])
                for kc in range(KC):
                    S.add("pe", lambda e, kc=kc, ms=ms, o4=o4: e.matmul(PS[o4 + 3][:], lhsT=WGB[:, kc, ms], rhs=xblk[:, kc, :], start=(kc == 0), stop=(kc == KC - 1)),
                          r=[("wgb", 0), ("xblk",)], w=[("ps", o4 + 3)])
                S.add("act", lambda e, o4=o4: e.activation(out=sga[:], in_=PS[o4 + 1][:], func=AF.Sigmoid), r=[("ps", o4 + 1)], w=[("sga",)])
                S.add("act", lambda e, o4=o4: e.activation(out=sgb[:], in_=PS[o4 + 3][:], func=AF.Sigmoid), r=[("ps", o4 + 3)], w=[("sgb",)])
                S.add("dve", lambda e, o4=o4: e.tensor_tensor(out=u1[:], in0=PS[o4][:], in1=sga[:], op=ALU.mult), r=[("ps", o4), ("sga",)], w=[("u1",)])
                S.add("dve", lambda e, o4=o4: e.tensor_tensor(out=u2[:], in0=PS[o4 + 2][:], in1=sgb[:], op=ALU.mult), r=[("ps", o4 + 2), ("sgb",)], w=[("u2",)])
                S.add("pool", lambda e, m=m: e.tensor_tensor(out=ufin[:, m, :], in0=u1[:], in1=u2[:], op=ALU.add), r=[("u1",), ("u2",)], w=[("ufin",)])
            S.add("act", lambda e, sl=sl: e.activation(out=yretT[:, :, sl], in_=ufin[:], func=AF.Copy), r=[("ufin",)], w=[("yretT", tb)])

    def pass_o2(s):
        WO = carve(0, [128, KC, D], BF16)
        U = carve(48 * K, [128, KC, T], BF16)
        cast_dma_cols(WO, wo_d, 0, D, "wo", step=D)
        i2 = 0
        for tb in range(4):
            sl = tbs(tb)
            for m in range(KC):
                pb = i2 % 2
                i2 += 1
                ms = slice(m * 128, (m + 1) * 128)
                for kc in range(KC):
                    S.add("pe", lambda e, kc=kc, ms=ms, pb=pb, sl=sl: e.matmul(PS[pb][:], lhsT=WO[:, kc, ms], rhs=U[:, kc, sl], start=(kc == 0), stop=(kc == KC - 1)),
                          r=[("wo", 0), ("yretT", tb)], w=[("ps", pb)])
                S.add("dve", lambda e, m=m, pb=pb, sl=sl: e.scalar_tensor_tensor(out=R[:, m, sl], in0=R[:, m, sl], scalar=ALPHA, in1=PS[pb][:], op0=ALU.mult, op1=ALU.add),
                      r=[("ps", pb), ("R", tb, m)], w=[("R", tb, m)])
            layer_norm(tb, 1)

    def mixer_phase(s):
        if "r" in mixp:
            pass_ret(s)
            S.barrier()
        if "m" in mixp:
            pass_moba(s)
            S.barrier()
        if "o1" in mixp:
            pass_o1(s)
            S.barrier()
        if "o2" in mixp:
            pass_o2(s)

    for s in range(nseq):
        if "load" in phases:
            for tb in range(4):
                S.add("sp", lambda e, tb=tb: e.dma_start(out=R[:, :, tbs(tb)], in_=xT[:, :, s * T + tb * 512: s * T + (tb + 1) * 512]),
                      w=Rkeys(tb), dsem=("xin", tb))
            S.barrier()
        if "ffn1" in phases:
            ffn_phase(s, 0)
            S.barrier()
        if "mix" in phases:
            mixer_phase(s)
            S.barrier()
        if "ffn2" in phases:
            ffn_phase(s, 1)
            S.barrier()
        out_phase(s)
    S.add("sp", lambda e: e.nop(), r=[k for tb in range(4) for k in Rkeys(tb)], w=[k for tb in range(4) for k in Rkeys(tb)])

    engines = {
        "pe": ("tensor", nc.tensor),
        "act": ("scalar", nc.scalar),
        "dve": ("vector", nc.vector),
        "pool": ("gpsimd", nc.gpsimd),
        "sp": ("sync", nc.sync),
    }
    S.emit(nc, engines)
    return nc, S


def _feat_major(v):
    return np.ascontiguousarray(np.asarray(v, np.float32).reshape(KC, 128).T)


def _module_consts():
    c = {}
    pos = np.arange(T, dtype=np.float32)
    inv = (1.0 / (np.float32(10000.0) ** np.linspace(0.0, 1.0, 64, dtype=np.float32))).astype(np.float32)
    ang = (pos[:, None] * inv[None, :]).astype(np.float32)
    rcs = np.stack([np.cos(ang), np.sin(ang)], axis=1).astype(np.float32)
    c["rcs"] = np.ascontiguousarray(rcs.reshape(16, 128, 2, 64).transpose(1, 0, 2, 3))
    inv2 = (1.0 / (np.float32(500000.0) ** (np.arange(0, 16, 2, dtype=np.float32) / np.float32(16)))).astype(np.float32)
    ang2 = (pos[:, None] * inv2[None, :]).astype(np.float32)
    mcs = np.stack([np.cos(ang2), np.sin(ang2)], axis=1).astype(np.float32)
    c["mcs"] = np.ascontiguousarray(mcs.reshape(16, 128, 2, 8).transpose(1, 0, 2, 3))
    g = 1.0 - 2.0 ** (-5.0 - np.arange(4, dtype=np.float64))
    idx = np.arange(128, dtype=np.float64)
    sc = 128.0 ** -0.5
    rdec = np.zeros((128, 1032), np.float32)
    for h in range(4):
        diff = idx[None, :] - idx[:, None]
        dt = np.where(diff >= 0, g[h] ** np.maximum(diff, 0.0), 0.0) * sc
        rdec[:, h * 128:(h + 1) * 128] = dt
        rdec[:, 512 + h * 128:512 + (h + 1) * 128] = (g[h] ** (idx + 1.0))[None, :]
        rdec[:, 1024 + h] = g[h] ** (127.0 - idx) * sc
    c["rdec"] = rdec
    mcb = np.zeros((128, 1152), np.float32)
    mcb[:, 0:128] = (idx[:, None] <= idx[None, :]).astype(np.float32)
    for n in range(8):
        mcb[n, 128 + n * 128:128 + (n + 1) * 128] = 1.0
    c["mconstb"] = mcb.astype(ml_dtypes.bfloat16)
    return c


def prep_shared(inp):
    sh = {}
    for f, (gu, dn) in enumerate([("ffn1_w_gu", "ffn1_w_down"), ("ffn2_w_gu", "ffn2_w_down")]):
        w = np.asarray(inp[gu], np.float32)[0]
        w = w.reshape(KC, 128, 2, NJ, 128)
        sh["wgu%d" % (f + 1)] = np.ascontiguousarray(w.transpose(3, 1, 2, 0, 4))
        sh["wd%d" % (f + 1)] = np.ascontiguousarray(np.asarray(inp[dn], np.float32)[0].reshape(NJ, 128, D))
    lnp = np.stack([_feat_major(inp[k][0]) for k in ("ln1_g", "ln1_b", "lnm_g", "lnm_b", "ln2_g", "ln2_b")], axis=1)
    sh["lnp"] = np.ascontiguousarray(lnp)
    def kmajor(w, nk):
        w = np.asarray(w, np.float32)
        return np.ascontiguousarray(w.reshape(nk, 128, w.shape[-1]).transpose(1, 0, 2))
    sh["win"] = kmajor(inp["w_in"][0], KC)
    sh["retp"] = kmajor(inp["ret_proj"][0], KC)
    sh["mobp"] = kmajor(inp["moba_proj"][0], 4)
    sh["wout"] = kmajor(inp["w_out"][0], KC)
    sh.update(_module_consts())
    cb = np.zeros((128, 256), np.float32)
    cb[:, 0:128] = np.eye(128, dtype=np.float32)
    cb[:, 128:256] = 1.0 / 1024.0
    sh["constb"] = cb.astype(ml_dtypes.bfloat16)
    return sh


def shard_x(x, nseq=SEQ_PER_CORE, ncores=NCORES):
    x = np.asarray(x, np.float32)
    maps = []
    for c in range(ncores):
        xc = x[c * nseq:(c + 1) * nseq].reshape(nseq * T, KC, 128)
        maps.append(np.ascontiguousarray(xc.transpose(2, 1, 0)))
    return maps


def unshard(outs, nseq=SEQ_PER_CORE):
    res = []
    for o in outs:
        res.append(np.asarray(o, np.float32).transpose(2, 1, 0).reshape(nseq, T, D))
    return np.ascontiguousarray(np.concatenate(res, axis=0))


_CACHE = {}


def kernel(**inputs):
    if "nc" not in _CACHE:
        _CACHE["nc"] = build_program()[0]
    nc = _CACHE["nc"]
    sh = prep_shared(inputs)
    xs = shard_x(inputs["x"])
    in_maps = []
    for c in range(NCORES):
        m = dict(sh)
        m["xT"] = xs[c]
        in_maps.append(m)
    res = run_bass_kernel_spmd(nc, in_maps, core_ids=list(range(NCORES)))
    return unshard([r["outT"] for r in res.results])
```

```python
import math
import numpy as np
import ml_dtypes
import concourse.bass as bass
import concourse.mybir as mybir
from concourse.bass_utils import run_bass_kernel_spmd

F32 = mybir.dt.float32
BF16 = mybir.dt.bfloat16
AF = mybir.ActivationFunctionType
ALU = mybir.AluOpType
AX = mybir.AxisListType

D = 1024
T = 2048
KC = 8
DFF = 2816
NJ = 22
NCORES = 8
SEQ_PER_CORE = 2
ALPHA = 2.0 ** 0.25
LN_EPS = 1e-5
GN_EPS = 1e-5
WIN = 6656
GROUPS = [(0, 8), (8, 15), (15, 22)]
NEG = -30000.0


class Op:
    __slots__ = ("eng", "fn", "deps", "dsem", "sig", "cnt")

    def __init__(self, eng, fn, deps, dsem):
        self.eng = eng
        self.fn = fn
        self.deps = deps
        self.dsem = dsem
        self.sig = dsem is not None
        self.cnt = 0


class Sched:
    def __init__(self):
        self.ops = []
        self.lastw = {}
        self.readers = {}
        self.bar = set()
        self.last_stream = {}

    def add(self, eng, fn, r=(), w=(), dsem=None):
        i = len(self.ops)
        deps = set(self.bar)
        for k in r:
            j = self.lastw.get(k)
            if j is not None:
                deps.add(j)
        for k in w:
            j = self.lastw.get(k)
            if j is not None:
                deps.add(j)
            rd = self.readers.get(k)
            if rd:
                deps.update(rd.values())
        stream = ("dma", dsem) if dsem is not None else eng
        for k in r:
            self.readers.setdefault(k, {})[stream] = i
        for k in w:
            self.lastw[k] = i
            self.readers[k] = {}
        self.last_stream[stream] = i
        self.ops.append(Op(eng, fn, deps, dsem))
        return i

    def barrier(self):
        self.bar = set(self.last_stream.values())

    def emit(self, nc, engines):
        ops = self.ops

        def stream_of(o):
            return ("dma", o.dsem) if o.dsem is not None else o.eng

        for i, o in enumerate(ops):
            best = {}
            for j in o.deps:
                p = ops[j]
                st = stream_of(p)
                if p.dsem is None and p.eng == "pe" and o.eng == "pe" and o.dsem is None:
                    continue
                if st not in best or best[st] < j:
                    best[st] = j
            o.deps = best
            for j in best.values():
                ops[j].sig = True
        cnt = {}
        for o in ops:
            if o.sig:
                st = stream_of(o)
                inc = 16 if o.dsem is not None else 1
                cnt[st] = cnt.get(st, 0) + inc
                o.cnt = cnt[st]
        sems = {}
        import contextlib
        stack = contextlib.ExitStack()
        for k, st in enumerate(cnt.keys()):
            sems[st] = stack.enter_context(nc.semaphore("s%d" % k))
        self.max_counts = dict(cnt)
        with stack:
            with nc.Block() as block:
                for ename, (deco, _h) in engines.items():
                    my = [(i, o) for i, o in enumerate(ops) if o.eng == ename]
                    if not my:
                        continue

                    def body(eng, my=my, ename=ename):
                        waited = {}
                        for i, o in my:
                            for st, j in o.deps.items():
                                v = ops[j].cnt
                                if waited.get(st, 0) < v:
                                    eng.wait_ge(sems[st], v)
                                    waited[st] = v
                            ins = o.fn(eng)
                            if o.sig:
                                st = stream_of(o)
                                ins.then_inc(sems[st], 16 if o.dsem is not None else 1)
                                if o.dsem is None:
                                    pass
                    getattr(block, deco)(body)


def _tile(nc, name, shape, dt):
    return nc.sbuf_tensor(name, list(shape), dt).__enter__()


def _ptile(nc, name, shape, dt):
    return nc.psum_tensor(name, list(shape), dt).__enter__()


def build_program(nseq=SEQ_PER_CORE, phases=("ffn1", "mix", "ffn2"), dbg=None, mixp=("r", "m", "o1", "o2")):
    nc = bass.Bass("TRN2", target_bir_lowering=False)
    S = Sched()
    NT = nseq * T

    def din(name, shape, dt=F32):
        return nc.dram_tensor(name, list(shape), dt, kind="ExternalInput").ap()

    xT = din("xT", [128, KC, NT])
    outT = nc.dram_tensor("outT", [128, KC, NT], F32, kind="ExternalOutput").ap()
    wgu_d = [din("wgu1", [NJ, 128, 2, KC, 128]), din("wgu2", [NJ, 128, 2, KC, 128])]
    wd_d = [din("wd1", [NJ, 128, D]), din("wd2", [NJ, 128, D])]
    lnp_d = din("lnp", [128, 6, KC])
    cb_d = din("constb", [128, 256], BF16)
    win_d = din("win", [128, KC, WIN])
    rp_d = din("retp", [128, KC, D])
    mp_d = din("mobp", [128, 4, D])
    wo_d = din("wout", [128, KC, D])
    rcs_d = din("rcs", [128, 16, 2, 64])
    mcs_d = din("mcs", [128, 16, 2, 8])
    rdec_d = din("rdec", [128, 1032])
    mcb_d = din("mconstb", [128, 1152], BF16)

    R = _tile(nc, "R", [128, KC, T], F32)
    ARENA = _tile(nc, "ARENA", [128, 73216], BF16)
    lnp = _tile(nc, "lnp_sb", [128, 6, KC], F32)
    constb = _tile(nc, "constb_sb", [128, 256], BF16)
    ident = constb[:, 0:128]
    ones_div = constb[:, 128:256]
    epsc = _tile(nc, "epsc", [128, 2], F32)

    def carve(off_bytes, shape, dt):
        n = int(np.prod(shape[1:]))
        if dt == BF16:
            a = ARENA[:, off_bytes // 2: off_bytes // 2 + n]
        else:
            a = ARENA[:, off_bytes // 2: off_bytes // 2 + 2 * n].bitcast(F32)
        if len(shape) == 2:
            return a
        names = " ".join("d%d" % i for i in range(1, len(shape)))
        kw = {"d%d" % i: shape[i] for i in range(1, len(shape))}
        return a.rearrange("p (%s) -> p %s" % (names, names), **kw)

    K = 1024
    Xb = carve(0, [128, KC, T], BF16)
    actT = carve(32 * K, [128, 8, T], BF16)
    Wgu = [carve(64 * K + 4 * K * b, [128, 2, KC, 128], BF16) for b in range(3)]
    Wd = carve(76 * K, [128, 8, D], BF16)
    zb = carve(92 * K, [128, KC, 512], BF16)
    zsq = carve(100 * K, [128, KC, 512], BF16)
    mean_sb = carve(108 * K, [128, 512], F32)
    rstd_sb = carve(110 * K, [128, 512], F32)
    var_sb = carve(112 * K, [128, 512], F32)
    sg = [carve(114 * K + 1 * K * b, [128, 512], BF16) for b in range(2)]

    PSALL = _ptile(nc, "psall", [128, 4096], F32)
    PS = [PSALL[:, b * 512:(b + 1) * 512] for b in range(8)]

    def PSB(b):
        return PSALL[:, b * 512:(b + 1) * 512].bitcast(BF16)

    S.add("sp", lambda e: e.dma_start(out=lnp[:], in_=lnp_d), w=[("lnp",)], dsem="c_lnp")
    S.add("sp", lambda e: e.dma_start(out=constb[:], in_=cb_d), w=[("constb",)], dsem="c_cb")
    S.add("dve", lambda e: e.memset(epsc[:, 0:1], LN_EPS), w=[("epsc",)])
    S.add("dve", lambda e: e.memset(epsc[:, 1:2], GN_EPS), w=[("epsc",)])

    cnt = {"wgu": 0}

    def tbs(tb):
        return slice(tb * 512, (tb + 1) * 512)

    def Rkeys(tb):
        return [("R", tb, kc) for kc in range(KC)]

    def layer_norm(tb, gi):
        sl = tbs(tb)
        Rb = R[:, :, sl]
        S.add("dve", lambda e: e.tensor_copy(out=zb[:], in_=Rb), r=Rkeys(tb), w=[("zb",)])
        S.add("act", lambda e: e.activation(out=zsq[:], in_=Rb, func=AF.Square), r=Rkeys(tb), w=[("zsq",)])
        pm, pq = PS[6], PS[7]
        for kc in range(KC):
            S.add("pe", lambda e, kc=kc: e.matmul(pm[:], lhsT=ones_div, rhs=zb[:, kc, :], start=(kc == 0), stop=(kc == KC - 1)),
                  r=[("zb",), ("constb",)], w=[("ps", 6)])
        for kc in range(KC):
            S.add("pe", lambda e, kc=kc: e.matmul(pq[:], lhsT=ones_div, rhs=zsq[:, kc, :], start=(kc == 0), stop=(kc == KC - 1)),
                  r=[("zsq",), ("constb",)], w=[("ps", 7)])
        S.add("act", lambda e: e.activation(out=mean_sb[:], in_=pm[:], func=AF.Copy), r=[("ps", 6)], w=[("mean",)])
        S.add("act", lambda e: e.activation(out=var_sb[:], in_=pm[:], func=AF.Square), r=[("ps", 6)], w=[("var",)])
        S.add("dve", lambda e: e.tensor_tensor(out=var_sb[:], in0=pq[:], in1=var_sb[:], op=ALU.subtract),
              r=[("ps", 7), ("var",)], w=[("var",)])
        S.add("act", lambda e: e.activation(out=var_sb[:], in_=var_sb[:], func=AF.Sqrt, bias=epsc[:, 0:1]),
              r=[("var",), ("epsc",)], w=[("var",)])
        S.add("dve", lambda e: e.reciprocal(out=rstd_sb[:], in_=var_sb[:]), r=[("var",)], w=[("rstd",)])
        mb = mean_sb[:].unsqueeze(1).to_broadcast([128, KC, 512])
        rb = rstd_sb[:].unsqueeze(1).to_broadcast([128, KC, 512])
        S.add("dve", lambda e: e.tensor_tensor(out=Rb, in0=Rb, in1=mb, op=ALU.subtract),
              r=Rkeys(tb) + [("mean",)], w=Rkeys(tb))
        S.add("dve", lambda e: e.tensor_tensor(out=Rb, in0=Rb, in1=rb, op=ALU.mult),
              r=Rkeys(tb) + [("rstd",)], w=Rkeys(tb))
        for kc in range(KC):
            S.add("act", lambda e, kc=kc: e.activation(out=R[:, kc, sl], in_=R[:, kc, sl], func=AF.Identity,
                                                      scale=lnp[:, 2 * gi, kc:kc + 1], bias=lnp[:, 2 * gi + 1, kc:kc + 1]),
                  r=[("R", tb, kc), ("lnp",)], w=[("R", tb, kc)])

    def ffn_phase(s, f):
        gi = 0 if f == 0 else 2
        if f == 0:
            for tb in range(4):
                S.add("sp", lambda e, tb=tb: e.dma_start(out=R[:, :, tbs(tb)], in_=xT[:, :, s * T + tb * 512: s * T + (tb + 1) * 512]),
                      w=Rkeys(tb), dsem=("xin", tb))
        for tb in range(4):
            S.add("dve", lambda e, tb=tb: e.tensor_copy(out=Xb[:, :, tbs(tb)], in_=R[:, :, tbs(tb)]), r=Rkeys(tb), w=[("xb", tb)])
            S.add("act", lambda e, tb=tb: e.activation(out=R[:, :, tbs(tb)], in_=R[:, :, tbs(tb)], func=AF.Copy, scale=ALPHA),
                  r=Rkeys(tb), w=Rkeys(tb))
        it = 0
        for (j0, j1) in GROUPS:
            G = j1 - j0
            for jl in range(G):
                S.add("pool", lambda e, jl=jl, j=j0 + jl: e.dma_start(out=Wd[:, jl, :], in_=wd_d[f][j]),
                      w=[("wd", jl)], dsem=("wd", jl))
            for jl in range(G):
                j = j0 + jl
                b = cnt["wgu"] % 3
                cnt["wgu"] += 1
                S.add("pool", lambda e, b=b, j=j: e.dma_start(out=Wgu[b][:], in_=wgu_d[f][j]), w=[("wgu", b)], dsem=("wgu", b))
                for tb in range(4):
                    pg, pu = PS[it % 2], PS[2 + it % 2]
                    kg, ku, ks = ("ps", it % 2), ("ps", 2 + it % 2), ("sg", it % 2)
                    sgt = sg[it % 2]
                    it += 1
                    for kc in range(KC):
                        S.add("pe", lambda e, pg=pg, b=b, kc=kc, tb=tb: e.matmul(pg[:], lhsT=Wgu[b][:, 0, kc, :], rhs=Xb[:, kc, tbs(tb)],
                                                                                start=(kc == 0), stop=(kc == KC - 1)),
                              r=[("wgu", b), ("xb", tb)], w=[kg])
                    for kc in range(KC):
                        S.add("pe", lambda e, pu=pu, b=b, kc=kc, tb=tb: e.matmul(pu[:], lhsT=Wgu[b][:, 1, kc, :], rhs=Xb[:, kc, tbs(tb)],
                                                                                start=(kc == 0), stop=(kc == KC - 1)),
                              r=[("wgu", b), ("xb", tb)], w=[ku])
                    S.add("act", lambda e, pg=pg, sgt=sgt: e.activation(out=sgt[:], in_=pg[:], func=AF.Silu), r=[kg], w=[ks])
                    S.add("dve", lambda e, pu=pu, sgt=sgt, jl=jl, tb=tb: e.tensor_tensor(out=actT[:, jl, tbs(tb)], in0=pu[:], in1=sgt[:], op=ALU.mult),
                          r=[ku, ks], w=[("actT", jl, tb)])
            lastg = (j1 == NJ)
            order = [(m, tb) for tb in range(4) for m in range(KC)] if lastg else [(m, tb) for m in range(KC) for tb in range(4)]
            i2 = 0
            for (m, tb) in order:
                pd = PS[4 + i2 % 2]
                kd = ("ps", 4 + i2 % 2)
                i2 += 1
                for jl in range(G):
                    S.add("pe", lambda e, pd=pd, jl=jl, m=m, tb=tb, G=G: e.matmul(pd[:], lhsT=Wd[:, jl, m * 128:(m + 1) * 128], rhs=actT[:, jl, tbs(tb)],
                                                                               start=(jl == 0), stop=(jl == G - 1)),
                          r=[("wd", jl), ("actT", jl, tb)], w=[kd])
                S.add("dve", lambda e, pd=pd, m=m, tb=tb: e.scalar_tensor_tensor(out=R[:, m, tbs(tb)], in0=pd[:], scalar=0.5, in1=R[:, m, tbs(tb)],
                                                                               op0=ALU.mult, op1=ALU.add),
                      r=[kd, ("R", tb, m)], w=[("R", tb, m)])
                if lastg and m == KC - 1:
                    layer_norm(tb, gi)

    def out_phase(s):
        for tb in range(4):
            S.add("sp", lambda e, tb=tb: e.dma_start(out=outT[:, :, s * T + tb * 512: s * T + (tb + 1) * 512], in_=R[:, :, tbs(tb)]),
                  r=Rkeys(tb), dsem=("xout", tb))


    G_H = [1.0 - 2.0 ** (-5.0 - h) for h in range(4)]
    GC = [g ** 128 for g in G_H]

    def tls(t):
        return slice(t * 128, (t + 1) * 128)

    def cast_dma_cols(dst, src_d, c0, c1, keybase, step=1536):
        for pi, a in enumerate(range(c0, c1, step)):
            b_ = min(a + step, c1)
            k = (keybase, pi)
            S.add("pool", lambda e, a=a, b_=b_: e.dma_start(out=dst[:, :, a - c0:b_ - c0], in_=src_d[:, :, a:b_]), w=[k], dsem=k)

    def xbt_cast(dst, t, key):
        S.add("act", lambda e: e.activation(out=dst[:], in_=R[:, :, tls(t)], func=AF.Copy),
              r=[("R", t // 4, kc) for kc in range(KC)], w=[key])

    def pass_ret(s):
        Wr = carve(0, [128, KC, 3072], BF16)
        yretT = carve(48 * K, [128, KC, T], BF16)
        o = 80 * K
        rcs = carve(o, [128, 16, 2, 64], F32); o += 8 * K
        rdec = carve(o, [128, 1032], F32); o += 4128
        DT = rdec[:, 0:512]
        GQ = rdec[:, 512:1024]
        gk = rdec[:, 1024:1028]
        xbt = [carve(o + 2 * K * i, [128, KC, 128], BF16) for i in range(2)]; o += 4 * K
        qkr = carve(o, [128, 8, 2, 64], BF16); o += 2 * K
        tA = carve(o, [128, 8, 2, 64], F32); o += 4 * K
        tB = carve(o, [128, 8, 64], F32); o += 2 * K
        tC = carve(o, [128, 8, 64], F32); o += 2 * K
        kdec2 = [carve(o + 1 * K * i, [128, 4, 128], BF16) for i in range(2)]; o += 2 * K
        vbf2 = [carve(o + 2 * K * i, [128, 1024], BF16) for i in range(2)]; o += 4 * K
        srg2 = [carve(o + 4 * K * i, [128, 1024], F32) for i in range(2)]; o += 8 * K
        qT2 = [carve(o + 1 * K * i, [128, 4, 128], BF16) for i in range(2)]; o += 2 * K
        qdT2 = [carve(o + 1 * K * i, [128, 4, 128], BF16) for i in range(2)]; o += 2 * K
        kT2 = [carve(o + 1 * K * i, [128, 4, 128], BF16) for i in range(2)]; o += 2 * K
        sT = carve(o, [128, 4, 128], BF16); o += 1 * K
        state = carve(o, [128, 4, 256], F32); o += 4 * K
        stbf = carve(o, [128, 4, 256], BF16); o += 2 * K
        stats = carve(o, [128, 4, 6], F32); o += 128
        mv = carve(o, [128, 4, 2], F32); o += 64
        rs = carve(o, [128, 4], F32); o += 64
        yn = carve(o, [128, 1024], F32); o += 4 * K
        yrt2 = [carve(o + 2 * K * i, [128, 1024], BF16) for i in range(2)]; o += 4 * K
        assert o <= 143 * K, o

        cast_dma_cols(Wr, win_d, 0, 3072, "wr")
        S.add("sp", lambda e: e.dma_start(out=rcs[:], in_=rcs_d), w=[("rcs",)], dsem="c_rcs")
        S.add("sp", lambda e: e.dma_start(out=rdec[:], in_=rdec_d), w=[("rdec",)], dsem="c_rdec")
        S.add("dve", lambda e: e.memset(state[:].rearrange("p h e -> p (h e)"), 0.0), w=[("state",)])
        S.add("dve", lambda e: e.memset(stbf[:].rearrange("p h e -> p (h e)"), 0.0), w=[("stbf",)])

        def stage_A(t):
            i = t % 2
            xb = xbt[i]
            kx = ("xbt", i)
            kdec, vbf, srg, qT, qdT, kT = kdec2[i], vbf2[i], srg2[i], qT2[i], qdT2[i], kT2[i]
            xbt_cast(xb, t, kx)
            for cb in range(6):
                for kc in range(KC):
                    S.add("pe", lambda e, cb=cb, kc=kc: e.matmul(PS[cb][:], lhsT=xb[:, kc, :], rhs=Wr[:, kc, cb * 512:(cb + 1) * 512],
                                                              start=(kc == 0), stop=(kc == KC - 1)),
                          r=[kx, ("wr", cb // 3)], w=[("ps", cb)])
            cos16 = rcs[:, t, 0, :].unsqueeze(1).to_broadcast([128, 16, 64])
            sin8 = rcs[:, t, 1, :].unsqueeze(1).to_broadcast([128, 8, 64])
            P16 = PSALL[:, 0:1024].rearrange("p (g i) -> p g i", g=16, i=64)
            P8 = PSALL[:, 0:1024].rearrange("p (g f i) -> p g f i", g=8, f=2, i=64)
            tA16 = tA[:].rearrange("p g f i -> p (g f) i")
            pk = [("ps", 0), ("ps", 1)]
            S.add("dve", lambda e: e.tensor_tensor(out=tA16, in0=P16, in1=cos16, op=ALU.mult), r=pk + [("rcs",)], w=[("tA",)])
            S.add("dve", lambda e: e.tensor_tensor(out=tB[:], in0=P8[:, :, 1, :], in1=sin8, op=ALU.mult), r=pk + [("rcs",)], w=[("tB",)])
            S.add("dve", lambda e: e.tensor_tensor(out=tC[:], in0=P8[:, :, 0, :], in1=sin8, op=ALU.mult), r=pk + [("rcs",)], w=[("tC",)])
            S.add("pool", lambda e: e.tensor_tensor(out=qkr[:, :, 0, :], in0=tA[:, :, 0, :], in1=tB[:], op=ALU.subtract),
                  r=[("tA",), ("tB",)], w=[("qkr0",)])
            S.add("pool", lambda e: e.tensor_tensor(out=qkr[:, :, 1, :], in0=tA[:, :, 1, :], in1=tC[:], op=ALU.add),
                  r=[("tA",), ("tC",)], w=[("qkr1",)])
            qk_keys = [("qkr0",), ("qkr1",)]
            qflat = qkr[:, 0:4].rearrange("p h f i -> p h (f i)")
            kflat = qkr[:, 4:8].rearrange("p h f i -> p h (f i)")
            S.add("pool", lambda e: e.tensor_tensor(out=kdec[:], in0=kflat, in1=gk.unsqueeze(2).to_broadcast([128, 4, 128]), op=ALU.mult),
                  r=qk_keys + [("rdec",)], w=[("kdec", i)])
            S.add("act", lambda e: e.activation(out=vbf[:], in_=PSALL[:, 1024:2048], func=AF.Copy), r=[("ps", 2), ("ps", 3)], w=[("vbf", i)])
            S.add("act", lambda e: e.activation(out=srg[:], in_=PSALL[:, 2048:3072], func=AF.Silu), r=[("ps", 4), ("ps", 5)], w=[("srg", i)])
            for h in range(4):
                S.add("pe", lambda e, h=h: e.matmul(PS[6][:, h * 128:(h + 1) * 128], lhsT=qflat[:, h, :], rhs=ident, start=True, stop=True),
                      r=qk_keys + [("constb",)], w=[("ps", 6)])
            for h in range(4):
                S.add("pe", lambda e, h=h: e.matmul(PS[7][:, h * 128:(h + 1) * 128], lhsT=kflat[:, h, :], rhs=ident, start=True, stop=True),
                      r=qk_keys + [("constb",)], w=[("ps", 7)])
            S.add("act", lambda e: e.activation(out=qT[:].rearrange("p h c -> p (h c)"), in_=PS[6][:], func=AF.Copy), r=[("ps", 6)], w=[("qT", i)])
            S.add("dve", lambda e: e.tensor_tensor(out=qdT[:].rearrange("p h c -> p (h c)"), in0=PS[6][:], in1=GQ, op=ALU.mult),
                  r=[("ps", 6), ("rdec",)], w=[("qdT", i)])
            S.add("act", lambda e: e.activation(out=kT[:].rearrange("p h c -> p (h c)"), in_=PS[7][:], func=AF.Copy), r=[("ps", 7)], w=[("kT", i)])

        def stage_B1(t):
            i = t % 2
            kdec, vbf, srg, qT, qdT, kT, yrt = kdec2[i], vbf2[i], srg2[i], qT2[i], qdT2[i], kT2[i], yrt2[i]
            for h in range(4):
                S.add("pe", lambda e, h=h: e.matmul(PS[6][:, h * 128:(h + 1) * 128], lhsT=kT[:, h, :], rhs=qT[:, h, :], start=True, stop=True),
                      r=[("kT", i), ("qT", i)], w=[("ps", 6)])
            S.add("dve", lambda e: e.tensor_tensor(out=sT[:].rearrange("p h c -> p (h c)"), in0=PS[6][:], in1=DT, op=ALU.mult),
                  r=[("ps", 6), ("rdec",)], w=[("sT",)])
            for h in range(4):
                po = PSALL[:, 2048 + h * 256:2048 + (h + 1) * 256]
                S.add("pe", lambda e, h=h, po=po: e.matmul(po, lhsT=sT[:, h, :], rhs=vbf[:, h * 256:(h + 1) * 256], start=True, stop=False),
                      r=[("sT",), ("vbf", i)], w=[("ps", 4 + h // 2)])
                S.add("pe", lambda e, h=h, po=po: e.matmul(po, lhsT=qdT[:, h, :], rhs=stbf[:, h, :], start=False, stop=True),
                      r=[("qdT", i), ("stbf",)], w=[("ps", 4 + h // 2)])
            for h in range(4):
                pkv = PSALL[:, 1024 + h * 256:1024 + (h + 1) * 256]
                S.add("pe", lambda e, h=h, pkv=pkv: e.matmul(pkv, lhsT=kdec[:, h, :], rhs=vbf[:, h * 256:(h + 1) * 256], start=True, stop=True),
                      r=[("kdec", i), ("vbf", i)], w=[("ps", 2 + h // 2)])
            for h in range(4):
                pkv = PSALL[:, 1024 + h * 256:1024 + (h + 1) * 256]
                S.add("dve", lambda e, h=h, pkv=pkv: e.scalar_tensor_tensor(out=state[:, h, :], in0=state[:, h, :], scalar=GC[h], in1=pkv,
                                                                         op0=ALU.mult, op1=ALU.add),
                      r=[("state",), ("ps", 2 + h // 2)], w=[("state",)])
            S.add("act", lambda e: e.activation(out=stbf[:], in_=state[:], func=AF.Copy), r=[("state",)], w=[("stbf",)])
            pk45 = [("ps", 4), ("ps", 5)]
            for h in range(4):
                po = PSALL[:, 2048 + h * 256:2048 + (h + 1) * 256]
                S.add("dve", lambda e, h=h, po=po: e.bn_stats(out=stats[:, h, :], in_=po), r=pk45, w=[("stats",)])
            for h in range(4):
                S.add("dve", lambda e, h=h: e.bn_aggr(out=mv[:, h, :], in_=stats[:, h, :]), r=[("stats",)], w=[("mv",)])
            S.add("act", lambda e: e.activation(out=rs[:], in_=mv[:, :, 1], func=AF.Sqrt, bias=epsc[:, 1:2]), r=[("mv",), ("epsc",)], w=[("rs",)])
            S.add("dve", lambda e: e.reciprocal(out=rs[:], in_=rs[:]), r=[("rs",)], w=[("rs",)])
            for h in range(4):
                po = PSALL[:, 2048 + h * 256:2048 + (h + 1) * 256]
                S.add("dve", lambda e, h=h, po=po: e.tensor_scalar(out=yn[:, h * 256:(h + 1) * 256], in0=po, scalar1=mv[:, h, 0:1], scalar2=rs[:, h:h + 1],
                                                                op0=ALU.subtract, op1=ALU.mult),
                      r=pk45 + [("mv",), ("rs",)], w=[("yn",)])
            S.add("pool", lambda e: e.tensor_tensor(out=yrt[:], in0=yn[:], in1=srg[:], op=ALU.mult), r=[("yn",), ("srg", i)], w=[("yrt", i)])

        def stage_B2(t):
            i = t % 2
            yrt = yrt2[i]
            for c in range(8):
                S.add("pe", lambda e, c=c: e.matmul(PSALL[:, 3072 + c * 128:3072 + (c + 1) * 128], lhsT=yrt[:, c * 128:(c + 1) * 128], rhs=ident, start=True, stop=True),
                      r=[("yrt", i), ("constb",)], w=[("ps", 6 + c // 4)])
            S.add("act", lambda e: e.activation(out=yretT[:, :, tls(t)], in_=PSALL[:, 3072:4096].rearrange("p (c q) -> p c q", c=8), func=AF.Copy),
                  r=[("ps", 6), ("ps", 7)], w=[("yretT", t // 4)])

        for it in range(-2, 16):
            if 0 <= it + 2 < 16:
                stage_A(it + 2)
            if 0 <= it + 1 < 16:
                stage_B1(it + 1)
            if 0 <= it < 16:
                stage_B2(it)

    def pass_moba(s):
        Wm = carve(0, [128, KC, 1536], BF16)
        kTa = carve(24 * K, [128, 4, T], BF16)
        vaug = carve(96 * K, [128, 16, 8, 66], BF16)
        ymobaT = carve(80 * K, [128, 4, T], BF16)
        o = 96 * K + 16896
        mcs = carve(o, [128, 16, 2, 8], F32); o += 1 * K
        mcb = carve(o, [128, 1152], BF16); o += 2304
        tri01 = mcb[:, 0:128]
        E128 = mcb[:, 128:1152].rearrange("p (n k) -> p n k", n=8)
        xbt = [carve(o + 2 * K * i, [128, KC, 128], BF16) for i in range(2)]; o += 4 * K
        qktm = carve(o, [128, 16, 64], BF16); o += 2 * K
        r1 = carve(o, [128, 16, 8], F32); o += 512
        r2 = carve(o, [128, 16, 8], F32); o += 512
        qTb = carve(o, [128, 4, 256], BF16); o += 2 * K
        qz = carve(44 * K, [128, 4, 2, 256], BF16)
        ksum = carve(o, [128, 4, 8], F32); o += 128
        ksb = carve(o, [128, 4, 64], BF16); o += 512
        gs = carve(o, [128, 8, 8], F32); o += 256
        cmp_ = carve(o, [128, 8, 8, 8], F32); o += 2 * K
        cntt = carve(o, [128, 8, 8], F32); o += 256
        negm = carve(o, [128, 8, 32], BF16); o += 512
        negT = carve(o, [128, 8, 256], BF16); o += 4 * K
        expP = [carve(o + 1024 * i, [128, 512], BF16) for i in range(3)]; o += 3 * K
        rec = carve(o, [128, 2, 2, 4], F32); o += 64
        ytm = carve(o, [128, 2, 512], BF16); o += 2 * K
        qkf = carve(40 * K, [128, 1024], F32)
        assert o <= 143 * K, o

        cast_dma_cols(Wm, win_d, 3072, 4608, "wm")
        S.add("sp", lambda e: e.dma_start(out=mcs[:], in_=mcs_d), w=[("mcs",)], dsem="c_mcs")
        S.add("sp", lambda e: e.dma_start(out=mcb[:], in_=mcb_d), w=[("mcb",)], dsem="c_mcb")
        S.add("dve", lambda e: e.memset(vaug[:].rearrange("p t h d -> p (t h d)"), 1.0), w=[("vaug", t) for t in range(16)])
        S.add("dve", lambda e: e.memset(negT[:].rearrange("p h q -> p (h q)"), 0.0), w=[("negT",)])
        S.add("dve", lambda e: e.memset(qz[:].rearrange("p c j q -> p (c j q)"), 0.0), w=[("qz", 0), ("qz", 1)])
        S.add("dve", lambda e: e.memset(negm[:].rearrange("p h n -> p (h n)"), 0.0), w=[("negm",)])
        S.add("dve", lambda e: e.memset(ksum[:].rearrange("p c n -> p (c n)"), 0.0), w=[("ksum",)])
        S.add("dve", lambda e: e.memset(ksb[:].rearrange("p c n -> p (c n)"), 0.0), w=[("ksb",)])

        PQK = PSALL[:, 0:1024].rearrange("p (g d) -> p g d", g=16, d=64)
        sti = 0
        for b in range(8):
            if dbg == 100:
                break
            for tt in range(2):
                t = 2 * b + tt
                xb = xbt[t % 2]
                kx = ("xbt", t % 2)
                xbt_cast(xb, t, kx)
                for cb in range(3):
                    for kc in range(KC):
                        S.add("pe", lambda e, cb=cb, kc=kc, xb=xb: e.matmul(PS[cb][:], lhsT=xb[:, kc, :], rhs=Wm[:, kc, cb * 512:(cb + 1) * 512],
                                                                         start=(kc == 0), stop=(kc == KC - 1)),
                              r=[kx, ("wm", 0)], w=[("ps", cb)])
                if dbg == 101:
                    continue
                pk = [("ps", 0), ("ps", 1)]
                cosb = mcs[:, t, 0, :].unsqueeze(1).to_broadcast([128, 16, 8])
                sinb = mcs[:, t, 1, :].unsqueeze(1).to_broadcast([128, 16, 8])
                qkf16 = qkf[:].rearrange("p (g d) -> p g d", g=16, d=64)
                x1 = qkf16[:, :, 0:8]
                x2 = qkf16[:, :, 8:16]
                S.add("act", lambda e: e.activation(out=qkf[:], in_=PSALL[:, 0:1024], func=AF.Copy), r=pk, w=[("qkf",)])
                S.add("dve", lambda e, qkf16=qkf16: e.tensor_copy(out=qktm[:], in_=qkf16), r=[("qkf",)], w=[("qktm_c",)])
                S.add("pool", lambda e, x1=x1, cosb=cosb: e.tensor_tensor(out=r1[:], in0=x1, in1=cosb, op=ALU.mult), r=[("qkf",), ("mcs",)], w=[("r1",)])
                S.add("pool", lambda e, x2=x2, sinb=sinb: e.tensor_tensor(out=r2[:], in0=x2, in1=sinb, op=ALU.mult), r=[("qkf",), ("mcs",)], w=[("r2",)])
                S.add("pool", lambda e: e.tensor_tensor(out=qktm[:, :, 0:8], in0=r1[:], in1=r2[:], op=ALU.subtract), r=[("r1",), ("r2",), ("qktm_c",)], w=[("qktm_a",)])
                S.add("pool", lambda e, x1=x1, sinb=sinb: e.tensor_tensor(out=r1[:], in0=x1, in1=sinb, op=ALU.mult), r=[("qkf",), ("mcs",)], w=[("r1",)])
                S.add("pool", lambda e, x2=x2, cosb=cosb: e.tensor_tensor(out=r2[:], in0=x2, in1=cosb, op=ALU.mult), r=[("qkf",), ("mcs",)], w=[("r2",)])
                S.add("pool", lambda e: e.tensor_tensor(out=qktm[:, :, 8:16], in0=r1[:], in1=r2[:], op=ALU.add), r=[("r1",), ("r2",), ("qktm_c",)], w=[("qktm_b",)])
                qk_keys = [("qktm_a",), ("qktm_b",), ("qktm_c",)]
                if dbg == 102:
                    continue
                S.add("act", lambda e, t=t: e.activation(out=vaug[:, t, :, 0:64], in_=PS[2][:].rearrange("p (h d) -> p h d", h=8), func=AF.Copy),
                      r=[("ps", 2)], w=[("vaug", t)])
                if dbg == 103:
                    continue
                qf = qktm[:, 0:8, :].rearrange("p h d -> p (h d)")
                kf = qktm[:, 8:16, :].rearrange("p h d -> p (h d)")
                for c in range(4):
                    S.add("pe", lambda e, c=c, qf=qf: e.matmul(PS[3][:, c * 128:(c + 1) * 128], lhsT=qf[:, c * 128:(c + 1) * 128], rhs=ident, start=True, stop=True),
                          r=qk_keys + [("constb",)], w=[("ps", 3)])
                for c in range(4):
                    S.add("pe", lambda e, c=c, kf=kf: e.matmul(PS[4][:, c * 128:(c + 1) * 128], lhsT=kf[:, c * 128:(c + 1) * 128], rhs=ident, start=True, stop=True),
                          r=qk_keys + [("constb",)], w=[("ps", 4)])
                S.add("act", lambda e, tt=tt: e.activation(out=qTb[:, :, tt * 128:(tt + 1) * 128], in_=PS[3][:].rearrange("p (c q) -> p c q", c=4), func=AF.Copy),
                      r=[("ps", 3)], w=[("qTb", tt)])
                S.add("act", lambda e, tt=tt: e.activation(out=qz[0:64, :, 0, tt * 128:(tt + 1) * 128], in_=PS[3][0:64, :].rearrange("p (c q) -> p c q", c=4), func=AF.Copy),
                      r=[("ps", 3)], w=[("qz", tt)])
                S.add("act", lambda e, tt=tt: e.activation(out=qz[64:128, :, 1, tt * 128:(tt + 1) * 128], in_=PS[3][64:128, :].rearrange("p (c q) -> p c q", c=4), func=AF.Copy),
                      r=[("ps", 3)], w=[("qz", tt)])
                S.add("act", lambda e, t=t: e.activation(out=kTa[:, :, tls(t)], in_=PS[4][:].rearrange("p (c q) -> p c q", c=4), func=AF.Copy),
                      r=[("ps", 4)], w=[("kTa", t)])
            if dbg is not None and (dbg < 11 or (100 <= dbg < 120)):
                continue
            glvl = 4 if (dbg is None or dbg < 120) else dbg - 120
            if b >= 4 and not (dbg is not None and dbg < 12):
                for tt in range(2):
                    for c in range(4):
                        S.add("pe", lambda e, c=c, tt=tt: e.matmul(PS[2][:, c * 64:(c + 1) * 64], lhsT=qTb[:, c, tt * 128:(tt + 1) * 128],
                                                                rhs=ksb[:, c, :], start=True, stop=True),
                              r=[("qTb", tt), ("ksb",)], w=[("ps", 2)])
                    S.add("act", lambda e: e.activation(out=gs[:].rearrange("p (c j) n -> p c (j n)", c=4), in_=PS[2][:, 0:256].rearrange("p (c x) -> p c x", c=4)[:, :, 0:16], func=AF.Copy),
                          r=[("ps", 2)], w=[("gs",)])
                    if glvl < 2:
                        continue
                    gm = gs[:, :, 0:b].unsqueeze(2).to_broadcast([128, 8, b, b])
                    gn = gs[:, :, 0:b].unsqueeze(3).to_broadcast([128, 8, b, b])
                    S.add("dve", lambda e, gm=gm, gn=gn, b=b: e.tensor_tensor(out=cmp_[:, :, 0:b, 0:b], in0=gm, in1=gn, op=ALU.is_gt), r=[("gs",)], w=[("cmp",)])
                    S.add("dve", lambda e, b=b: e.tensor_reduce(out=cntt[:, :, 0:b], in_=cmp_[:, :, 0:b, 0:b], axis=AX.X, op=ALU.add), r=[("cmp",)], w=[("cnt",)])
                    S.add("dve", lambda e, b=b: e.tensor_scalar(out=negm[:, :, 0:b], in0=cntt[:, :, 0:b], scalar1=2.5, scalar2=NEG, op0=ALU.is_gt, op1=ALU.mult),
                          r=[("cnt",)], w=[("negm",)])
                    if glvl < 3:
                        continue
                    for h in range(8):
                        S.add("pe", lambda e, h=h: e.matmul(PSALL[0:32, h * 128:(h + 1) * 128], lhsT=negm[:, h, :], rhs=ident, start=True, stop=True),
                              r=[("negm",), ("constb",)], w=[("ps", h // 4)])
                    S.add("act", lambda e, tt=tt: e.activation(out=negT[0:8, :, tt * 128:(tt + 1) * 128], in_=PSALL[0:8, 0:1024].rearrange("p (h q) -> p h q", h=8), func=AF.Copy),
                          r=[("ps", 0), ("ps", 1)], w=[("negT",)])
            items = [(h, n) for h in range(8) for n in list(range(b)) + [None]]
            usemask_b = (b >= 4) and not (dbg is not None and dbg < 12) and glvl >= 4

            def emit_S(item, slot, b=b, usemask_b=usemask_b):
                h, n = item
                c, j = h // 2, h % 2
                st = PS[1 + slot]
                kst = ("ps", 1 + slot)
                ex = expP[slot]
                kex = ("expP", slot)
                qh = qz[:, c, j, :]
                rq = [("qz", 0), ("qz", 1)]
                if n is not None:
                    for jj in range(2):
                        kt = 2 * n + jj
                        so = st[:, jj * 256:(jj + 1) * 256]
                        S.add("pe", lambda e, kt=kt, so=so: e.matmul(so, lhsT=kTa[:, c, tls(kt)], rhs=qh, start=True, stop=(not usemask_b)),
                              r=[("kTa", kt)] + rq, w=[kst])
                        if usemask_b:
                            S.add("pe", lambda e, so=so: e.matmul(so, lhsT=E128[:, n, :], rhs=negT[:, h, :], start=False, stop=True),
                                  r=[("negT",), ("mcb",)], w=[kst])
                    S.add("act", lambda e: e.activation(out=ex[:], in_=st[:], func=AF.Exp, scale=0.125), r=[kst], w=[kex])
                else:
                    S.add("pe", lambda e: e.matmul(st[:, 0:256], lhsT=kTa[:, c, tls(2 * b)], rhs=qh, start=True, stop=True),
                          r=[("kTa", 2 * b)] + rq, w=[kst])
                    S.add("pe", lambda e: e.matmul(st[:, 384:512], lhsT=kTa[:, c, tls(2 * b + 1)], rhs=qh[:, 128:256], start=True, stop=True),
                          r=[("kTa", 2 * b + 1)] + rq, w=[kst])
                    S.add("act", lambda e: e.activation(out=ex[:, 0:256], in_=st[:, 0:256], func=AF.Exp, scale=0.125), r=[kst], w=[kex])
                    S.add("act", lambda e: e.activation(out=ex[:, 384:512], in_=st[:, 384:512], func=AF.Exp, scale=0.125), r=[kst], w=[kex])
                    S.add("pool", lambda e: e.tensor_tensor(out=ex[:, 0:128], in0=ex[:, 0:128], in1=tri01, op=ALU.mult), r=[kex, ("mcb",)], w=[kex])
                    S.add("pool", lambda e: e.tensor_tensor(out=ex[:, 384:512], in0=ex[:, 384:512], in1=tri01, op=ALU.mult), r=[kex, ("mcb",)], w=[kex])

            def emit_PV(item, slot, b=b):
                h, n = item
                hg = h // 4
                pos = [PS[4 + hg], PS[6 + hg]]
                okeys = [("ps", 4 + hg), ("ps", 6 + hg)]
                ex = expP[slot]
                kex = ("expP", slot)
                hs = slice((h % 4) * 66, (h % 4) * 66 + 66)
                if n is not None:
                    for jj in range(2):
                        kt = 2 * n + jj
                        for qt in range(2):
                            S.add("pe", lambda e, kt=kt, qt=qt, jj=jj: e.matmul(pos[qt][:, hs], lhsT=ex[:, jj * 256 + qt * 128:jj * 256 + (qt + 1) * 128],
                                                                              rhs=vaug[:, kt, h, :], start=(n == 0 and jj == 0), stop=False),
                                  r=[kex, ("vaug", kt)], w=[okeys[qt]])
                else:
                    S.add("pe", lambda e: e.matmul(pos[0][:, hs], lhsT=ex[:, 0:128], rhs=vaug[:, 2 * b, h, :], start=(b == 0), stop=True),
                          r=[kex, ("vaug", 2 * b)], w=[okeys[0]])
                    S.add("pe", lambda e: e.matmul(pos[1][:, hs], lhsT=ex[:, 128:256], rhs=vaug[:, 2 * b, h, :], start=(b == 0), stop=False),
                          r=[kex, ("vaug", 2 * b)], w=[okeys[1]])
                    S.add("pe", lambda e: e.matmul(pos[1][:, hs], lhsT=ex[:, 384:512], rhs=vaug[:, 2 * b + 1, h, :], start=False, stop=True),
                          r=[kex, ("vaug", 2 * b + 1)], w=[okeys[1]])
                    if h % 4 == 3:
                        for qt in range(2):
                            po = pos[qt][:, 0:264].rearrange("p (h d) -> p h d", h=4)
                            S.add("dve", lambda e, po=po, qt=qt: e.reciprocal(out=rec[:, qt, hg, :], in_=po[:, :, 64]), r=[okeys[qt]], w=[("rec", qt, hg)])
                            S.add("dve", lambda e, po=po, qt=qt: e.tensor_tensor(
                                out=ytm[:, qt, hg * 256:(hg + 1) * 256].rearrange("p (h d) -> p h d", h=4), in0=po[:, :, 0:64],
                                in1=rec[:, qt, hg, :].unsqueeze(2).to_broadcast([128, 4, 64]), op=ALU.mult),
                                r=[okeys[qt], ("rec", qt, hg)], w=[("ytm", qt, hg)])

            SKEW = 2
            for i in range(len(items) + SKEW):
                if i < len(items):
                    emit_S(items[i], (sti + i) % 3)
                if i >= SKEW:
                    emit_PV(items[i - SKEW], (sti + i - SKEW) % 3)
            sti += len(items)
            S.add("dve", lambda e, b=b: e.tensor_reduce(out=ksum[:, :, b], in_=kTa[:, :, b * 256:(b + 1) * 256], axis=AX.X, op=ALU.add),
                  r=[("kTa", 2 * b), ("kTa", 2 * b + 1)], w=[("ksum",)])
            S.add("act", lambda e: e.activation(out=ksb[0:64, :, 0:8], in_=ksum[0:64, :, :], func=AF.Copy), r=[("ksum",)], w=[("ksb",)])
            S.add("act", lambda e: e.activation(out=ksb[64:128, :, 8:16], in_=ksum[64:128, :, :], func=AF.Copy), r=[("ksum",)], w=[("ksb",)])
            for qt in range(2):
                for c in range(4):
                    S.add("pe", lambda e, c=c, qt=qt: e.matmul(PS[3][:, c * 128:(c + 1) * 128], lhsT=ytm[:, qt, c * 128:(c + 1) * 128], rhs=ident, start=True, stop=True),
                          r=[("ytm", qt, 0), ("ytm", qt, 1), ("constb",)], w=[("ps", 3)])
                S.add("act", lambda e, t=2 * b + qt: e.activation(out=ymobaT[:, :, tls(t)], in_=PS[3][:].rearrange("p (c q) -> p c q", c=4), func=AF.Copy),
                      r=[("ps", 3)], w=[("ymobaT", t // 4)])

    def pass_o1(s):
        RP = carve(0, [128, KC, D], BF16)
        WGA = carve(16 * K, [128, KC, D], BF16)
        WGB = carve(32 * K, [128, KC, D], BF16)
        yretT = carve(48 * K, [128, KC, T], BF16)
        ymobaT = carve(80 * K, [128, 4, T], BF16)
        MP = carve(96 * K, [128, 4, D], BF16)
        xblk = carve(104 * K, [128, KC, 512], BF16)
        ufin = carve(112 * K, [128, KC, 512], BF16)
        sga = carve(120 * K, [128, 512], F32)
        sgb = carve(122 * K, [128, 512], F32)
        u1 = carve(124 * K, [128, 512], F32)
        u2 = carve(126 * K, [128, 512], F32)
        for half in range(2):
            for (dst, srcd, off, kb) in ((RP, rp_d, 0, "rp"), (WGA, win_d, 4608, "wga"), (MP, mp_d, 0, "mp"), (WGB, win_d, 5632, "wgb")):
                k = (kb, half)
                S.add("pool", lambda e, dst=dst, srcd=srcd, off=off, half=half: e.dma_start(out=dst[:, :, half * 512:(half + 1) * 512],
                                                                                    in_=srcd[:, :, off + half * 512:off + (half + 1) * 512]),
                      w=[k], dsem=k)
        for tb in range(4):
            sl = tbs(tb)
            S.add("act", lambda e, sl=sl: e.activation(out=xblk[:], in_=R[:, :, sl], func=AF.Copy), r=Rkeys(tb), w=[("xblk",)])
            for m in range(KC):
                o4 = 4 * (m % 2)
                ms = slice(m * 128, (m + 1) * 128)
                for kc in range(KC):
                    S.add("pe", lambda e, kc=kc, ms=ms, o4=o4, sl=sl: e.matmul(PS[o4][:], lhsT=RP[:, kc, ms], rhs=yretT[:, kc, sl], start=(kc == 0), stop=(kc == KC - 1)),
                          r=[("rp", m // 4), ("yretT", tb)], w=[("ps", o4)])
                for kc in range(KC):
                    S.add("pe", lambda e, kc=kc, ms=ms, o4=o4: e.matmul(PS[o4 + 1][:], lhsT=WGA[:, kc, ms], rhs=xblk[:, kc, :], start=(kc == 0), stop=(kc == KC - 1)),
                          r=[("wga", m // 4), ("xblk",)], w=[("ps", o4 + 1)])
                for c in range(4):
                    S.add("pe", lambda e, c=c, ms=ms, o4=o4, sl=sl: e.matmul(PS[o4 + 2][:], lhsT=MP[:, c, ms], rhs=ymobaT[:, c, sl], start=(c == 0), stop=(c == 3)),
                          r=[("mp", m // 4), ("ymobaT", tb)], w=[("ps", o4 + 2)])
                for kc in range(KC):
                    S.add("pe", lambda e, kc=kc, ms=ms, o4=o4: e.matmul(PS[o4 + 3][:], lhsT=WGB[:, kc, ms], rhs=xblk[:, kc, :], start=(kc == 0), stop=(kc == KC - 1)),
                          r=[("wgb", m // 4), ("xblk",)], w=[("ps", o4 + 3)])
                S.add("act", lambda e, o4=o4: e.activation(out=sga[:], in_=PS[o4 + 1][:], func=AF.Sigmoid), r=[("ps", o4 + 1)], w=[("sga",)])
                S.add("act", lambda e, o4=o4: e.activation(out=sgb[:], in_=PS[o4 + 3][:], func=AF.Sigmoid), r=[("ps", o4 + 3)], w=[("sgb",)])
                S.add("dve", lambda e, o4=o4: e.tensor_tensor(out=u1[:], in0=PS[o4][:], in1=sga[:], op=ALU.mult), r=[("ps", o4), ("sga",)], w=[("u1",)])
                S.add("dve", lambda e, o4=o4: e.tensor_tensor(out=u2[:], in0=PS[o4 + 2][:], in1=sgb[:], op=ALU.mult), r=[("ps", o4 + 2), ("sgb",)], w=[("u2",)])
                S.add("pool", lambda e, m=m: e.tensor_tensor(out=ufin[:, m, :], in0=u1[:], in1=u2[:], op=ALU.add), r=[("u1",), ("u2",)], w=[("ufin",)])
            S.add("act", lambda e, sl=sl: e.activation(out=yretT[:, :, sl], in_=ufin[:], func=AF.Copy), r=[("ufin",)], w=[("yretT", tb)])

    def pass_o2(s):
        WO = carve(0, [128, KC, D], BF16)
        U = carve(48 * K, [128, KC, T], BF16)
        cast_dma_cols(WO, wo_d, 0, D, "wo", step=D)
        i2 = 0
        for tb in range(4):
            sl = tbs(tb)
            for m in range(KC):
                pb = i2 % 2
                i2 += 1
                ms = slice(m * 128, (m + 1) * 128)
                for kc in range(KC):
                    S.add("pe", lambda e, kc=kc, ms=ms, pb=pb, sl=sl: e.matmul(PS[pb][:], lhsT=WO[:, kc, ms], rhs=U[:, kc, sl], start=(kc == 0), stop=(kc == KC - 1)),
                          r=[("wo", 0), ("yretT", tb)], w=[("ps", pb)])
                S.add("dve", lambda e, m=m, pb=pb, sl=sl: e.scalar_tensor_tensor(out=R[:, m, sl], in0=R[:, m, sl], scalar=ALPHA, in1=PS[pb][:], op0=ALU.mult, op1=ALU.add),
                      r=[("ps", pb), ("R", tb, m)], w=[("R", tb, m)])
            layer_norm(tb, 1)

    def mixer_phase(s):
        if "r" in mixp:
            pass_ret(s)
            S.barrier()
        if "m" in mixp:
            pass_moba(s)
            S.barrier()
        if "o1" in mixp:
            pass_o1(s)
            S.barrier()
        if "o2" in mixp:
            pass_o2(s)

    for s in range(nseq):
        if "load" in phases:
            for tb in range(4):
                S.add("sp", lambda e, tb=tb: e.dma_start(out=R[:, :, tbs(tb)], in_=xT[:, :, s * T + tb * 512: s * T + (tb + 1) * 512]),
                      w=Rkeys(tb), dsem=("xin", tb))
            S.barrier()
        if "ffn1" in phases:
            ffn_phase(s, 0)
            S.barrier()
        if "mix" in phases:
            mixer_phase(s)
            S.barrier()
        if "ffn2" in phases:
            ffn_phase(s, 1)
            S.barrier()
        out_phase(s)
    S.add("sp", lambda e: e.nop(), r=[k for tb in range(4) for k in Rkeys(tb)], w=[k for tb in range(4) for k in Rkeys(tb)])

    engines = {
        "pe": ("tensor", nc.tensor),
        "act": ("scalar", nc.scalar),
        "dve": ("vector", nc.vector),
        "pool": ("gpsimd", nc.gpsimd),
        "sp": ("sync", nc.sync),
    }
    S.emit(nc, engines)
    return nc, S


def _feat_major(v):
    return np.ascontiguousarray(np.asarray(v, np.float32).reshape(KC, 128).T)


def _module_consts():
    c = {}
    pos = np.arange(T, dtype=np.float32)
    inv = (1.0 / (np.float32(10000.0) ** np.linspace(0.0, 1.0, 64, dtype=np.float32))).astype(np.float32)
    ang = (pos[:, None] * inv[None, :]).astype(np.float32)
    rcs = np.stack([np.cos(ang), np.sin(ang)], axis=1).astype(np.float32)
    c["rcs"] = np.ascontiguousarray(rcs.reshape(16, 128, 2, 64).transpose(1, 0, 2, 3))
    inv2 = (1.0 / (np.float32(500000.0) ** (np.arange(0, 16, 2, dtype=np.float32) / np.float32(16)))).astype(np.float32)
    ang2 = (pos[:, None] * inv2[None, :]).astype(np.float32)
    mcs = np.stack([np.cos(ang2), np.sin(ang2)], axis=1).astype(np.float32)
    c["mcs"] = np.ascontiguousarray(mcs.reshape(16, 128, 2, 8).transpose(1, 0, 2, 3))
    g = 1.0 - 2.0 ** (-5.0 - np.arange(4, dtype=np.float64))
    idx = np.arange(128, dtype=np.float64)
    sc = 128.0 ** -0.5
    rdec = np.zeros((128, 1032), np.float32)
    for h in range(4):
        diff = idx[None, :] - idx[:, None]
        dt = np.where(diff >= 0, g[h] ** np.maximum(diff, 0.0), 0.0) * sc
        rdec[:, h * 128:(h + 1) * 128] = dt
        rdec[:, 512 + h * 128:512 + (h + 1) * 128] = (g[h] ** (idx + 1.0))[None, :]
        rdec[:, 1024 + h] = g[h] ** (127.0 - idx) * sc
    c["rdec"] = rdec
    mcb = np.zeros((128, 1152), np.float32)
    mcb[:, 0:128] = (idx[:, None] <= idx[None, :]).astype(np.float32)
    for n in range(8):
        mcb[n, 128 + n * 128:128 + (n + 1) * 128] = 1.0
    c["mconstb"] = mcb.astype(ml_dtypes.bfloat16)
    return c


def prep_shared(inp):
    sh = {}
    for f, (gu, dn) in enumerate([("ffn1_w_gu", "ffn1_w_down"), ("ffn2_w_gu", "ffn2_w_down")]):
        w = np.asarray(inp[gu], np.float32)[0]
        w = w.reshape(KC, 128, 2, NJ, 128)
        sh["wgu%d" % (f + 1)] = np.ascontiguousarray(w.transpose(3, 1, 2, 0, 4))
        sh["wd%d" % (f + 1)] = np.ascontiguousarray(np.asarray(inp[dn], np.float32)[0].reshape(NJ, 128, D))
    lnp = np.stack([_feat_major(inp[k][0]) for k in ("ln1_g", "ln1_b", "lnm_g", "lnm_b", "ln2_g", "ln2_b")], axis=1)
    sh["lnp"] = np.ascontiguousarray(lnp)
    def kmajor(w, nk):
        w = np.asarray(w, np.float32)
        return np.ascontiguousarray(w.reshape(nk, 128, w.shape[-1]).transpose(1, 0, 2))
    sh["win"] = kmajor(inp["w_in"][0], KC)
    sh["retp"] = kmajor(inp["ret_proj"][0], KC)
    sh["mobp"] = kmajor(inp["moba_proj"][0], 4)
    sh["wout"] = kmajor(inp["w_out"][0], KC)
    sh.update(_module_consts())
    cb = np.zeros((128, 256), np.float32)
    cb[:, 0:128] = np.eye(128, dtype=np.float32)
    cb[:, 128:256] = 1.0 / 1024.0
    sh["constb"] = cb.astype(ml_dtypes.bfloat16)
    return sh


def shard_x(x, nseq=SEQ_PER_CORE, ncores=NCORES):
    x = np.asarray(x, np.float32)
    maps = []
    for c in range(ncores):
        xc = x[c * nseq:(c + 1) * nseq].reshape(nseq * T, KC, 128)
        maps.append(np.ascontiguousarray(xc.transpose(2, 1, 0)))
    return maps


def unshard(outs, nseq=SEQ_PER_CORE):
    res = []
    for o in outs:
        res.append(np.asarray(o, np.float32).transpose(2, 1, 0).reshape(nseq, T, D))
    return np.ascontiguousarray(np.concatenate(res, axis=0))


_CACHE = {}


def kernel(**inputs):
    if "nc" not in _CACHE:
        _CACHE["nc"] = build_program()[0]
    nc = _CACHE["nc"]
    sh = prep_shared(inputs)
    xs = shard_x(inputs["x"])
    in_maps = []
    for c in range(NCORES):
        m = dict(sh)
        m["xT"] = xs[c]
        in_maps.append(m)
    res = run_bass_kernel_spmd(nc, in_maps, core_ids=list(range(NCORES)))
    return unshard([r["outT"] for r in res.results])
```
